# Optimizing a Trainium2 kernel written in Bass

```python
import jax
import jax.numpy as jnp
from jax import lax
import numpy as np

D_MODEL = 2048
BATCH = 4
SEQ = 4096
DEPTH = 4

GRID_W = 64
CTX_LEN = 256
HEAD_DIM = 128
A_HEADS = 8
A_KV = 2
A_WIN = 128
A_BLOCK = 128
B_HEADS = 8
B_WIN_ROWS = 8
B_WIN_COLS = 16
B_QCOLS = 16
C_WIDTH = 1024
CONV_W = 3
D_FF = 5632
N_MOD = 6
ROPE_THETA = 10000.0
EPS = 1e-6
NEG = -1e30

A_Q = A_HEADS * HEAD_DIM
A_KVW = A_KV * HEAD_DIM
B_W = B_HEADS * HEAD_DIM
SPLITS = (A_Q, A_KVW, A_KVW, B_W, B_W, B_W, C_WIDTH, C_WIDTH, C_WIDTH, D_MODEL, D_MODEL, D_MODEL)
IN_COLS = sum(SPLITS)

kernel_name = 'hybrid_latent_trunk'


def rmsnorm(x, g):
    xf = x.astype(jnp.float32)
    y = xf * lax.rsqrt(jnp.mean(xf * xf, axis=-1, keepdims=True) + EPS)
    return y.astype(x.dtype) * g


def modulate(h, shift, scale):
    return h * (1 + scale) + shift


def split_heads(t, n):
    return t.reshape(t.shape[:-1] + (n, t.shape[-1] // n))


def project(h, w):
    return jnp.split(h @ w, np.cumsum(SPLITS)[:-1].tolist(), axis=-1)


def dwconv3(u, w):
    return lax.conv_general_dilated(
        u, w[:, None, :].astype(u.dtype), window_strides=(1,),
        padding=((CONV_W // 2, CONV_W // 2),),
        dimension_numbers=('NWC', 'WIO', 'NWC'),
        feature_group_count=u.shape[-1])


def axial_rope(s_len):
    t = jnp.arange(s_len)
    row = (t // GRID_W).astype(jnp.float32)
    col = (t % GRID_W).astype(jnp.float32)
    n_freq = HEAD_DIM // 4
    inv = 1.0 / (ROPE_THETA ** (jnp.arange(n_freq, dtype=jnp.float32) / n_freq))
    ar = row[:, None] * inv
    ac = col[:, None] * inv
    ang = jnp.concatenate([ar, ar, ac, ac], axis=-1)
    return jnp.cos(ang), jnp.sin(ang)


def rotate_half(u):
    u1, u2 = jnp.split(u, 2, axis=-1)
    return jnp.concatenate([-u2, u1], axis=-1)


def apply_rope(u, cos, sin):
    uf = u.astype(jnp.float32)
    ur, uc = jnp.split(uf, 2, axis=-1)
    rot = jnp.concatenate([rotate_half(ur), rotate_half(uc)], axis=-1)
    return (uf * cos[:, None, :] + rot * sin[:, None, :]).astype(u.dtype)


def dense_attn(q, k, v, sink):
    bsz, t_len, n_h, hd = q.shape
    n_g = k.shape[2]
    rep = n_h // n_g
    qg = q.reshape(bsz, t_len, n_g, rep, hd)
    s = jnp.einsum('btgrd,bmgd->bgrtm', qg, k, preferred_element_type=jnp.float32) * hd ** -0.5
    if sink is not None:
        sk = jnp.broadcast_to(sink.astype(jnp.float32).reshape(1, n_g, rep, 1, 1), s.shape[:-1] + (1,))
        p = jax.nn.softmax(jnp.concatenate([s, sk], axis=-1), axis=-1)[..., :-1]
    else:
        p = jax.nn.softmax(s, axis=-1)
    o = jnp.einsum('bgrtm,bmgd->btgrd', p.astype(v.dtype), v)
    return o.reshape(bsz, t_len, n_h * hd)


def window_gqa(q, k, v, kc, vc, sink):
    bsz, s_len = q.shape[0], q.shape[1]
    nb = s_len // A_BLOCK
    rep = A_HEADS // A_KV
    qb = q.reshape(bsz, nb, A_BLOCK, A_KV, rep, HEAD_DIM)

    def band(u):
        up = jnp.pad(u.reshape(bsz, nb, A_BLOCK, A_KV, HEAD_DIM), ((0, 0), (1, 1), (0, 0), (0, 0), (0, 0)))
        return jnp.concatenate([up[:, :-2], up[:, 1:-1], up[:, 2:]], axis=2)

    kb, vb = band(k), band(v)
    scale = HEAD_DIM ** -0.5
    s_loc = jnp.einsum('bnqgrd,bnkgd->bngrqk', qb, kb, preferred_element_type=jnp.float32) * scale
    qpos = jnp.arange(A_BLOCK)[:, None]
    kpos = jnp.arange(3 * A_BLOCK)[None, :] - A_BLOCK
    kabs = jnp.arange(nb)[:, None] * A_BLOCK + kpos
    valid = (jnp.abs(kpos - qpos) <= A_WIN)[None] & ((kabs >= 0) & (kabs < s_len))[:, None, :]
    s_loc = jnp.where(valid[None, :, None, None], s_loc, NEG)
    s_ctx = jnp.einsum('bnqgrd,blgd->bngrql', qb, kc, preferred_element_type=jnp.float32) * scale
    s_sink = jnp.broadcast_to(sink.astype(jnp.float32).reshape(1, 1, A_KV, rep, 1, 1), s_loc.shape[:-1] + (1,))
    p = jax.nn.softmax(jnp.concatenate([s_loc, s_ctx, s_sink], axis=-1), axis=-1).astype(v.dtype)
    n_loc = 3 * A_BLOCK
    n_ctx = kc.shape[1]
    o = (jnp.einsum('bngrqk,bnkgd->bnqgrd', p[..., :n_loc], vb)
         + jnp.einsum('bngrql,blgd->bnqgrd', p[..., n_loc:n_loc + n_ctx], vc))
    return o.reshape(bsz, s_len, A_Q)


def neighbourhood_attn(q, k, v, kc, vc, rpb):
    bsz, s_len = q.shape[0], q.shape[1]
    rows = s_len // GRID_W
    kr = min(B_WIN_ROWS, rows)
    ncb = GRID_W // B_QCOLS
    span = B_QCOLS + B_WIN_COLS
    qcol = np.arange(GRID_W).reshape(ncb, B_QCOLS)
    cstart = np.clip(qcol - B_WIN_COLS // 2, 0, GRID_W - B_WIN_COLS)
    gstart = np.clip(np.arange(ncb) * B_QCOLS - B_WIN_COLS // 2, 0, GRID_W - span)
    kcol = gstart[:, None] + np.arange(span)
    rel = kcol[:, None, :] - qcol[:, :, None]
    col_ok = (kcol[:, None, :] >= cstart[:, :, None]) & (kcol[:, None, :] < cstart[:, :, None] + B_WIN_COLS)
    col_idx = np.clip(rel, 1 - B_WIN_COLS, B_WIN_COLS - 1) + B_WIN_COLS - 1
    col_ok = jnp.asarray(col_ok)[None, None, :, :, None, :]
    qg = q.reshape(bsz, rows, ncb, B_QCOLS, B_HEADS, HEAD_DIM)
    kg = k.reshape(bsz, rows, GRID_W, B_HEADS, HEAD_DIM)
    vg = v.reshape(bsz, rows, GRID_W, B_HEADS, HEAD_DIM)
    scale = HEAD_DIM ** -0.5
    n_loc = kr * span

    def one_row(r):
        rs = jnp.clip(r - kr // 2, 0, rows - kr)
        q_r = lax.dynamic_index_in_dim(qg, r, axis=1, keepdims=False)
        k_r = jnp.take(lax.dynamic_slice_in_dim(kg, rs, kr, axis=1), kcol, axis=2)
        v_r = jnp.take(lax.dynamic_slice_in_dim(vg, rs, kr, axis=1), kcol, axis=2)
        row_idx = rs + jnp.arange(kr) - r + B_WIN_ROWS - 1
        bias = jnp.take(rpb, row_idx, axis=1)[:, :, col_idx]
        bias = jnp.transpose(bias, (0, 2, 3, 1, 4)).astype(jnp.float32)
        s_loc = jnp.einsum('bjqhd,brjkhd->bhjqrk', q_r, k_r, preferred_element_type=jnp.float32) * scale + bias[None]
        s_loc = jnp.where(col_ok, s_loc, NEG).reshape(bsz, B_HEADS, ncb, B_QCOLS, n_loc)
        s_ctx = jnp.einsum('bjqhd,blhd->bhjql', q_r, kc, preferred_element_type=jnp.float32) * scale
        p = jax.nn.softmax(jnp.concatenate([s_loc, s_ctx], axis=-1), axis=-1).astype(v.dtype)
        p_loc = p[..., :n_loc].reshape(bsz, B_HEADS, ncb, B_QCOLS, kr, span)
        o = (jnp.einsum('bhjqrk,brjkhd->bjqhd', p_loc, v_r)
             + jnp.einsum('bhjql,blhd->bjqhd', p[..., n_loc:], vc))
        return o.reshape(bsz, GRID_W, B_W)

    out = lax.map(one_row, jnp.arange(rows))
    return jnp.moveaxis(out, 0, 1).reshape(bsz, s_len, B_W)


def short_conv(u, g_pre, g_post, w):
    return g_post * dwconv3(g_pre * u, w)


def merge(a, b, cb, za, zb, zc, w_pa, w_pb, w_pc, w_o):
    m = (jax.nn.sigmoid(za) * (a @ w_pa) + jax.nn.sigmoid(zb) * (b @ w_pb)
         + jax.nn.sigmoid(zc) * (cb @ w_pc))
    return m @ w_o


def conv_ffn(h, w_up, conv_f, w_down):
    gate, val = jnp.split(dwconv3(h @ w_up, conv_f), 2, axis=-1)
    return (jax.nn.silu(gate) * val) @ w_down


def setup_inputs(seed: int = 0) -> dict:
    key = jax.random.key(seed)
    ks = jax.random.split(key, 24)

    def nrm(k, shape, scale):
        return jax.random.normal(k, shape, jnp.float32) * scale

    L = DEPTH
    return {
        'x': nrm(ks[0], (BATCH, SEQ, D_MODEL), 1.0),
        'c': nrm(ks[1], (BATCH, D_MODEL), 1.0),
        'ctx': nrm(ks[2], (BATCH, CTX_LEN, D_MODEL), 1.0),
        'c_ctx': nrm(ks[3], (D_MODEL,), 1.0),
        'w_mod': nrm(ks[4], (L, D_MODEL, N_MOD * D_MODEL), D_MODEL ** -0.5),
        'b_mod': nrm(ks[5], (L, N_MOD * D_MODEL), 0.02),
        'norm1': 1.0 + nrm(ks[6], (L, D_MODEL), 0.02),
        'w_in': nrm(ks[7], (L, D_MODEL, IN_COLS), D_MODEL ** -0.5),
        'sink': nrm(ks[8], (L, A_HEADS), 0.5),
        'rpb': nrm(ks[9], (L, B_HEADS, 2 * B_WIN_ROWS - 1, 2 * B_WIN_COLS - 1), 0.5),
        'conv_c': nrm(ks[10], (L, CONV_W, C_WIDTH), CONV_W ** -0.5),
        'w_pa': nrm(ks[11], (L, A_Q, D_MODEL), A_Q ** -0.5),
        'w_pb': nrm(ks[12], (L, B_W, D_MODEL), B_W ** -0.5),
        'w_pc': nrm(ks[13], (L, C_WIDTH, D_MODEL), C_WIDTH ** -0.5),
        'w_o': nrm(ks[14], (L, D_MODEL, D_MODEL), D_MODEL ** -0.5),
        'norm2': 1.0 + nrm(ks[15], (L, D_MODEL), 0.02),
        'w_up': nrm(ks[16], (L, D_MODEL, 2 * D_FF), D_MODEL ** -0.5),
        'conv_f': nrm(ks[17], (L, CONV_W, 2 * D_FF), CONV_W ** -0.5),
        'w_down': nrm(ks[18], (L, D_FF, D_MODEL), D_FF ** -0.5),
        'norm_f': 1.0 + nrm(ks[19], (D_MODEL,), 0.02),
    }


def reference(x, c, ctx, c_ctx, w_mod, b_mod, norm1, w_in, sink, rpb, conv_c,
              w_pa, w_pb, w_pc, w_o, norm2, w_up, conv_f, w_down, norm_f):
    s_len = x.shape[1]
    cos, sin = axial_rope(s_len)
    silu_c = jax.nn.silu(c)
    silu_cc = jax.nn.silu(c_ctx)
    xc = ctx
    for l in range(DEPTH):
        mx = jnp.split((silu_c @ w_mod[l] + b_mod[l])[:, None, :], N_MOD, axis=-1)
        mc = jnp.split(silu_cc @ w_mod[l] + b_mod[l], N_MOD, axis=-1)
        hx = modulate(rmsnorm(x, norm1[l]), mx[0], mx[1])
        hc = modulate(rmsnorm(xc, norm1[l]), mc[0], mc[1])
        qa, ka, va, qb, kb, vb, uc, gpre, gpost, za, zb, zc = project(hx, w_in[l])
        qa_c, ka_c, va_c, qb_c, kb_c, vb_c, uc_c, gpre_c, gpost_c, za_c, zb_c, zc_c = project(hc, w_in[l])
        ka_ctx, va_ctx = split_heads(ka_c, A_KV), split_heads(va_c, A_KV)
        kb_ctx, vb_ctx = split_heads(kb_c, B_HEADS), split_heads(vb_c, B_HEADS)
        a = window_gqa(apply_rope(split_heads(qa, A_HEADS), cos, sin),
                       apply_rope(split_heads(ka, A_KV), cos, sin),
                       split_heads(va, A_KV), ka_ctx, va_ctx, sink[l])
        b = neighbourhood_attn(split_heads(qb, B_HEADS), split_heads(kb, B_HEADS),
                               split_heads(vb, B_HEADS), kb_ctx, vb_ctx, rpb[l])
        cb = short_conv(uc, gpre, gpost, conv_c[l])
        x = x + mx[2] * merge(a, b, cb, za, zb, zc, w_pa[l], w_pb[l], w_pc[l], w_o[l])
        if l < DEPTH - 1:
            a_c = dense_attn(split_heads(qa_c, A_HEADS), ka_ctx, va_ctx, sink[l])
            b_c = dense_attn(split_heads(qb_c, B_HEADS), kb_ctx, vb_ctx, None)
            cb_c = short_conv(uc_c, gpre_c, gpost_c, conv_c[l])
            xc = xc + mc[2] * merge(a_c, b_c, cb_c, za_c, zb_c, zc_c, w_pa[l], w_pb[l], w_pc[l], w_o[l])
        x = x + mx[5] * conv_ffn(modulate(rmsnorm(x, norm2[l]), mx[3], mx[4]), w_up[l], conv_f[l], w_down[l])
        if l < DEPTH - 1:
            xc = xc + mc[5] * conv_ffn(modulate(rmsnorm(xc, norm2[l]), mc[3], mc[4]), w_up[l], conv_f[l], w_down[l])
    return rmsnorm(x, norm_f)
```

```python
import numpy as np
import concourse.bass as bass
import concourse.mybir as mybir

F32 = mybir.dt.float32
BF16 = mybir.dt.bfloat16
ALU = mybir.AluOpType
AF = mybir.ActivationFunctionType
AX = mybir.AxisListType

ENGS = ('pe', 'act', 'dve', 'pool', 'sp')
DMA_ENGS = ('sp', 'act', 'pool')
NDMASEM = 20


class Res:
    __slots__ = ('name', 'w', 'r')

    def __init__(self, name=''):
        self.name = name
        self.w = None
        self.r = []


class Op:
    __slots__ = ('eng', 'fn', 'deps', 'dma', 'sem', 'val', 'sig', 'pre')

    def __init__(self, eng, fn, dma):
        self.eng = eng
        self.fn = fn
        self.dma = dma
        self.deps = []
        self.sem = None
        self.val = 0
        self.sig = False
        self.pre = None


class Prog:
    def __init__(self, nc, stack, same_eng_sync=('act', 'dve', 'pool')):
        self.nc = nc
        self.same = set(same_eng_sync)
        self.q = {e: [] for e in ENGS}
        self.touched = []
        self.esem = {e: stack.enter_context(nc.semaphore('s_' + e)) for e in ENGS if e != 'sp'}
        self.ecnt = {e: 0 for e in ENGS}
        self.dsem = {e: [stack.enter_context(nc.semaphore('d_%s%d' % (e, i))) for i in range(NDMASEM)]
                     for e in DMA_ENGS}
        self.dcnt = {e: [0] * NDMASEM for e in DMA_ENGS}
        self.drr = {e: 0 for e in DMA_ENGS}
        self.bar = stack.enter_context(nc.semaphore('bar'))
        self.nbar = 0
        self.waited = {e: {} for e in ENGS}
        self.nops = 0

    def eng_obj(self, e):
        nc = self.nc
        return {'pe': nc.tensor, 'act': nc.scalar, 'dve': nc.vector, 'pool': nc.gpsimd, 'sp': nc.sync}[e]

    def op(self, eng, fn, reads=(), writes=(), dma=False):
        o = Op(eng, fn, dma)
        deps = []
        for r in reads:
            if r.w is not None:
                deps.append(r.w)
        for w in writes:
            if w.w is not None:
                deps.append(w.w)
            deps.extend(w.r)
        for r in reads:
            if not r.r and r.w is None:
                self.touched.append(r)
            r.r.append(o)
        for w in writes:
            if not w.r and w.w is None:
                self.touched.append(w)
            w.w = o
            w.r = []
        seen = set()
        for d in deps:
            if d is o or id(d) in seen:
                continue
            seen.add(id(d))
            if d.eng == eng and not d.dma and eng not in self.same:
                continue
            o.deps.append(d)
            d.sig = True
        if dma:
            o.sig = True
        self.q[eng].append(o)
        self.nops += 1
        return o

    def dma(self, eng, out, in_, reads=(), writes=()):
        return self.op(eng, lambda e: e.dma_start(out=out, in_=in_), reads, writes, dma=True)

    def flush(self):
        nc = self.nc
        for e in ENGS:
            if e == 'sp':
                continue
            last = None
            for o in self.q[e]:
                if not o.dma:
                    last = o
            if last is not None:
                last.sig = True
        for e in ENGS:
            for o in self.q[e]:
                if o.dma:
                    k = self.drr[e]
                    self.drr[e] = (k + 1) % NDMASEM
                    if self.dcnt[e][k] > 0:
                        o.pre = (self.dsem[e][k], self.dcnt[e][k])
                    self.dcnt[e][k] += 16
                    o.sem = self.dsem[e][k]
                    o.val = self.dcnt[e][k]
                elif o.sig:
                    self.ecnt[e] += 1
                    o.sem = self.esem[e]
                    o.val = self.ecnt[e]
        self.nbar += 1
        nactive = len(ENGS)
        bar_target = self.nbar * nactive

        def make(e):
            def body(eng):
                wt = self.waited[e]

                def wait(sem, val):
                    if wt.get(id(sem), 0) >= val:
                        return
                    eng.wait_ge(sem, val)
                    wt[id(sem)] = val

                for o in self.q[e]:
                    need = {}
                    for d in o.deps:
                        k = id(d.sem)
                        if k not in need or need[k][1] < d.val:
                            need[k] = (d.sem, d.val)
                    if o.pre is not None:
                        k = id(o.pre[0])
                        if k not in need or need[k][1] < o.pre[1]:
                            need[k] = o.pre
                    for sem, val in need.values():
                        wait(sem, val)
                    ins = o.fn(eng)
                    if o.sem is not None:
                        ins.then_inc(o.sem, 16 if o.dma else 1)
                if e in DMA_ENGS:
                    for k in range(NDMASEM):
                        if self.dcnt[e][k] > 0:
                            wait(self.dsem[e][k], self.dcnt[e][k])
                if e != 'sp' and self.ecnt[e] > 0:
                    wait(self.esem[e], self.ecnt[e])
                eng.sem_inc(self.bar, 1)
                eng.wait_ge(self.bar, bar_target)
            return body

        with nc.Block() as block:
            block.tensor(make('pe'))
            block.scalar(make('act'))
            block.vector(make('dve'))
            block.gpsimd(make('pool'))
            block.sync(make('sp'))
        for r in self.touched:
            r.w = None
            r.r = []
        self.touched = []
        self.q = {e: [] for e in ENGS}

from contextlib import ExitStack
from concourse.bass_utils import run_bass_kernel_spmd
import ml_dtypes

L = 4
D = 2048
TOK = 3840
CTX0 = 3584
NXIN = [28, 25, 22, 19]
NMIX = [26, 23, 20, 17]
NOUT = [25, 22, 19, 16]
EPS = 1e-6
QS = 128.0 ** -0.5
NEGM = -30000.0
QA, KA, QB, KB, UC, GPRE, GPOST, ZA, ZB, ZC = 0, 1024, 1280, 2304, 3328, 4352, 5376, 6400, 8448, 10496
PROJ_ROWS = 12544
BLOCKS = ([('rope', QA + h * 128, QS, True) for h in range(8)] + [('rope', KA + g * 128, 1.0, False) for g in range(2)]
          + [('plain', QB + j * 256, QS, True) for j in range(4)] + [('plain', KB + j * 256, 1.0, False) for j in range(4)]
          + [('plain', UC + j * 256, 1.0, False) for j in range(8)] + [('plain', GPOST + j * 256, 1.0, True) for j in range(4)]
          + [('plain', ZA + j * 256, 1.0, True) for j in range(24)]
          + [('tok', j * 256, 1.0, False) for j in range(5)])
assert len(BLOCKS) == 59

_uid = [0]
ST_Q = 'pool'


def uid():
    _uid[0] += 1
    return _uid[0]


class Ring:
    def __init__(self, st, nc, name, shape, dt, n, psum=False, nsub=1):
        mk = nc.psum_tensor if psum else nc.sbuf_tensor
        self.t = [st.enter_context(mk("%s%d_%d" % (name, i, uid()), shape, dt)) for i in range(n)]
        self.r = [[Res() for _ in range(nsub)] for _ in range(n)]
        self.nsub = nsub
        self.i = -1

    def nxt(self):
        self.i = (self.i + 1) % len(self.t)
        r = self.r[self.i]
        return self.t[self.i], (r[0] if self.nsub == 1 else r)


class K:
    def __init__(self, NL=4, dbg=None):
        self.NL = NL
        self.dbg = dbg
        nc = self.nc = bass.Bass("TRN2", target_bir_lowering=False)

        def din(name, shape, dt=F32):
            return nc.dram_tensor(name, shape, dt, kind="ExternalInput").ap()
        self.xin = din("xin", [TOK, D])
        self.csil = din("csil", [128, 16, 2])
        self.ident = din("ident", [128, 128])
        self.wmod = din("wmod", [L, 48, 128, 16, 256])
        self.bmod = din("bmod", [L, 128, 96])
        self.g1 = din("g1", [L, 128, 16])
        self.g2 = din("g2", [L, 128, 16])
        self.gf = din("gf", [128, 16])
        self.win = din("win", [L, 59, 128, 16, 256])
        self.cosT = din("cosT", [128, TOK])
        self.sinT = din("sinT", [128, TOK])
        self.sinkr = din("sinkr", [128, L * 8])
        self.maskA = din("maskA", [128, 2, 640])
        self.biasB = din("biasB", [L, 3, 128, 8, 896])
        self.convc = din("convc", [L, 128, 8, 3])
        self.convf = din("convf", [L, 128, 88, 3])
        self.wpa = din("wpa", [L, 128, 8, D])
        self.wpb = din("wpb", [L, 128, 8, D])
        self.wpc = din("wpc", [L, 128, 8, D])
        self.wo = din("wo", [L, 8, 128, 16, 256])
        self.wup = din("wup", [L, 44, 128, 16, 256])
        self.wdn = din("wdn", [L, 16, 128, 44, 128])
        self.out = nc.dram_tensor("out", [2048, D], F32, kind="ExternalOutput").ap()
        self.xT = nc.dram_tensor("xT", [D, TOK], F32).ap()
        self.proj = nc.dram_tensor("proj", [PROJ_ROWS, TOK], BF16).ap()
        self.vtok = nc.dram_tensor("vtok", [TOK, 1280], BF16).ap()
        self.mix = nc.dram_tensor("mix", [3072, TOK], BF16).ap()
        self.mT = nc.dram_tensor("mT", [D, TOK], BF16).ap()
        self.gT = nc.dram_tensor("gT", [5632, TOK], BF16).ap()
        self.xTv = self.xT.rearrange("(c p) t -> p c t", p=128)
        self.dbg_out = {}
        if dbg:
            for name, shape, dt in dbg:
                self.dbg_out[name] = nc.dram_tensor("dbg_" + name, shape, dt, kind="ExternalOutput").ap()

    def sb(self, st, shape, dt, name='t'):
        return st.enter_context(self.nc.sbuf_tensor("%s_%d" % (name, uid()), shape, dt))

    def mm(self, out, lhsT, rhs, start, stop, reads, writes):
        return self.P.op('pe', lambda e: e.matmul(out, lhsT=lhsT, rhs=rhs, start=start, stop=stop), reads, writes)

    def tr(self, out, in_, ident, reads, writes):
        return self.P.op('pe', lambda e: e.transpose(out, in_, ident), reads, writes)

    def act(self, out, in_, func, reads, writes, bias=None, scale=None, accum=None):
        kw = {}
        if bias is not None:
            kw['bias'] = bias
        if scale is not None:
            kw['scale'] = scale
        if accum is not None:
            kw['accum_out'] = accum
        return self.P.op('act', lambda e: e.activation(out=out, in_=in_, func=func, **kw), reads, writes)

    def tt(self, eng, out, in0, in1, op, reads, writes):
        return self.P.op(eng, lambda e: e.tensor_tensor(out=out, in0=in0, in1=in1, op=op), reads, writes)

    def ts(self, eng, out, in0, s1, s2, op0, op1, reads, writes):
        if s2 is None:
            return self.P.op(eng, lambda e: e.tensor_scalar(out=out, in0=in0, scalar1=s1, scalar2=None, op0=op0),
                             reads, writes)
        return self.P.op(eng, lambda e: e.tensor_scalar(out=out, in0=in0, scalar1=s1, scalar2=s2, op0=op0, op1=op1),
                         reads, writes)

    def stt(self, eng, out, in0, scalar, in1, op0, op1, reads, writes):
        return self.P.op(eng, lambda e: e.scalar_tensor_tensor(out=out, in0=in0, scalar=scalar, in1=in1,
                                                               op0=op0, op1=op1), reads, writes)

    def cp(self, eng, out, in_, reads, writes):
        if eng == 'act':
            return self.act(out, in_, AF.Copy, reads, writes)
        return self.P.op(eng, lambda e: e.tensor_copy(out=out, in_=in_), reads, writes)

    def memset(self, eng, ap, val, writes):
        return self.P.op(eng, lambda e: e.memset(ap, val), [], writes)

    def dma(self, eng, out, in_, reads=(), writes=()):
        return self.P.dma(eng, out, in_, reads, writes)

    def wstream(self, ring, n, src_fn, depth=2):
        slots = {}

        def issue(i):
            w_t, w_r = ring.nxt()
            self.dma('pool', w_t[:], src_fn(i), writes=[w_r])
            slots[i] = (w_t, w_r)
        for i in range(min(depth, n)):
            issue(i)
        for i in range(n):
            if i + depth < n:
                issue(i + depth)
            w_t, w_r = slots.pop(i)
            yield i, w_t, w_r

    def phase_transpose_in(self, mc0):
        P = self.P
        with ExitStack() as st:
            idf = self.sb(st, [128, 128], F32)
            r_id = Res()
            self.dma('sp', idf[:], self.ident, writes=[r_id])
            scf = self.sb(st, [128, 16, 2], F32)
            r_sc = Res()
            self.dma('sp', scf[:], self.csil, writes=[r_sc])
            self.act(scf[:], scf[:], AF.Silu, [r_sc], [r_sc])
            self.cp('dve', self.scb[:], scf[:], [r_sc], [r_sc])
            P.flush()
            units = self.mod_units(0, st, mc0)
            xt = Ring(st, self.nc, 'xt', [128, D], F32, 2)
            ps = Ring(st, self.nc, 'tp', [128, 4, 128], F32, 4, psum=True)
            ot = Ring(st, self.nc, 'ot', [128, 16, 128], F32, 2, nsub=4)
            for i in range(TOK // 128):
                x_t, x_r = xt.nxt()
                self.dma('sp', x_t[:], self.xin[i * 128:(i + 1) * 128, :], writes=[x_r])
                o_t, o_r = ot.nxt()
                for q in range(4):
                    p_t, p_r = ps.nxt()
                    for j in range(4):
                        c = q * 4 + j
                        self.tr(p_t[:, j, :], x_t[:, c * 128:(c + 1) * 128], idf[:], [x_r, r_id], [p_r])
                    self.cp('act' if q % 2 else 'dve', o_t[:, q * 4:(q + 1) * 4, :], p_t[:], [p_r], [o_r[q]])
                self.dma(ST_Q, self.xTv[:, :, i * 128:(i + 1) * 128], o_t[:], reads=o_r)
                next(units, None)
                next(units, None)
            for _ in units:
                pass
            P.flush()

    def mod_units(self, l, st, mc):
        bm = self.sb(st, [128, 96], F32)
        gg = self.sb(st, [128, 2, 16], F32)
        raw = self.sb(st, [128, 96, 2], F32)
        r_bm, r_gg = Res(), Res()
        self.dma('sp', bm[:], self.bmod[l], writes=[r_bm])
        self.dma('sp', gg[:, 0, :], self.g1[l], writes=[r_gg])
        self.dma('sp', gg[:, 1, :], self.g2[l], writes=[r_gg])
        wm = Ring(st, self.nc, 'wm', [128, 16, 256], BF16, 3)
        ps = Ring(st, self.nc, 'mp', [128, 2], F32, 2, psum=True)
        r_raw = [Res() for _ in range(96)]
        scb = self.scb
        for ch, w_t, w_r in self.wstream(wm, 48, lambda i: self.wmod[l, i]):
            for cc in range(2):
                col = ch * 2 + cc
                p_t, p_r = ps.nxt()
                for kc in range(16):
                    self.mm(p_t[:], w_t[:, kc, cc * 128:(cc + 1) * 128], scb[:, kc, :], kc == 0, kc == 15,
                            [w_r], [p_r])
                self.ts('dve', raw[:, col, :], p_t[:], bm[:, col:col + 1], None, ALU.add, None,
                        [p_r, r_bm], [r_raw[col]])
            yield
        rv = raw[:].rearrange("p (m c) s -> p m c s", m=6)
        r_mc = Res()
        for s_ in range(2):
            for half in range(2):
                m0 = half * 3
                self.stt('dve', mc[:, s_, m0 + 0, :], rv[:, m0 + 1, :, s_], 1.0, gg[:, half, :], ALU.add, ALU.mult,
                         r_raw + [r_gg], [r_mc])
                self.cp('dve', mc[:, s_, m0 + 1, :], rv[:, m0 + 0, :, s_], r_raw, [r_mc])
                self.cp('dve', mc[:, s_, m0 + 2, :], rv[:, m0 + 2, :, s_], r_raw, [r_mc])
        yield

    def norm_blocks(self, st, blocks, dst, Aof, Bof, ones, r_ones):
        xb = Ring(st, self.nc, 'xb', [128, 16, 128], F32, 2)
        sq = Ring(st, self.nc, 'sq', [128, 16, 128], BF16, 2)
        ss = Ring(st, self.nc, 'ss', [128, 128], F32, 2, psum=True)
        rs = Ring(st, self.nc, 'rs', [128, 128], F32, 2)
        tm = Ring(st, self.nc, 'tm', [128, 16, 128], F32, 2)
        r_dst = Res()
        for (tok0, s, col0) in blocks:
            x_t, x_r = xb.nxt()
            self.dma('sp', x_t[:], self.xTv[:, :, tok0:tok0 + 128], writes=[x_r])
            q_t, q_r = sq.nxt()
            self.act(q_t[:], x_t[:], AF.Square, [x_r], [q_r])
            s_t, s_r = ss.nxt()
            for c in range(16):
                self.mm(s_t[:], ones[:], q_t[:, c, :], c == 0, c == 15, [q_r, r_ones], [s_r])
            r_t, r_r = rs.nxt()
            self.ts('dve', r_t[:], s_t[:], 1.0 / D, EPS, ALU.mult, ALU.add, [s_r], [r_r])
            self.act(r_t[:], r_t[:], AF.Sqrt, [r_r], [r_r])
            self.P.op('dve', lambda e, r_t=r_t: e.reciprocal(out=r_t[:], in_=r_t[:]), [r_r], [r_r])
            t_t, t_r = tm.nxt()
            self.tt('dve', t_t[:], x_t[:], r_t[:].unsqueeze(1).to_broadcast([128, 16, 128]), ALU.mult,
                    [x_r, r_r], [t_r])
            for c in range(16):
                if c % 2 == 0:
                    self.act(dst[:, c, col0:col0 + 128], t_t[:, c, :], AF.Identity, [t_r], [],
                             bias=Bof(s, c), scale=Aof(s, c))
                else:
                    self.ts('dve', dst[:, c, col0:col0 + 128], t_t[:, c, :], Aof(s, c), Bof(s, c),
                            ALU.mult, ALU.add, [t_r], [])

    def phase_norm_inproj(self, l, mc):
        P = self.P
        nin = NXIN[l]
        T_in = nin * 128
        Th = T_in + 256
        with ExitStack() as st0:
            hT = self.sb(st0, [128, 16, Th], BF16, 'hT')
            with ExitStack() as st:
                ones = self.sb(st, [128, 128], BF16)
                r_ones = Res()
                self.memset('pool', ones[:], 1.0, [r_ones])
                blocks = [(i * 128, 0, i * 128) for i in range(nin)] + [(CTX0 + j * 128, 1, T_in + j * 128)
                                                                         for j in range(2)]
                self.norm_blocks(st, blocks, hT, lambda s, c: mc[:, s, 0, c:c + 1], lambda s, c: mc[:, s, 1, c:c + 1],
                                 ones, r_ones)
                P.flush()
            with ExitStack() as st:
                wr = Ring(st, self.nc, 'w', [128, 16, 256], BF16, 3)
                ps = Ring(st, self.nc, 'ps', [128, 512], F32, 6, psum=True)
                cs = Ring(st, self.nc, 'cs', [128, 2, 512], F32, 2)
                tmp = Ring(st, self.nc, 'tmp', [128, 2, 512], F32, 2)
                sg = Ring(st, self.nc, 'sg', [128, 512], BF16, 4)
                xb_full = [(b0, min(512, T_in - b0), b0) for b0 in range(0, T_in, 512)] + [(T_in, 256, CTX0)]
                Tq = NMIX[l] * 128
                xb_q = [(b0, min(512, Tq - b0), b0) for b0 in range(0, Tq, 512)] + [(T_in, 256, CTX0)]
                tiles = [(i * 128, i * 128) for i in range(nin)] + [(T_in + j * 128, CTX0 + j * 128) for j in range(2)]
                ev = 0
                for bi, w_t, w_r in self.wstream(wr, len(BLOCKS), lambda i: self.win[l, i]):
                    kind, dest, scale, qonly = BLOCKS[bi]
                    xblocks = xb_q if qonly else xb_full
                    if kind == 'rope':
                        for (c0, wd, t0) in xblocks:
                            pa, pa_r = ps.nxt()
                            pb, pb_r = ps.nxt()
                            for kc in range(16):
                                self.mm(pa[:, :wd], w_t[:, kc, 0:128], hT[:, kc, c0:c0 + wd], kc == 0, kc == 15,
                                        [w_r], [pa_r])
                            for kc in range(16):
                                self.mm(pb[:, :wd], w_t[:, kc, 128:256], hT[:, kc, c0:c0 + wd], kc == 0, kc == 15,
                                        [w_r], [pb_r])
                            c_t, c_r = cs.nxt()
                            self.dma('sp', c_t[:, 0, :wd], self.cosT[:, t0:t0 + wd], writes=[c_r])
                            self.dma('sp', c_t[:, 1, :wd], self.sinT[:, t0:t0 + wd], writes=[c_r])
                            m_t, m_r = tmp.nxt()
                            self.stt('dve', m_t[:, 0, :wd], pa[:, :wd], scale, c_t[:, 0, :wd], ALU.mult, ALU.mult,
                                     [pa_r, c_r], [m_r])
                            self.stt('dve', m_t[:, 1, :wd], pb[:, :wd], scale, c_t[:, 1, :wd], ALU.mult, ALU.mult,
                                     [pb_r, c_r], [m_r])
                            g_t, g_r = sg.nxt()
                            self.tt('dve', g_t[:, :wd], m_t[:, 0, :wd], m_t[:, 1, :wd], ALU.add, [m_r], [g_r])
                            self.dma(ST_Q, self.proj[dest:dest + 128, t0:t0 + wd], g_t[:, :wd], reads=[g_r])
                    elif kind == 'plain':
                        for j in range(2):
                            for (c0, wd, t0) in xblocks:
                                pa, pa_r = ps.nxt()
                                for kc in range(16):
                                    self.mm(pa[:, :wd], w_t[:, kc, j * 128:(j + 1) * 128], hT[:, kc, c0:c0 + wd],
                                            kc == 0, kc == 15, [w_r], [pa_r])
                                g_t, g_r = sg.nxt()
                                ev += 1
                                if ev % 2:
                                    self.act(g_t[:, :wd], pa[:, :wd], AF.Copy, [pa_r], [g_r], scale=scale)
                                else:
                                    self.ts('dve', g_t[:, :wd], pa[:, :wd], scale, None, ALU.mult, None, [pa_r], [g_r])
                                self.dma(ST_Q, self.proj[dest + j * 128:dest + (j + 1) * 128, t0:t0 + wd], g_t[:, :wd],
                                         reads=[g_r])
                    else:
                        for (c0, t0) in tiles:
                            pa, pa_r = ps.nxt()
                            for kc in range(16):
                                self.mm(pa[:, :256], hT[:, kc, c0:c0 + 128], w_t[:, kc, :], kc == 0, kc == 15,
                                        [w_r], [pa_r])
                            g_t, g_r = sg.nxt()
                            ev += 1
                            self.cp('act' if ev % 2 else 'dve', g_t[:, :256], pa[:, :256], [pa_r], [g_r])
                            self.dma(ST_Q, self.vtok[t0:t0 + 128, dest:dest + 256], g_t[:, :256], reads=[g_r])
                P.flush()

    def phase_attn(self, l, do_ctx):
        P = self.P
        nmix = NMIX[l]
        proj, vtok, mix = self.proj, self.vtok, self.mix
        with ExitStack() as st:
            nc = self.nc
            kcA = self.sb(st, [128, 2, 256], BF16)
            vcA = self.sb(st, [128, 2, 256], BF16)
            kcB = self.sb(st, [128, 8, 256], BF16)
            vcB = self.sb(st, [128, 2, 1024], BF16)
            snk = self.sb(st, [128, 8], F32)
            mA = self.sb(st, [128, 2, 640], F32)
            idf = self.sb(st, [128, 128], F32)
            idb = self.sb(st, [128, 128], BF16)
            r_c, r_sink, r_mA, r_idf, r_id = Res(), Res(), Res(), Res(), Res()
            self.dma('sp', kcA[:], proj[KA:KA + 256, CTX0:CTX0 + 256].rearrange("(g p) t -> p g t", p=128), writes=[r_c])
            self.dma('sp', kcB[:], proj[KB:KB + 1024, CTX0:CTX0 + 256].rearrange("(g p) t -> p g t", p=128), writes=[r_c])
            self.dma('sp', vcA[:], vtok[CTX0:CTX0 + 256, 0:256].rearrange("(c p) v -> p c v", p=128), writes=[r_c])
            self.dma('sp', vcB[:], vtok[CTX0:CTX0 + 256, 256:1280].rearrange("(c p) v -> p c v", p=128), writes=[r_c])
            self.dma('sp', snk[:], self.sinkr[:, l * 8:(l + 1) * 8], writes=[r_sink])
            self.dma('sp', mA[:], self.maskA, writes=[r_mA])
            self.dma('sp', idf[:], self.ident, writes=[r_idf])
            self.cp('dve', idb[:], idf[:], [r_idf], [r_id])
            mAb = self.sb(st, [128, 2, 384], BF16)
            self.cp('dve', mAb[:], mA[:, :, 0:384], [r_mA], [r_mA])
            ND = 3
            R = dict(
                qa=Ring(st, nc, 'qa', [128, 8, 128], BF16, ND), ka=Ring(st, nc, 'ka', [128, 2, 384], BF16, ND),
                va=Ring(st, nc, 'va', [128, 3, 256], BF16, ND), qb=Ring(st, nc, 'qb', [128, 8, 128], BF16, ND),
                kb=Ring(st, nc, 'kb', [128, 8, 640], BF16, 2), vb=Ring(st, nc, 'vb', [128, 5, 1024], BF16, 2),
                bB=Ring(st, nc, 'bB', [128, 8, 896], F32, 2),
                S=Ring(st, nc, 'S', [128, 1024], F32, 2, psum=True),
                PTp=Ring(st, nc, 'PTp', [128, 8, 128], BF16, 2, psum=True),
                O=Ring(st, nc, 'O', [128, 512], F32, 2, psum=True),
                Sb=Ring(st, nc, 'Sb', [128, 896], F32, 4), Pm=Ring(st, nc, 'Pm', [128, 896], BF16, 4),
                Pn=Ring(st, nc, 'Pn', [128, 896], BF16, 4), st=Ring(st, nc, 'st', [128, 8], F32, 8),
                PTA=Ring(st, nc, 'PTA', [128, 5, 4, 128], BF16, 3), PTB=Ring(st, nc, 'PTB', [128, 7, 128], BF16, 4),
                oa=Ring(st, nc, 'oa', [128, 8, 128], BF16, 2), ob=Ring(st, nc, 'ob', [128, 8, 128], BF16, 2),
            )
            qtiles = [('x', i) for i in range(nmix)] + ([('c', j) for j in range(2)] if do_ctx else [])
            evc = [0]

            def ev_eng():
                evc[0] += 1
                return 'act' if evc[0] % 2 else 'dve'

            def load_tile(ti):
                kind, i = qtiles[ti]
                isx = kind == 'x'
                tcol = i * 128 if isx else CTX0 + i * 128
                T = dict(isx=isx, i=i, tcol=tcol)
                T['qa'] = R['qa'].nxt()
                T['qb'] = R['qb'].nxt()
                self.dma('sp', T['qa'][0][:], proj[QA:QA + 1024, tcol:tcol + 128].rearrange("(h p) t -> p h t", p=128),
                         writes=[T['qa'][1]])
                self.dma('sp', T['qb'][0][:], proj[QB:QB + 1024, tcol:tcol + 128].rearrange("(h p) t -> p h t", p=128),
                         writes=[T['qb'][1]])
                if isx:
                    sa = max(i - 1, 0)
                    sbt = max(i - 2, 0)
                    var = 0 if i == 0 else (1 if i == 1 else 2)
                    for nm in ('ka', 'va', 'kb', 'vb', 'bB'):
                        T[nm] = R[nm].nxt()
                    self.dma('sp', T['ka'][0][:], proj[KA:KA + 256, sa * 128:(sa + 3) * 128].rearrange("(g p) t -> p g t", p=128),
                             writes=[T['ka'][1]])
                    self.dma('sp', T['va'][0][:], vtok[sa * 128:(sa + 3) * 128, 0:256].rearrange("(c p) v -> p c v", p=128),
                             writes=[T['va'][1]])
                    self.dma('sp', T['kb'][0][:], proj[KB:KB + 1024, sbt * 128:(sbt + 5) * 128].rearrange("(g p) t -> p g t", p=128),
                             writes=[T['kb'][1]])
                    self.dma('sp', T['vb'][0][:], vtok[sbt * 128:(sbt + 5) * 128, 256:1280].rearrange("(c p) v -> p c v", p=128),
                             writes=[T['vb'][1]])
                    self.dma('sp', T['bB'][0][:], self.biasB[l, var], writes=[T['bB'][1]])
                T['oa'] = R['oa'].nxt()
                T['ob'] = R['ob'].nxt()
                return T

            tiles = {}
            jobs = []
            for ti in range(len(qtiles)):
                for mixer in ('A', 'B'):
                    for h in range(8):
                        jobs.append(dict(ti=ti, mixer=mixer, h=h))
            grpA = {}

            def stageA(J):
                ti, h, mixer = J['ti'], J['h'], J['mixer']
                if mixer == 'A' and h == 0 and ti == 0:
                    tiles[0] = load_tile(0)
                if mixer == 'A' and h == 5 and ti + 1 < len(qtiles):
                    tiles[ti + 1] = load_tile(ti + 1)
                T = tiles[ti]
                isx = T['isx']
                S_t, S_r = R['S'].nxt()
                stt, st_r = R['st'].nxt()
                J['st'] = (stt, st_r)
                if mixer == 'A':
                    g = h // 4
                    q, q_r = T['qa'][0][:, h, :], T['qa'][1]
                    W = 640 if isx else 256
                    if isx:
                        ka_t, ka_r = T['ka']
                        self.mm(S_t[:, 0:384], q, ka_t[:, g, :], True, False, [q_r, ka_r], [S_r])
                        self.mm(S_t[:, 0:384], idb[:], mAb[:, 0 if T['i'] == 0 else 1, :], False, True, [r_id, r_mA], [S_r])
                        self.mm(S_t[:, 384:512], q, kcA[:, g, 0:128], True, True, [q_r, r_c], [S_r])
                        self.mm(S_t[:, 512:640], q, kcA[:, g, 128:256], True, True, [q_r, r_c], [S_r])
                        table, t_r = None, None
                    else:
                        self.mm(S_t[:, 0:256], q, kcA[:, g, :], True, True, [q_r, r_c], [S_r])
                        table, t_r = None, None
                    sinkcol = snk[:, h:h + 1]
                else:
                    q, q_r = T['qb'][0][:, h, :], T['qb'][1]
                    W = 896 if isx else 256
                    if isx:
                        kb_t, kb_r = T['kb']
                        self.mm(S_t[:, 0:512], q, kb_t[:, h, 0:512], True, True, [q_r, kb_r], [S_r])
                        self.mm(S_t[:, 512:640], q, kb_t[:, h, 512:640], True, True, [q_r, kb_r], [S_r])
                        self.mm(S_t[:, 640:896], q, kcB[:, h, :], True, True, [q_r, r_c], [S_r])
                        table, t_r = T['bB'][0][:, h, :], T['bB'][1]
                    else:
                        self.mm(S_t[:, 0:256], q, kcB[:, h, :], True, True, [q_r, r_c], [S_r])
                        table, t_r = None, None
                    sinkcol = None
                J['W'] = W
                J['sink'] = sinkcol
                S_view = S_t[:, 0:W]
                if table is not None:
                    sb_t, sb_r = R['Sb'].nxt()
                    self.tt('dve', sb_t[:, :W], S_view, table, ALU.add, [S_r, t_r], [sb_r])
                    src, src_r = sb_t[:, :W], sb_r
                else:
                    src, src_r = S_view, S_r
                self.P.op('dve', lambda e: e.reduce_max(out=stt[:, 0:1], in_=src, axis=AX.X), [src_r], [st_r])
                if sinkcol is not None:
                    self.ts('dve', stt[:, 1:2], stt[:, 0:1], sinkcol, -1.0, ALU.max, ALU.mult, [st_r, r_sink], [st_r])
                else:
                    self.ts('dve', stt[:, 1:2], stt[:, 0:1], -1.0, None, ALU.mult, None, [st_r], [st_r])
                self.memset('dve', stt[:, 2:3], 0.0, [st_r])
                J['Sb'] = (src, src_r)

            def stageB1(J):
                stt, st_r = J['st']
                src, sb_r = J['Sb']
                W = J['W']
                pm_t, pm_r = R['Pm'].nxt()
                J['Pm'] = (pm_t, pm_r)
                self.act(pm_t[:, :W], src, AF.Exp, [sb_r, st_r], [pm_r, st_r], bias=stt[:, 1:2], scale=1.0,
                         accum=stt[:, 2:3])
                if J['sink'] is not None:
                    self.act(stt[:, 3:4], J['sink'], AF.Exp, [st_r, r_sink], [st_r], bias=stt[:, 1:2], scale=1.0)

            def stageB2(J):
                stt, st_r = J['st']
                pm_t, pm_r = J['Pm']
                W = J['W']
                if J['sink'] is not None:
                    self.tt('dve', stt[:, 2:3], stt[:, 2:3], stt[:, 3:4], ALU.add, [st_r], [st_r])
                self.P.op('dve', lambda e: e.reciprocal(out=stt[:, 5:6], in_=stt[:, 2:3]), [st_r], [st_r])
                pn_t, pn_r = R['Pn'].nxt()
                J['Pn'] = (pn_t, pn_r)
                self.act(pn_t[:, :W], pm_t[:, :W], AF.Copy, [pm_r, st_r], [pn_r], scale=stt[:, 5:6])

            def stageC(J):
                pn_t, pn_r = J['Pn']
                nch = J['W'] // 128
                pt_t, pt_r = R['PTp'].nxt()
                for c in range(nch):
                    self.tr(pt_t[:, c, :], pn_t[:, c * 128:(c + 1) * 128], idb[:], [pn_r, r_id], [pt_r])
                if J['mixer'] == 'A':
                    hh = J['h'] % 4
                    key = (J['ti'], J['h'] // 4)
                    if hh == 0:
                        grpA[key] = R['PTA'].nxt()
                    pta_t, pta_r = grpA[key]
                    self.cp(ev_eng(), pta_t[:, 0:nch, hh, :], pt_t[:, 0:nch, :], [pt_r], [pta_r])
                else:
                    ptb_t, ptb_r = R['PTB'].nxt()
                    J['PTB'] = (ptb_t, ptb_r)
                    self.cp(ev_eng(), ptb_t[:, 0:nch, :], pt_t[:, 0:nch, :], [pt_r], [ptb_r])

            curO = {}

            def stageD(J):
                T = tiles[J['ti']]
                isx, tcol = T['isx'], T['tcol']
                h = J['h']
                nch = J['W'] // 128
                if J['mixer'] == 'A':
                    if h % 4 != 3:
                        return
                    g = h // 4
                    pta_t, pta_r = grpA.pop((J['ti'], g))
                    O_t, O_r = R['O'].nxt()
                    for c in range(nch):
                        if isx and c < 3:
                            v, v_r = T['va'][0][:, c, g * 128:(g + 1) * 128], T['va'][1]
                        else:
                            cc = c - 3 if isx else c
                            v, v_r = vcA[:, cc, g * 128:(g + 1) * 128], r_c
                        self.mm(O_t[:], v, pta_t[:, c, :, :].rearrange("p h q -> p (h q)"), c == 0, c == nch - 1,
                                [v_r, pta_r], [O_r])
                    oa_t, oa_r = T['oa']
                    self.cp(ev_eng(), oa_t[:, g * 4:(g + 1) * 4, :], O_t[:].rearrange("p (h q) -> p h q", h=4),
                            [O_r], [oa_r])
                    if g == 1:
                        self.dma(ST_Q, mix[0:1024, tcol:tcol + 128].rearrange("(h p) t -> p h t", p=128), oa_t[:],
                                 reads=[oa_r])
                else:
                    ptb_t, ptb_r = J['PTB']
                    hq = h % 4
                    if hq == 0:
                        curO['B'] = R['O'].nxt()
                    O_t, O_r = curO['B']
                    for c in range(nch):
                        if isx and c < 5:
                            v, v_r = T['vb'][0][:, c, h * 128:(h + 1) * 128], T['vb'][1]
                        else:
                            cc = c - 5 if isx else c
                            v, v_r = vcB[:, cc, h * 128:(h + 1) * 128], r_c
                        self.mm(O_t[:, hq * 128:(hq + 1) * 128], v, ptb_t[:, c, :], c == 0, c == nch - 1,
                                [v_r, ptb_r], [O_r])
                    ob_t, ob_r = T['ob']
                    if hq == 3:
                        self.cp(ev_eng(), ob_t[:, h - 3:h + 1, :], O_t[:].rearrange("p (h q) -> p h q", h=4),
                                [O_r], [ob_r])
                    if h == 7:
                        self.dma(ST_Q, mix[1024:2048, tcol:tcol + 128].rearrange("(h p) t -> p h t", p=128), ob_t[:],
                                 reads=[ob_r])

            stages = [stageA, stageB1, stageB2, stageC, stageD]
            nj = len(jobs)
            for t in range(nj + len(stages) - 1):
                for k, fn in enumerate(stages):
                    j = t - k
                    if 0 <= j < nj:
                        fn(jobs[j])
            P.flush()

    def phase_conv(self, l, do_ctx):
        P = self.P
        nmix = NMIX[l]
        Tc = nmix * 128
        proj, mix = self.proj, self.mix
        with ExitStack() as st:
            nc = self.nc
            cw = self.sb(st, [128, 8, 3], F32)
            r_cw = Res()
            self.dma('sp', cw[:], self.convc[l], writes=[r_cw])
            W = Tc + 2
            inb = Ring(st, nc, 'cin', [128, 3, W], BF16, 2)
            pp = Ring(st, nc, 'cp', [128, W], F32, 2)
            oo = Ring(st, nc, 'co', [128, W], F32, 2)
            cb = Ring(st, nc, 'cb', [128, W], BF16, 2)
            segs = [(0, Tc, True)] + ([(CTX0, 256, False)] if do_ctx else [])
            k = 0
            for cc in range(8):
                for (t0, n, has_next) in segs:
                    eng = 'dve'
                    i_t, i_r = inb.nxt()
                    nl = n + 1 if has_next else n
                    self.memset(eng, i_t[:, :, 0:1], 0.0, [i_r])
                    if not has_next:
                        self.memset(eng, i_t[:, :, n + 1:n + 2], 0.0, [i_r])
                    for z, base in enumerate((UC, GPRE, GPOST)):
                        self.dma('sp', i_t[:, z, 1:1 + nl], proj[base + cc * 128:base + (cc + 1) * 128, t0:t0 + nl],
                                 writes=[i_r])
                    p_t, p_r = pp.nxt()
                    self.tt('dve', p_t[:, 0:n + 2], i_t[:, 0, 0:n + 2], i_t[:, 1, 0:n + 2], ALU.mult, [i_r], [p_r])
                    o_t, o_r = oo.nxt()
                    self.act(o_t[:, 0:n], p_t[:, 0:n], AF.Copy, [p_r, r_cw], [o_r], scale=cw[:, cc, 0:1])
                    self.stt('dve', o_t[:, 0:n], p_t[:, 1:n + 1], cw[:, cc, 1:2], o_t[:, 0:n], ALU.mult, ALU.add,
                             [p_r, r_cw, o_r], [o_r])
                    self.stt('dve', o_t[:, 0:n], p_t[:, 2:n + 2], cw[:, cc, 2:3], o_t[:, 0:n], ALU.mult, ALU.add,
                             [p_r, r_cw, o_r], [o_r])
                    c_t, c_r = cb.nxt()
                    self.tt('dve', c_t[:, 0:n], o_t[:, 0:n], i_t[:, 2, 1:n + 1], ALU.mult, [o_r, i_r], [c_r])
                    self.dma(ST_Q, mix[2048 + cc * 128:2048 + (cc + 1) * 128, t0:t0 + n], c_t[:, 0:n], reads=[c_r])
            P.flush()

    def phase_merge(self, l, do_ctx):
        P = self.P
        nmix = NMIX[l]
        Tm = nmix * 128
        proj, mix, mT = self.proj, self.mix, self.mT
        with ExitStack() as st:
            nc = self.nc
            wp = [self.sb(st, [128, 8, D], BF16, 'wp') for _ in range(3)]
            r_wp = [Res() for _ in range(3)]
            for z, src in enumerate((self.wpa, self.wpb, self.wpc)):
                for hf in range(2):
                    self.dma('pool', wp[z][:, hf * 4:(hf + 1) * 4, :], src[l, :, hf * 4:(hf + 1) * 4, :], writes=[r_wp[z]])
            mb = Ring(st, nc, 'mb', [128, 24, 512], BF16, 2)
            zb = Ring(st, nc, 'zb', [128, 3, 512], BF16, 2)
            sg = Ring(st, nc, 'sg', [128, 3, 512], F32, 2)
            ps = Ring(st, nc, 'ps', [128, 512], F32, 6, psum=True)
            t1 = Ring(st, nc, 't1', [128, 512], F32, 2)
            t2 = Ring(st, nc, 't2', [128, 512], F32, 2)
            t3 = Ring(st, nc, 't3', [128, 512], F32, 2)
            mo = Ring(st, nc, 'mo', [128, 512], BF16, 2)
            tblocks = [(b0, min(512, Tm - b0)) for b0 in range(0, Tm, 512)] + ([(CTX0, 256)] if do_ctx else [])
            mixv = mix.rearrange("(k p) t -> p k t", p=128)
            zv = proj[ZA:ZA + 6144, :].rearrange("(z c p) t -> p z c t", z=3, c=16, p=128)
            for (t0, wd) in tblocks:
                m_t, m_r = mb.nxt()
                for z in range(3):
                    self.dma('sp', m_t[:, z * 8:(z + 1) * 8, :wd], mixv[:, z * 8:(z + 1) * 8, t0:t0 + wd], writes=[m_r])
                for n in range(16):
                    z_t, z_r = zb.nxt()
                    self.dma('sp', z_t[:, :, :wd], zv[:, :, n, t0:t0 + wd], writes=[z_r])
                    s_t, s_r = sg.nxt()
                    self.act(s_t[:, :, :wd], z_t[:, :, :wd], AF.Sigmoid, [z_r], [s_r])
                    pz = []
                    for z in range(3):
                        p_t, p_r = ps.nxt()
                        for kc in range(8):
                            self.mm(p_t[:, :wd], wp[z][:, kc, n * 128:(n + 1) * 128], m_t[:, z * 8 + kc, :wd],
                                    kc == 0, kc == 7, [r_wp[z], m_r], [p_r])
                        pz.append((p_t, p_r))
                    a_t, a_r = t1.nxt()
                    b_t, b_r = t2.nxt()
                    c_t, c_r = t3.nxt()
                    self.tt('dve', a_t[:, :wd], pz[0][0][:, :wd], s_t[:, 0, :wd], ALU.mult, [pz[0][1], s_r], [a_r])
                    self.tt('dve', b_t[:, :wd], pz[1][0][:, :wd], s_t[:, 1, :wd], ALU.mult, [pz[1][1], s_r], [b_r])
                    self.tt('dve', c_t[:, :wd], pz[2][0][:, :wd], s_t[:, 2, :wd], ALU.mult, [pz[2][1], s_r], [c_r])
                    self.tt('dve', a_t[:, :wd], a_t[:, :wd], b_t[:, :wd], ALU.add, [a_r, b_r], [a_r])
                    o_t, o_r = mo.nxt()
                    self.tt('dve', o_t[:, :wd], a_t[:, :wd], c_t[:, :wd], ALU.add, [a_r, c_r], [o_r])
                    self.dma(ST_Q, mT[n * 128:(n + 1) * 128, t0:t0 + wd], o_t[:, :wd], reads=[o_r])
            P.flush()

    def resid_update(self, rings, p_t, p_r, wd, n, t0, gate, r_mc):
        xo = rings.nxt()
        x_t, x_r = xo
        src = self.xT[n * 128:(n + 1) * 128, t0:t0 + wd]
        self.dma('sp', x_t[:, :wd], src, writes=[x_r])
        self.stt('dve', x_t[:, :wd], p_t[:, :wd], gate, x_t[:, :wd], ALU.mult, ALU.add, [p_r, x_r, r_mc], [x_r])
        self.dma(ST_Q, src, x_t[:, :wd], reads=[x_r])

    def phase_wo(self, l, mc, do_ctx):
        P = self.P
        nmix = NMIX[l]
        Tm = nmix * 128
        Tt = Tm + (256 if do_ctx else 0)
        with ExitStack() as st:
            nc = self.nc
            ms = self.sb(st, [128, 16, Tt], BF16, 'ms')
            r_ms = Res()
            mTv = self.mT.rearrange("(k p) t -> p k t", p=128)
            for k4 in range(4):
                self.dma('sp', ms[:, k4 * 4:(k4 + 1) * 4, 0:Tm], mTv[:, k4 * 4:(k4 + 1) * 4, 0:Tm], writes=[r_ms])
            if do_ctx:
                self.dma('sp', ms[:, :, Tm:Tm + 256], mTv[:, :, CTX0:CTX0 + 256], writes=[r_ms])
            wr = Ring(st, nc, 'w', [128, 16, 256], BF16, 3)
            ps = Ring(st, nc, 'ps', [128, 512], F32, 4, psum=True)
            xo = Ring(st, nc, 'xo', [128, 512], F32, 4)
            r_mc = Res()
            tblocks = [(b0, min(512, Tm - b0), b0, 0) for b0 in range(0, Tm, 512)] + ([(Tm, 256, CTX0, 1)] if do_ctx else [])
            for wb, w_t, w_r in self.wstream(wr, 8, lambda i: self.wo[l, i]):
                for j in range(2):
                    n = wb * 2 + j
                    for (c0, wd, t0, s) in tblocks:
                        p_t, p_r = ps.nxt()
                        for kc in range(16):
                            self.mm(p_t[:, :wd], w_t[:, kc, j * 128:(j + 1) * 128], ms[:, kc, c0:c0 + wd],
                                    kc == 0, kc == 15, [w_r, r_ms], [p_r])
                        self.resid_update(xo, p_t, p_r, wd, n, t0, mc[:, s, 2, n:n + 1], r_mc)
            P.flush()

    def phase_ffn(self, l, mc, do_ctx, mc_next=None):
        P = self.P
        nmix = NMIX[l]
        nout = NOUT[l]
        Tx = nmix * 128
        Hc = Tx + 259
        cbase = Tx + 1
        with ExitStack() as st0:
            nc = self.nc
            hT = self.sb(st0, [128, 16, Hc], BF16, 'h2T')
            with ExitStack() as st:
                ones = self.sb(st, [128, 128], BF16)
                r_ones = Res()
                self.memset('pool', ones[:], 1.0, [r_ones])
                for col in (0, Tx + 1, Tx + 258):
                    self.memset('pool', hT[:, :, col:col + 1], 0.0, [])
                blocks = [(i * 128, 0, 1 + i * 128) for i in range(nmix)]
                if do_ctx:
                    blocks += [(CTX0 + j * 128, 1, Tx + 2 + j * 128) for j in range(2)]
                self.norm_blocks(st, blocks, hT, lambda s, c: mc[:, s, 3, c:c + 1], lambda s, c: mc[:, s, 4, c:c + 1],
                                 ones, r_ones)
                P.flush()
            with ExitStack() as st:
                cf = self.sb(st, [128, 88, 3], F32)
                r_cf = Res()
                self.dma('sp', cf[:], self.convf[l], writes=[r_cf])
                wr = Ring(st, nc, 'w', [128, 16, 256], BF16, 3)
                ps = Ring(st, nc, 'ps', [128, 512], F32, 6, psum=True)
                cg = Ring(st, nc, 'cg', [128, 2, 512], F32, 3)
                sl = Ring(st, nc, 'sl', [128, 512], F32, 2)
                go = Ring(st, nc, 'go', [128, 512], BF16, 3)
                To = nout * 128
                oblocks = [(o0, min(510, To - o0), o0) for o0 in range(0, To, 510)]
                if do_ctx:
                    oblocks += [(cbase, 256, CTX0)]
                units = self.mod_units(l + 1, st, mc_next) if mc_next is not None else iter(())
                for j, w_t, w_r in self.wstream(wr, 44, lambda i: self.wup[l, i]):
                    next(units, None)
                    for (c0, n, t0) in oblocks:
                        pg, pg_r = ps.nxt()
                        pv, pv_r = ps.nxt()
                        for kc in range(16):
                            self.mm(pg[:, :n + 2], w_t[:, kc, 0:128], hT[:, kc, c0:c0 + n + 2], kc == 0, kc == 15,
                                    [w_r], [pg_r])
                        for kc in range(16):
                            self.mm(pv[:, :n + 2], w_t[:, kc, 128:256], hT[:, kc, c0:c0 + n + 2], kc == 0, kc == 15,
                                    [w_r], [pv_r])
                        c_t, c_r = cg.nxt()
                        for z, (pz, pz_r, fcol) in enumerate(((pg, pg_r, j), (pv, pv_r, 44 + j))):
                            self.act(c_t[:, z, :n], pz[:, 1:n + 1], AF.Copy, [pz_r, r_cf], [c_r], scale=cf[:, fcol, 1:2])
                            self.stt('dve', c_t[:, z, :n], pz[:, 0:n], cf[:, fcol, 0:1], c_t[:, z, :n], ALU.mult, ALU.add,
                                     [pz_r, r_cf, c_r], [c_r])
                            self.stt('dve', c_t[:, z, :n], pz[:, 2:n + 2], cf[:, fcol, 2:3], c_t[:, z, :n], ALU.mult, ALU.add,
                                     [pz_r, r_cf, c_r], [c_r])
                        s_t, s_r = sl.nxt()
                        self.act(s_t[:, :n], c_t[:, 0, :n], AF.Silu, [c_r], [s_r])
                        g_t, g_r = go.nxt()
                        self.tt('dve', g_t[:, :n], s_t[:, :n], c_t[:, 1, :n], ALU.mult, [s_r, c_r], [g_r])
                        self.dma(ST_Q, self.gT[j * 128:(j + 1) * 128, t0:t0 + n], g_t[:, :n], reads=[g_r])
                for _ in units:
                    pass
                P.flush()

    def phase_down(self, l, mc, do_ctx):
        P = self.P
        To = NOUT[l] * 128
        with ExitStack() as st:
            nc = self.nc
            gs = self.sb(st, [128, 44, 1024], BF16, 'gs')
            r_gs = Res()
            wr = Ring(st, nc, 'w', [128, 44, 128], BF16, 3)
            ps = Ring(st, nc, 'ps', [128, 512], F32, 4, psum=True)
            xo = Ring(st, nc, 'xo', [128, 512], F32, 4)
            r_mc = Res()
            gTv = self.gT.rearrange("(k p) t -> p k t", p=128)
            sblocks = [(b0, min(1024, To - b0), 0) for b0 in range(0, To, 1024)] + ([(CTX0, 256, 1)] if do_ctx else [])
            ws = self.wstream(wr, 16 * len(sblocks), lambda i: self.wdn[l, i % 16])
            for (s0, sw, s) in sblocks:
                for k4 in range(4):
                    self.dma('sp', gs[:, k4 * 11:(k4 + 1) * 11, 0:sw], gTv[:, k4 * 11:(k4 + 1) * 11, s0:s0 + sw],
                             writes=[r_gs])
                for n in range(16):
                    _, w_t, w_r = next(ws)
                    for b0 in range(0, sw, 512):
                        wd = min(512, sw - b0)
                        p_t, p_r = ps.nxt()
                        for kc in range(44):
                            self.mm(p_t[:, :wd], w_t[:, kc, :], gs[:, kc, b0:b0 + wd], kc == 0, kc == 43,
                                    [w_r, r_gs], [p_r])
                        self.resid_update(xo, p_t, p_r, wd, n, s0 + b0, mc[:, s, 5, n:n + 1], r_mc)
            P.flush()

    def phase_final(self):
        P = self.P
        with ExitStack() as st:
            nc = self.nc
            gfs = self.sb(st, [128, 16], F32)
            idf = self.sb(st, [128, 128], F32)
            ones = self.sb(st, [128, 128], BF16)
            r_g, r_id, r_ones = Res(), Res(), Res()
            self.dma('sp', gfs[:], self.gf, writes=[r_g])
            self.dma('sp', idf[:], self.ident, writes=[r_id])
            self.memset('pool', ones[:], 1.0, [r_ones])
            xb = Ring(st, nc, 'xb', [128, 16, 128], F32, 2)
            sq = Ring(st, nc, 'sq', [128, 16, 128], BF16, 2)
            ss = Ring(st, nc, 'ss', [128, 128], F32, 2, psum=True)
            rs = Ring(st, nc, 'rs', [128, 128], F32, 2)
            tm = Ring(st, nc, 'tm', [128, 16, 128], F32, 2)
            ps = Ring(st, nc, 'tp', [128, 4, 128], F32, 4, psum=True)
            ot = Ring(st, nc, 'ot', [128, D], F32, 2, nsub=4)
            for i in range(16):
                x_t, x_r = xb.nxt()
                self.dma('sp', x_t[:], self.xTv[:, :, i * 128:(i + 1) * 128], writes=[x_r])
                q_t, q_r = sq.nxt()
                self.act(q_t[:], x_t[:], AF.Square, [x_r], [q_r])
                s_t, s_r = ss.nxt()
                for c in range(16):
                    self.mm(s_t[:], ones[:], q_t[:, c, :], c == 0, c == 15, [q_r, r_ones], [s_r])
                r_t, r_r = rs.nxt()
                self.ts('dve', r_t[:], s_t[:], 1.0 / D, EPS, ALU.mult, ALU.add, [s_r], [r_r])
                self.act(r_t[:], r_t[:], AF.Sqrt, [r_r], [r_r])
                self.P.op('dve', lambda e, r_t=r_t: e.reciprocal(out=r_t[:], in_=r_t[:]), [r_r], [r_r])
                t_t, t_r = tm.nxt()
                self.tt('dve', t_t[:], x_t[:], r_t[:].unsqueeze(1).to_broadcast([128, 16, 128]), ALU.mult,
                        [x_r, r_r], [t_r])
                self.tt('dve', t_t[:], t_t[:], gfs[:].unsqueeze(2).to_broadcast([128, 16, 128]), ALU.mult,
                        [t_r, r_g], [t_r])
                o_t, o_r = ot.nxt()
                for q in range(4):
                    p_t, p_r = ps.nxt()
                    for j in range(4):
                        c = q * 4 + j
                        self.tr(p_t[:, j, :], t_t[:, c, :], idf[:], [t_r, r_id], [p_r])
                    self.cp('act' if q % 2 else 'dve', o_t[:, q * 512:(q + 1) * 512],
                            p_t[:].rearrange("p a b -> p (a b)"), [p_r], [o_r[q]])
                self.dma(ST_Q, self.out[i * 128:(i + 1) * 128, :], o_t[:], reads=o_r)
            P.flush()

    def dump(self, name, src):
        self.dma('sp', self.dbg_out[name], src)
        self.P.flush()

    def build(self, stop_after=None):
        nc = self.nc
        with ExitStack() as st:
            self.P = Prog(nc, st)
            self.scb = self.sb(st, [128, 16, 2], BF16, 'scb')
            mcs = [self.sb(st, [128, 2, 6, 16], F32, 'mc') for _ in range(2)]
            self.phase_transpose_in(mcs[0])
            for l in range(self.NL):
                do_ctx = l < L - 1
                mc = mcs[l % 2]
                mc_next = mcs[(l + 1) % 2] if l + 1 < self.NL else None
                self.phase_norm_inproj(l, mc)
                self.phase_attn(l, do_ctx)
                self.phase_conv(l, do_ctx)
                self.phase_merge(l, do_ctx)
                self.phase_wo(l, mc, do_ctx)
                self.phase_ffn(l, mc, do_ctx, mc_next)
                self.phase_down(l, mc, do_ctx)
            if self.dbg:
                for name, shape, dt in self.dbg:
                    self.dump(name, getattr(self, name))
            self.phase_final()
        return nc


_ROT_SRC = np.concatenate([np.arange(32, 64), np.arange(0, 32), np.arange(96, 128), np.arange(64, 96)])
_ROT_SIGN = np.concatenate([-np.ones(32), np.ones(32), -np.ones(32), np.ones(32)]).astype(np.float32)


def _fm(v, nch):
    return np.ascontiguousarray(np.swapaxes(v.reshape(v.shape[:-1] + (nch, 128)), -1, -2))


def _wblocks(w, order_cols, bw):
    Kd = w.shape[0]
    wg = w[:, order_cols]
    nb = wg.shape[1] // bw
    return np.ascontiguousarray(wg.reshape(Kd // 128, 128, nb, bw).transpose(2, 1, 0, 3))


def _win_cols():
    cols = []
    o = 0
    for h in range(8):
        base = h * 128
        cols += [base + np.arange(128), base + _ROT_SRC]
    o = 1024
    for g in range(2):
        base = o + g * 128
        cols += [base + np.arange(128), base + _ROT_SRC]
    cols += [np.arange(1536, 2560), np.arange(2560, 3584), np.arange(4608, 7680), np.arange(7680, 13824),
             np.arange(1280, 1536), np.arange(3584, 4608)]
    return np.concatenate(cols)


def _gpos(half, t):
    return t if half == 0 else 4095 - t


def _rope_tables(half):
    t = np.arange(3584)
    g = _gpos(half, t)
    row = (g // 64).astype(np.float32)
    col = (g % 64).astype(np.float32)
    inv = (1.0 / (np.float32(10000.0) ** (np.arange(32, dtype=np.float32) / np.float32(32)))).astype(np.float32)
    ar = row[:, None] * inv
    ac = col[:, None] * inv
    ang = np.concatenate([ar, ar, ac, ac], axis=-1)
    cos = np.ones((TOK, 128), np.float32)
    sin = np.zeros((TOK, 128), np.float32)
    cos[:3584] = np.cos(ang)
    sin[:3584] = np.sin(ang) * _ROT_SIGN[None, :]
    return np.ascontiguousarray(cos.T), np.ascontiguousarray(sin.T)


def _maskA():
    m = np.zeros((128, 2, 640), np.float32)
    a = np.arange(128)[:, None]
    kk = np.arange(384)[None, :]
    m[:, 0, :384] = np.where(np.abs(kk - a) <= 128, 0.0, NEGM)
    m[:, 1, :384] = np.where(np.abs(kk - 128 - a) <= 128, 0.0, NEGM)
    return m


def _biasB(rpb, half):
    out = np.zeros((L, 3, 128, 8, 896), np.float32)
    for var, (i, s) in enumerate(((0, 0), (1, 0), (5, 3))):
        tq = i * 128 + np.arange(128)
        tk = s * 128 + np.arange(640)
        gq = _gpos(half, tq)
        gk = _gpos(half, tk)
        rq, cq = gq // 64, gq % 64
        rk, ck = gk // 64, gk % 64
        rs = np.clip(rq - 4, 0, 56)
        cst = np.clip(cq - 8, 0, 48)
        ok = ((rk[None, :] >= rs[:, None]) & (rk[None, :] < rs[:, None] + 8)
              & (ck[None, :] >= cst[:, None]) & (ck[None, :] < cst[:, None] + 16))
        ri = np.clip(rk[None, :] - rq[:, None] + 7, 0, 14)
        ci = np.clip(ck[None, :] - cq[:, None] + 15, 0, 30)
        for l in range(L):
            for h in range(8):
                out[l, var, :, h, :640] = np.where(ok, rpb[l, h][ri, ci], NEGM)
    return out


_CACHE = {}


def _get_nc():
    if 'nc' not in _CACHE:
        _CACHE['nc'] = K().build()
    return _CACHE['nc']


def prep_inputs(x, c, ctx, c_ctx, w_mod, b_mod, norm1, w_in, sink, rpb, conv_c, w_pa, w_pb, w_pc, w_o, norm2, w_up,
                conv_f, w_down, norm_f):
    f = lambda a: np.asarray(a, dtype=np.float32)
    x, c, ctx, c_ctx = f(x), f(c), f(ctx), f(c_ctx)
    shared = {}
    shared['ident'] = np.eye(128, dtype=np.float32)
    wm = f(w_mod)
    shared['wmod'] = np.ascontiguousarray(wm.reshape(L, 16, 128, 48, 256).transpose(0, 3, 2, 1, 4))
    shared['bmod'] = _fm(f(b_mod), 96)
    shared['g1'] = _fm(f(norm1), 16)
    shared['g2'] = _fm(f(norm2), 16)
    shared['gf'] = _fm(f(norm_f), 16)
    wcols = _win_cols()
    wi = f(w_in)
    shared['win'] = np.stack([_wblocks(wi[l], wcols, 256) for l in range(L)])
    shared['sinkr'] = np.ascontiguousarray(np.broadcast_to(f(sink).reshape(1, L * 8), (128, L * 8)))
    shared['maskA'] = _maskA()
    for nm, w in (('wpa', w_pa), ('wpb', w_pb), ('wpc', w_pc)):
        shared[nm] = np.ascontiguousarray(f(w).reshape(L, 8, 128, D).transpose(0, 2, 1, 3))
    wo_ = f(w_o)
    shared['wo'] = np.stack([_wblocks(wo_[l], np.arange(D), 256) for l in range(L)])
    upcols = np.concatenate([np.concatenate([j * 128 + np.arange(128), 5632 + j * 128 + np.arange(128)])
                             for j in range(44)])
    wu = f(w_up)
    shared['wup'] = np.stack([_wblocks(wu[l], upcols, 256) for l in range(L)])
    wd_ = f(w_down)
    shared['wdn'] = np.stack([_wblocks(wd_[l], np.arange(D), 128) for l in range(L)])
    cc_ = f(conv_c)
    cf_ = f(conv_f)
    rp = f(rpb)
    per_half = []
    for half in range(2):
        d = {}
        d['cosT'], d['sinT'] = _rope_tables(half)
        d['biasB'] = _biasB(rp, half)
        cc = cc_ if half == 0 else cc_[:, ::-1, :]
        cf = cf_ if half == 0 else cf_[:, ::-1, :]
        d['convc'] = np.ascontiguousarray(cc.reshape(L, 3, 8, 128).transpose(0, 3, 2, 1))
        d['convf'] = np.ascontiguousarray(cf.reshape(L, 3, 88, 128).transpose(0, 3, 2, 1))
        per_half.append(d)
    in_maps = []
    for core in range(8):
        b, half = core // 2, core % 2
        m = dict(shared)
        m.update(per_half[half])
        xs = x[b] if half == 0 else x[b, ::-1]
        cs = ctx[b] if half == 0 else ctx[b, ::-1]
        m['xin'] = np.ascontiguousarray(np.concatenate([xs[:3584], cs], axis=0))
        cv = np.stack([c[b], c_ctx], axis=-1)
        m['csil'] = np.ascontiguousarray(cv.reshape(16, 128, 2).transpose(1, 0, 2))
        in_maps.append(m)
    return in_maps


def kernel(**inputs):
    in_maps = prep_inputs(**inputs)
    nc = _get_nc()
    res = run_bass_kernel_spmd(nc, in_maps, core_ids=list(range(8)))
    out = np.empty((4, 4096, D), np.float32)
    for core in range(8):
        b, half = core // 2, core % 2
        y = np.asarray(res.results[core]["out"], dtype=np.float32)
        if half == 0:
            out[b, :2048] = y
        else:
            out[b, 2048:] = y[::-1]
    return out
```

```python
import numpy as np
import concourse.bass as bass
import concourse.mybir as mybir

F32 = mybir.dt.float32
BF16 = mybir.dt.bfloat16
ALU = mybir.AluOpType
AF = mybir.ActivationFunctionType
AX = mybir.AxisListType

ENGS = ('pe', 'act', 'dve', 'pool', 'sp')
DMA_ENGS = ('sp', 'act', 'pool')
NDMASEM = 20


class Res:
    __slots__ = ('name', 'w', 'r')

    def __init__(self, name=''):
        self.name = name
        self.w = None
        self.r = []


class Op:
    __slots__ = ('eng', 'fn', 'deps', 'dma', 'sem', 'val', 'sig', 'pre')

    def __init__(self, eng, fn, dma):
        self.eng = eng
        self.fn = fn
        self.dma = dma
        self.deps = []
        self.sem = None
        self.val = 0
        self.sig = False
        self.pre = None


class Prog:
    def __init__(self, nc, stack, same_eng_sync=('act', 'dve', 'pool')):
        self.nc = nc
        self.same = set(same_eng_sync)
        self.q = {e: [] for e in ENGS}
        self.touched = []
        self.esem = {e: stack.enter_context(nc.semaphore('s_' + e)) for e in ENGS if e != 'sp'}
        self.ecnt = {e: 0 for e in ENGS}
        self.dsem = {e: [stack.enter_context(nc.semaphore('d_%s%d' % (e, i))) for i in range(NDMASEM)]
                     for e in DMA_ENGS}
        self.dcnt = {e: [0] * NDMASEM for e in DMA_ENGS}
        self.drr = {e: 0 for e in DMA_ENGS}
        self.bar = stack.enter_context(nc.semaphore('bar'))
        self.nbar = 0
        self.waited = {e: {} for e in ENGS}
        self.nops = 0

    def eng_obj(self, e):
        nc = self.nc
        return {'pe': nc.tensor, 'act': nc.scalar, 'dve': nc.vector, 'pool': nc.gpsimd, 'sp': nc.sync}[e]

    def op(self, eng, fn, reads=(), writes=(), dma=False):
        o = Op(eng, fn, dma)
        deps = []
        for r in reads:
            if r.w is not None:
                deps.append(r.w)
        for w in writes:
            if w.w is not None:
                deps.append(w.w)
            deps.extend(w.r)
        for r in reads:
            if not r.r and r.w is None:
                self.touched.append(r)
            r.r.append(o)
        for w in writes:
            if not w.r and w.w is None:
                self.touched.append(w)
            w.w = o
            w.r = []
        seen = set()
        for d in deps:
            if d is o or id(d) in seen:
                continue
            seen.add(id(d))
            if d.eng == eng and not d.dma and eng not in self.same:
                continue
            o.deps.append(d)
            d.sig = True
        if dma:
            o.sig = True
        self.q[eng].append(o)
        self.nops += 1
        return o

    def dma(self, eng, out, in_, reads=(), writes=()):
        return self.op(eng, lambda e: e.dma_start(out=out, in_=in_), reads, writes, dma=True)

    def flush(self):
        nc = self.nc
        for e in ENGS:
            if e == 'sp':
                continue
            last = None
            for o in self.q[e]:
                if not o.dma:
                    last = o
            if last is not None:
                last.sig = True
        for e in ENGS:
            for o in self.q[e]:
                if o.dma:
                    k = self.drr[e]
                    self.drr[e] = (k + 1) % NDMASEM
                    if self.dcnt[e][k] > 0:
                        o.pre = (self.dsem[e][k], self.dcnt[e][k])
                    self.dcnt[e][k] += 16
                    o.sem = self.dsem[e][k]
                    o.val = self.dcnt[e][k]
                elif o.sig:
                    self.ecnt[e] += 1
                    o.sem = self.esem[e]
                    o.val = self.ecnt[e]
        self.nbar += 1
        nactive = len(ENGS)
        bar_target = self.nbar * nactive

        def make(e):
            def body(eng):
                wt = self.waited[e]

                def wait(sem, val):
                    if wt.get(id(sem), 0) >= val:
                        return
                    eng.wait_ge(sem, val)
                    wt[id(sem)] = val

                for o in self.q[e]:
                    need = {}
                    for d in o.deps:
                        k = id(d.sem)
                        if k not in need or need[k][1] < d.val:
                            need[k] = (d.sem, d.val)
                    if o.pre is not None:
                        k = id(o.pre[0])
                        if k not in need or need[k][1] < o.pre[1]:
                            need[k] = o.pre
                    for sem, val in need.values():
                        wait(sem, val)
                    ins = o.fn(eng)
                    if o.sem is not None:
                        ins.then_inc(o.sem, 16 if o.dma else 1)
                if e in DMA_ENGS:
                    for k in range(NDMASEM):
                        if self.dcnt[e][k] > 0:
                            wait(self.dsem[e][k], self.dcnt[e][k])
                if e != 'sp' and self.ecnt[e] > 0:
                    wait(self.esem[e], self.ecnt[e])
                eng.sem_inc(self.bar, 1)
                eng.wait_ge(self.bar, bar_target)
            return body

        with nc.Block() as block:
            block.tensor(make('pe'))
            block.scalar(make('act'))
            block.vector(make('dve'))
            block.gpsimd(make('pool'))
            block.sync(make('sp'))
        for r in self.touched:
            r.w = None
            r.r = []
        self.touched = []
        self.q = {e: [] for e in ENGS}

from contextlib import ExitStack
from concourse.bass_utils import run_bass_kernel_spmd
import ml_dtypes

L = 4
D = 2048
TOK = 3840
CTX0 = 3584
NXIN = [28, 25, 22, 19]
NMIX = [26, 23, 20, 17]
NOUT = [25, 22, 19, 16]
EPS = 1e-6
QS = 128.0 ** -0.5
NEGM = -30000.0
QA, KA, QB, KB, UC, GPRE, GPOST, ZA, ZB, ZC = 0, 1024, 1280, 2304, 3328, 4352, 5376, 6400, 8448, 10496
PROJ_ROWS = 12544
BLOCKS = ([('rope', QA + h * 128, QS, True) for h in range(8)] + [('rope', KA + g * 128, 1.0, False) for g in range(2)]
          + [('plain', QB + j * 256, QS, True) for j in range(4)] + [('plain', KB + j * 256, 1.0, False) for j in range(4)]
          + [('plain', UC + j * 256, 1.0, False) for j in range(8)] + [('plain', GPOST + j * 256, 1.0, True) for j in range(4)]
          + [('plain', ZA + j * 256, 1.0, True) for j in range(24)]
          + [('tok', j * 256, 1.0, False) for j in range(5)])
assert len(BLOCKS) == 59

_uid = [0]
ST_Q = 'pool'


def uid():
    _uid[0] += 1
    return _uid[0]


class Ring:
    def __init__(self, st, nc, name, shape, dt, n, psum=False, nsub=1):
        mk = nc.psum_tensor if psum else nc.sbuf_tensor
        self.t = [st.enter_context(mk("%s%d_%d" % (name, i, uid()), shape, dt)) for i in range(n)]
        self.r = [[Res() for _ in range(nsub)] for _ in range(n)]
        self.nsub = nsub
        self.i = -1

    def nxt(self):
        self.i = (self.i + 1) % len(self.t)
        r = self.r[self.i]
        return self.t[self.i], (r[0] if self.nsub == 1 else r)


class K:
    def __init__(self, NL=4, dbg=None):
        self.NL = NL
        self.dbg = dbg
        nc = self.nc = bass.Bass("TRN2", target_bir_lowering=False)

        def din(name, shape, dt=F32):
            return nc.dram_tensor(name, shape, dt, kind="ExternalInput").ap()
        self.xin = din("xin", [TOK, D])
        self.csil = din("csil", [128, 16, 2])
        self.ident = din("ident", [128, 128])
        self.wmod = din("wmod", [L, 48, 128, 16, 256])
        self.bmod = din("bmod", [L, 128, 96])
        self.g1 = din("g1", [L, 128, 16])
        self.g2 = din("g2", [L, 128, 16])
        self.gf = din("gf", [128, 16])
        self.win = din("win", [L, 59, 128, 16, 256])
        self.cosT = din("cosT", [128, TOK])
        self.sinT = din("sinT", [128, TOK])
        self.sinkr = din("sinkr", [128, L * 8])
        self.maskA = din("maskA", [128, 2, 640])
        self.biasB = din("biasB", [L, 3, 128, 8, 896])
        self.convc = din("convc", [L, 128, 8, 3])
        self.convf = din("convf", [L, 128, 88, 3])
        self.wpa = din("wpa", [L, 128, 8, D])
        self.wpb = din("wpb", [L, 128, 8, D])
        self.wpc = din("wpc", [L, 128, 8, D])
        self.wo = din("wo", [L, 8, 128, 16, 256])
        self.wup = din("wup", [L, 44, 128, 16, 256])
        self.wdn = din("wdn", [L, 16, 128, 44, 128])
        self.out = nc.dram_tensor("out", [2048, D], F32, kind="ExternalOutput").ap()
        self.xT = nc.dram_tensor("xT", [D, TOK], F32).ap()
        self.proj = nc.dram_tensor("proj", [PROJ_ROWS, TOK], BF16).ap()
        self.vtok = nc.dram_tensor("vtok", [TOK, 1280], BF16).ap()
        self.mix = nc.dram_tensor("mix", [3072, TOK], BF16).ap()
        self.mT = nc.dram_tensor("mT", [D, TOK], BF16).ap()
        self.gT = nc.dram_tensor("gT", [5632, TOK], BF16).ap()
        self.xTv = self.xT.rearrange("(c p) t -> p c t", p=128)
        self.dbg_out = {}
        if dbg:
            for name, shape, dt in dbg:
                self.dbg_out[name] = nc.dram_tensor("dbg_" + name, shape, dt, kind="ExternalOutput").ap()

    def sb(self, st, shape, dt, name='t'):
        return st.enter_context(self.nc.sbuf_tensor("%s_%d" % (name, uid()), shape, dt))

    def mm(self, out, lhsT, rhs, start, stop, reads, writes):
        return self.P.op('pe', lambda e: e.matmul(out, lhsT=lhsT, rhs=rhs, start=start, stop=stop), reads, writes)

    def tr(self, out, in_, ident, reads, writes):
        return self.P.op('pe', lambda e: e.transpose(out, in_, ident), reads, writes)

    def act(self, out, in_, func, reads, writes, bias=None, scale=None, accum=None):
        kw = {}
        if bias is not None:
            kw['bias'] = bias
        if scale is not None:
            kw['scale'] = scale
        if accum is not None:
            kw['accum_out'] = accum
        return self.P.op('act', lambda e: e.activation(out=out, in_=in_, func=func, **kw), reads, writes)

    def tt(self, eng, out, in0, in1, op, reads, writes):
        return self.P.op(eng, lambda e: e.tensor_tensor(out=out, in0=in0, in1=in1, op=op), reads, writes)

    def ts(self, eng, out, in0, s1, s2, op0, op1, reads, writes):
        if s2 is None:
            return self.P.op(eng, lambda e: e.tensor_scalar(out=out, in0=in0, scalar1=s1, scalar2=None, op0=op0),
                             reads, writes)
        return self.P.op(eng, lambda e: e.tensor_scalar(out=out, in0=in0, scalar1=s1, scalar2=s2, op0=op0, op1=op1),
                         reads, writes)

    def stt(self, eng, out, in0, scalar, in1, op0, op1, reads, writes):
        return self.P.op(eng, lambda e: e.scalar_tensor_tensor(out=out, in0=in0, scalar=scalar, in1=in1,
                                                               op0=op0, op1=op1), reads, writes)

    def cp(self, eng, out, in_, reads, writes):
        if eng == 'act':
            return self.act(out, in_, AF.Copy, reads, writes)
        return self.P.op(eng, lambda e: e.tensor_copy(out=out, in_=in_), reads, writes)

    def memset(self, eng, ap, val, writes):
        return self.P.op(eng, lambda e: e.memset(ap, val), [], writes)

    def dma(self, eng, out, in_, reads=(), writes=()):
        return self.P.dma(eng, out, in_, reads, writes)

    def wstream(self, ring, n, src_fn, depth=2):
        slots = {}

        def issue(i):
            w_t, w_r = ring.nxt()
            self.dma('pool', w_t[:], src_fn(i), writes=[w_r])
            slots[i] = (w_t, w_r)
        for i in range(min(depth, n)):
            issue(i)
        for i in range(n):
            if i + depth < n:
                issue(i + depth)
            w_t, w_r = slots.pop(i)
            yield i, w_t, w_r

    def phase_transpose_in(self, mc0):
        P = self.P
        with ExitStack() as st:
            idf = self.sb(st, [128, 128], F32)
            r_id = Res()
            self.dma('sp', idf[:], self.ident, writes=[r_id])
            scf = self.sb(st, [128, 16, 2], F32)
            r_sc = Res()
            self.dma('sp', scf[:], self.csil, writes=[r_sc])
            self.act(scf[:], scf[:], AF.Silu, [r_sc], [r_sc])
            self.cp('dve', self.scb[:], scf[:], [r_sc], [r_sc])
            P.flush()
            units = self.mod_units(0, st, mc0)
            xt = Ring(st, self.nc, 'xt', [128, D], F32, 2)
            ps = Ring(st, self.nc, 'tp', [128, 4, 128], F32, 4, psum=True)
            ot = Ring(st, self.nc, 'ot', [128, 16, 128], F32, 2, nsub=4)
            for i in range(TOK // 128):
                x_t, x_r = xt.nxt()
                self.dma('sp', x_t[:], self.xin[i * 128:(i + 1) * 128, :], writes=[x_r])
                o_t, o_r = ot.nxt()
                for q in range(4):
                    p_t, p_r = ps.nxt()
                    for j in range(4):
                        c = q * 4 + j
                        self.tr(p_t[:, j, :], x_t[:, c * 128:(c + 1) * 128], idf[:], [x_r, r_id], [p_r])
                    self.cp('act' if q % 2 else 'dve', o_t[:, q * 4:(q + 1) * 4, :], p_t[:], [p_r], [o_r[q]])
                self.dma(ST_Q, self.xTv[:, :, i * 128:(i + 1) * 128], o_t[:], reads=o_r)
                next(units, None)
                next(units, None)
            for _ in units:
                pass
            P.flush()

    def mod_units(self, l, st, mc):
        bm = self.sb(st, [128, 96], F32)
        gg = self.sb(st, [128, 2, 16], F32)
        raw = self.sb(st, [128, 96, 2], F32)
        r_bm, r_gg = Res(), Res()
        self.dma('sp', bm[:], self.bmod[l], writes=[r_bm])
        self.dma('sp', gg[:, 0, :], self.g1[l], writes=[r_gg])
        self.dma('sp', gg[:, 1, :], self.g2[l], writes=[r_gg])
        wm = Ring(st, self.nc, 'wm', [128, 16, 256], BF16, 3)
        ps = Ring(st, self.nc, 'mp', [128, 2], F32, 2, psum=True)
        r_raw = [Res() for _ in range(96)]
        scb = self.scb
        for ch, w_t, w_r in self.wstream(wm, 48, lambda i: self.wmod[l, i]):
            for cc in range(2):
                col = ch * 2 + cc
                p_t, p_r = ps.nxt()
                for kc in range(16):
                    self.mm(p_t[:], w_t[:, kc, cc * 128:(cc + 1) * 128], scb[:, kc, :], kc == 0, kc == 15,
                            [w_r], [p_r])
                self.ts('dve', raw[:, col, :], p_t[:], bm[:, col:col + 1], None, ALU.add, None,
                        [p_r, r_bm], [r_raw[col]])
            yield
        rv = raw[:].rearrange("p (m c) s -> p m c s", m=6)
        r_mc = Res()
        for s_ in range(2):
            for half in range(2):
                m0 = half * 3
                self.stt('dve', mc[:, s_, m0 + 0, :], rv[:, m0 + 1, :, s_], 1.0, gg[:, half, :], ALU.add, ALU.mult,
                         r_raw + [r_gg], [r_mc])
                self.cp('dve', mc[:, s_, m0 + 1, :], rv[:, m0 + 0, :, s_], r_raw, [r_mc])
                self.cp('dve', mc[:, s_, m0 + 2, :], rv[:, m0 + 2, :, s_], r_raw, [r_mc])
        yield

    def norm_blocks(self, st, blocks, dst, Aof, Bof, ones, r_ones):
        xb = Ring(st, self.nc, 'xb', [128, 16, 128], F32, 2)
        sq = Ring(st, self.nc, 'sq', [128, 16, 128], BF16, 2)
        ss = Ring(st, self.nc, 'ss', [128, 128], F32, 2, psum=True)
        rs = Ring(st, self.nc, 'rs', [128, 128], F32, 2)
        tm = Ring(st, self.nc, 'tm', [128, 16, 128], F32, 2)
        r_dst = Res()
        for (tok0, s, col0) in blocks:
            x_t, x_r = xb.nxt()
            self.dma('sp', x_t[:], self.xTv[:, :, tok0:tok0 + 128], writes=[x_r])
            q_t, q_r = sq.nxt()
            self.act(q_t[:], x_t[:], AF.Square, [x_r], [q_r])
            s_t, s_r = ss.nxt()
            for c in range(16):
                self.mm(s_t[:], ones[:], q_t[:, c, :], c == 0, c == 15, [q_r, r_ones], [s_r])
            r_t, r_r = rs.nxt()
            self.ts('dve', r_t[:], s_t[:], 1.0 / D, EPS, ALU.mult, ALU.add, [s_r], [r_r])
            self.act(r_t[:], r_t[:], AF.Sqrt, [r_r], [r_r])
            self.P.op('dve', lambda e, r_t=r_t: e.reciprocal(out=r_t[:], in_=r_t[:]), [r_r], [r_r])
            t_t, t_r = tm.nxt()
            self.tt('dve', t_t[:], x_t[:], r_t[:].unsqueeze(1).to_broadcast([128, 16, 128]), ALU.mult,
                    [x_r, r_r], [t_r])
            for c in range(16):
                if c % 2 == 0:
                    self.act(dst[:, c, col0:col0 + 128], t_t[:, c, :], AF.Identity, [t_r], [],
                             bias=Bof(s, c), scale=Aof(s, c))
                else:
                    self.ts('dve', dst[:, c, col0:col0 + 128], t_t[:, c, :], Aof(s, c), Bof(s, c),
                            ALU.mult, ALU.add, [t_r], [])

    def phase_norm_inproj(self, l, mc):
        P = self.P
        nin = NXIN[l]
        T_in = nin * 128
        Th = T_in + 256
        with ExitStack() as st0:
            hT = self.sb(st0, [128, 16, Th], BF16, 'hT')
            with ExitStack() as st:
                ones = self.sb(st, [128, 128], BF16)
                r_ones = Res()
                self.memset('pool', ones[:], 1.0, [r_ones])
                blocks = [(i * 128, 0, i * 128) for i in range(nin)] + [(CTX0 + j * 128, 1, T_in + j * 128)
                                                                         for j in range(2)]
                self.norm_blocks(st, blocks, hT, lambda s, c: mc[:, s, 0, c:c + 1], lambda s, c: mc[:, s, 1, c:c + 1],
                                 ones, r_ones)
                P.flush()
            with ExitStack() as st:
                wr = Ring(st, self.nc, 'w', [128, 16, 256], BF16, 3)
                ps = Ring(st, self.nc, 'ps', [128, 512], F32, 6, psum=True)
                cs = Ring(st, self.nc, 'cs', [128, 2, 512], F32, 2)
                tmp = Ring(st, self.nc, 'tmp', [128, 2, 512], F32, 2)
                sg = Ring(st, self.nc, 'sg', [128, 512], BF16, 4)
                xb_full = [(b0, min(512, T_in - b0), b0) for b0 in range(0, T_in, 512)] + [(T_in, 256, CTX0)]
                Tq = NMIX[l] * 128
                xb_q = [(b0, min(512, Tq - b0), b0) for b0 in range(0, Tq, 512)] + [(T_in, 256, CTX0)]
                tiles = [(i * 128, i * 128) for i in range(nin)] + [(T_in + j * 128, CTX0 + j * 128) for j in range(2)]
                ev = 0
                for bi, w_t, w_r in self.wstream(wr, len(BLOCKS), lambda i: self.win[l, i]):
                    kind, dest, scale, qonly = BLOCKS[bi]
                    xblocks = xb_q if qonly else xb_full
                    if kind == 'rope':
                        for (c0, wd, t0) in xblocks:
                            pa, pa_r = ps.nxt()
                            pb, pb_r = ps.nxt()
                            for kc in range(16):
                                self.mm(pa[:, :wd], w_t[:, kc, 0:128], hT[:, kc, c0:c0 + wd], kc == 0, kc == 15,
                                        [w_r], [pa_r])
                            for kc in range(16):
                                self.mm(pb[:, :wd], w_t[:, kc, 128:256], hT[:, kc, c0:c0 + wd], kc == 0, kc == 15,
                                        [w_r], [pb_r])
                            c_t, c_r = cs.nxt()
                            self.dma('sp', c_t[:, 0, :wd], self.cosT[:, t0:t0 + wd], writes=[c_r])
                            self.dma('sp', c_t[:, 1, :wd], self.sinT[:, t0:t0 + wd], writes=[c_r])
                            m_t, m_r = tmp.nxt()
                            self.stt('dve', m_t[:, 0, :wd], pa[:, :wd], scale, c_t[:, 0, :wd], ALU.mult, ALU.mult,
                                     [pa_r, c_r], [m_r])
                            self.stt('dve', m_t[:, 1, :wd], pb[:, :wd], scale, c_t[:, 1, :wd], ALU.mult, ALU.mult,
                                     [pb_r, c_r], [m_r])
                            g_t, g_r = sg.nxt()
                            self.tt('dve', g_t[:, :wd], m_t[:, 0, :wd], m_t[:, 1, :wd], ALU.add, [m_r], [g_r])
                            self.dma(ST_Q, self.proj[dest:dest + 128, t0:t0 + wd], g_t[:, :wd], reads=[g_r])
                    elif kind == 'plain':
                        for j in range(2):
                            for (c0, wd, t0) in xblocks:
                                pa, pa_r = ps.nxt()
                                for kc in range(16):
                                    self.mm(pa[:, :wd], w_t[:, kc, j * 128:(j + 1) * 128], hT[:, kc, c0:c0 + wd],
                                            kc == 0, kc == 15, [w_r], [pa_r])
                                g_t, g_r = sg.nxt()
                                ev += 1
                                if ev % 2:
                                    self.act(g_t[:, :wd], pa[:, :wd], AF.Copy, [pa_r], [g_r], scale=scale)
                                else:
                                    self.ts('dve', g_t[:, :wd], pa[:, :wd], scale, None, ALU.mult, None, [pa_r], [g_r])
                                self.dma(ST_Q, self.proj[dest + j * 128:dest + (j + 1) * 128, t0:t0 + wd], g_t[:, :wd],
                                         reads=[g_r])
                    else:
                        for (c0, t0) in tiles:
                            pa, pa_r = ps.nxt()
                            for kc in range(16):
                                self.mm(pa[:, :256], hT[:, kc, c0:c0 + 128], w_t[:, kc, :], kc == 0, kc == 15,
                                        [w_r], [pa_r])
                            g_t, g_r = sg.nxt()
                            ev += 1
                            self.cp('act' if ev % 2 else 'dve', g_t[:, :256], pa[:, :256], [pa_r], [g_r])
                            self.dma(ST_Q, self.vtok[t0:t0 + 128, dest:dest + 256], g_t[:, :256], reads=[g_r])
                P.flush()

    def phase_attn(self, l, do_ctx):
        P = self.P
        nmix = NMIX[l]
        proj, vtok, mix = self.proj, self.vtok, self.mix
        with ExitStack() as st:
            nc = self.nc
            kcA = self.sb(st, [128, 2, 256], BF16)
            vcA = self.sb(st, [128, 2, 256], BF16)
            kcB = self.sb(st, [128, 8, 256], BF16)
            vcB = self.sb(st, [128, 2, 1024], BF16)
            snk = self.sb(st, [128, 8], F32)
            mA = self.sb(st, [128, 2, 640], F32)
            idf = self.sb(st, [128, 128], F32)
            idb = self.sb(st, [128, 128], BF16)
            r_c, r_sink, r_mA, r_idf, r_id = Res(), Res(), Res(), Res(), Res()
            self.dma('sp', kcA[:], proj[KA:KA + 256, CTX0:CTX0 + 256].rearrange("(g p) t -> p g t", p=128), writes=[r_c])
            self.dma('sp', kcB[:], proj[KB:KB + 1024, CTX0:CTX0 + 256].rearrange("(g p) t -> p g t", p=128), writes=[r_c])
            self.dma('sp', vcA[:], vtok[CTX0:CTX0 + 256, 0:256].rearrange("(c p) v -> p c v", p=128), writes=[r_c])
            self.dma('sp', vcB[:], vtok[CTX0:CTX0 + 256, 256:1280].rearrange("(c p) v -> p c v", p=128), writes=[r_c])
            self.dma('sp', snk[:], self.sinkr[:, l * 8:(l + 1) * 8], writes=[r_sink])
            self.dma('sp', mA[:], self.maskA, writes=[r_mA])
            self.dma('sp', idf[:], self.ident, writes=[r_idf])
            self.cp('dve', idb[:], idf[:], [r_idf], [r_id])
            mAb = self.sb(st, [128, 2, 384], BF16)
            self.cp('dve', mAb[:], mA[:, :, 0:384], [r_mA], [r_mA])
            ND = 3
            R = dict(
                qa=Ring(st, nc, 'qa', [128, 8, 128], BF16, ND), ka=Ring(st, nc, 'ka', [128, 2, 384], BF16, ND),
                va=Ring(st, nc, 'va', [128, 3, 256], BF16, ND), qb=Ring(st, nc, 'qb', [128, 8, 128], BF16, ND),
                kb=Ring(st, nc, 'kb', [128, 8, 640], BF16, 2), vb=Ring(st, nc, 'vb', [128, 5, 1024], BF16, 2),
                bB=Ring(st, nc, 'bB', [128, 8, 896], F32, 2),
                S=Ring(st, nc, 'S', [128, 1024], F32, 2, psum=True),
                PTp=Ring(st, nc, 'PTp', [128, 8, 128], BF16, 2, psum=True),
                O=Ring(st, nc, 'O', [128, 512], F32, 2, psum=True),
                Sb=Ring(st, nc, 'Sb', [128, 896], F32, 4), Pm=Ring(st, nc, 'Pm', [128, 896], BF16, 4),
                Pn=Ring(st, nc, 'Pn', [128, 896], BF16, 4), st=Ring(st, nc, 'st', [128, 8], F32, 8),
                PTA=Ring(st, nc, 'PTA', [128, 5, 4, 128], BF16, 3), PTB=Ring(st, nc, 'PTB', [128, 7, 128], BF16, 4),
                oa=Ring(st, nc, 'oa', [128, 8, 128], BF16, 2), ob=Ring(st, nc, 'ob', [128, 8, 128], BF16, 2),
            )
            qtiles = [('x', i) for i in range(nmix)] + ([('c', j) for j in range(2)] if do_ctx else [])
            evc = [0]

            def ev_eng():
                evc[0] += 1
                return 'act' if evc[0] % 2 else 'dve'

            def load_tile(ti):
                kind, i = qtiles[ti]
                isx = kind == 'x'
                tcol = i * 128 if isx else CTX0 + i * 128
                T = dict(isx=isx, i=i, tcol=tcol)
                T['qa'] = R['qa'].nxt()
                T['qb'] = R['qb'].nxt()
                self.dma('sp', T['qa'][0][:], proj[QA:QA + 1024, tcol:tcol + 128].rearrange("(h p) t -> p h t", p=128),
                         writes=[T['qa'][1]])
                self.dma('sp', T['qb'][0][:], proj[QB:QB + 1024, tcol:tcol + 128].rearrange("(h p) t -> p h t", p=128),
                         writes=[T['qb'][1]])
                if isx:
                    sa = max(i - 1, 0)
                    sbt = max(i - 2, 0)
                    var = 0 if i == 0 else (1 if i == 1 else 2)
                    for nm in ('ka', 'va', 'kb', 'vb', 'bB'):
                        T[nm] = R[nm].nxt()
                    self.dma('sp', T['ka'][0][:], proj[KA:KA + 256, sa * 128:(sa + 3) * 128].rearrange("(g p) t -> p g t", p=128),
                             writes=[T['ka'][1]])
                    self.dma('sp', T['va'][0][:], vtok[sa * 128:(sa + 3) * 128, 0:256].rearrange("(c p) v -> p c v", p=128),
                             writes=[T['va'][1]])
                    self.dma('sp', T['kb'][0][:], proj[KB:KB + 1024, sbt * 128:(sbt + 5) * 128].rearrange("(g p) t -> p g t", p=128),
                             writes=[T['kb'][1]])
                    self.dma('sp', T['vb'][0][:], vtok[sbt * 128:(sbt + 5) * 128, 256:1280].rearrange("(c p) v -> p c v", p=128),
                             writes=[T['vb'][1]])
                    self.dma('sp', T['bB'][0][:], self.biasB[l, var], writes=[T['bB'][1]])
                T['oa'] = R['oa'].nxt()
                T['ob'] = R['ob'].nxt()
                return T

            tiles = {}
            jobs = []
            for ti in range(len(qtiles)):
                for mixer in ('A', 'B'):
                    for h in range(8):
                        jobs.append(dict(ti=ti, mixer=mixer, h=h))
            grpA = {}

            def stageA(J):
                ti, h, mixer = J['ti'], J['h'], J['mixer']
                if mixer == 'A' and h == 0 and ti == 0:
                    tiles[0] = load_tile(0)
                if mixer == 'A' and h == 5 and ti + 1 < len(qtiles):
                    tiles[ti + 1] = load_tile(ti + 1)
                T = tiles[ti]
                isx = T['isx']
                S_t, S_r = R['S'].nxt()
                stt, st_r = R['st'].nxt()
                J['st'] = (stt, st_r)
                if mixer == 'A':
                    g = h // 4
                    q, q_r = T['qa'][0][:, h, :], T['qa'][1]
                    W = 640 if isx else 256
                    if isx:
                        ka_t, ka_r = T['ka']
                        self.mm(S_t[:, 0:384], q, ka_t[:, g, :], True, False, [q_r, ka_r], [S_r])
                        self.mm(S_t[:, 0:384], idb[:], mAb[:, 0 if T['i'] == 0 else 1, :], False, True, [r_id, r_mA], [S_r])
                        self.mm(S_t[:, 384:512], q, kcA[:, g, 0:128], True, True, [q_r, r_c], [S_r])
                        self.mm(S_t[:, 512:640], q, kcA[:, g, 128:256], True, True, [q_r, r_c], [S_r])
                        table, t_r = None, None
                    else:
                        self.mm(S_t[:, 0:256], q, kcA[:, g, :], True, True, [q_r, r_c], [S_r])
                        table, t_r = None, None
                    sinkcol = snk[:, h:h + 1]
                else:
                    q, q_r = T['qb'][0][:, h, :], T['qb'][1]
                    W = 896 if isx else 256
                    if isx:
                        kb_t, kb_r = T['kb']
                        self.mm(S_t[:, 0:512], q, kb_t[:, h, 0:512], True, True, [q_r, kb_r], [S_r])
                        self.mm(S_t[:, 512:640], q, kb_t[:, h, 512:640], True, True, [q_r, kb_r], [S_r])
                        self.mm(S_t[:, 640:896], q, kcB[:, h, :], True, True, [q_r, r_c], [S_r])
                        table, t_r = T['bB'][0][:, h, :], T['bB'][1]
                    else:
                        self.mm(S_t[:, 0:256], q, kcB[:, h, :], True, True, [q_r, r_c], [S_r])
                        table, t_r = None, None
                    sinkcol = None
                J['W'] = W
                J['sink'] = sinkcol
                S_view = S_t[:, 0:W]
                if table is not None:
                    sb_t, sb_r = R['Sb'].nxt()
                    self.tt('dve', sb_t[:, :W], S_view, table, ALU.add, [S_r, t_r], [sb_r])
                    src, src_r = sb_t[:, :W], sb_r
                else:
                    src, src_r = S_view, S_r
                self.P.op('dve', lambda e: e.reduce_max(out=stt[:, 0:1], in_=src, axis=AX.X), [src_r], [st_r])
                if sinkcol is not None:
                    self.ts('dve', stt[:, 1:2], stt[:, 0:1], sinkcol, -1.0, ALU.max, ALU.mult, [st_r, r_sink], [st_r])
                else:
                    self.ts('dve', stt[:, 1:2], stt[:, 0:1], -1.0, None, ALU.mult, None, [st_r], [st_r])
                self.memset('dve', stt[:, 2:3], 0.0, [st_r])
                J['Sb'] = (src, src_r)

            def stageB1(J):
                stt, st_r = J['st']
                src, sb_r = J['Sb']
                W = J['W']
                pm_t, pm_r = R['Pm'].nxt()
                J['Pm'] = (pm_t, pm_r)
                self.act(pm_t[:, :W], src, AF.Exp, [sb_r, st_r], [pm_r, st_r], bias=stt[:, 1:2], scale=1.0,
                         accum=stt[:, 2:3])
                if J['sink'] is not None:
                    self.act(stt[:, 3:4], J['sink'], AF.Exp, [st_r, r_sink], [st_r], bias=stt[:, 1:2], scale=1.0)

            def stageB2(J):
                stt, st_r = J['st']
                pm_t, pm_r = J['Pm']
                W = J['W']
                if J['sink'] is not None:
                    self.tt('dve', stt[:, 2:3], stt[:, 2:3], stt[:, 3:4], ALU.add, [st_r], [st_r])
                self.P.op('dve', lambda e: e.reciprocal(out=stt[:, 5:6], in_=stt[:, 2:3]), [st_r], [st_r])
                pn_t, pn_r = R['Pn'].nxt()
                J['Pn'] = (pn_t, pn_r)
                self.act(pn_t[:, :W], pm_t[:, :W], AF.Copy, [pm_r, st_r], [pn_r], scale=stt[:, 5:6])

            def stageC(J):
                pn_t, pn_r = J['Pn']
                nch = J['W'] // 128
                pt_t, pt_r = R['PTp'].nxt()
                for c in range(nch):
                    self.tr(pt_t[:, c, :], pn_t[:, c * 128:(c + 1) * 128], idb[:], [pn_r, r_id], [pt_r])
                if J['mixer'] == 'A':
                    hh = J['h'] % 4
                    key = (J['ti'], J['h'] // 4)
                    if hh == 0:
                        grpA[key] = R['PTA'].nxt()
                    pta_t, pta_r = grpA[key]
                    self.cp(ev_eng(), pta_t[:, 0:nch, hh, :], pt_t[:, 0:nch, :], [pt_r], [pta_r])
                else:
                    ptb_t, ptb_r = R['PTB'].nxt()
                    J['PTB'] = (ptb_t, ptb_r)
                    self.cp(ev_eng(), ptb_t[:, 0:nch, :], pt_t[:, 0:nch, :], [pt_r], [ptb_r])

            curO = {}

            def stageD(J):
                T = tiles[J['ti']]
                isx, tcol = T['isx'], T['tcol']
                h = J['h']
                nch = J['W'] // 128
                if J['mixer'] == 'A':
                    if h % 4 != 3:
                        return
                    g = h // 4
                    pta_t, pta_r = grpA.pop((J['ti'], g))
                    O_t, O_r = R['O'].nxt()
                    for c in range(nch):
                        if isx and c < 3:
                            v, v_r = T['va'][0][:, c, g * 128:(g + 1) * 128], T['va'][1]
                        else:
                            cc = c - 3 if isx else c
                            v, v_r = vcA[:, cc, g * 128:(g + 1) * 128], r_c
                        self.mm(O_t[:], v, pta_t[:, c, :, :].rearrange("p h q -> p (h q)"), c == 0, c == nch - 1,
                                [v_r, pta_r], [O_r])
                    oa_t, oa_r = T['oa']
                    self.cp(ev_eng(), oa_t[:, g * 4:(g + 1) * 4, :], O_t[:].rearrange("p (h q) -> p h q", h=4),
                            [O_r], [oa_r])
                    if g == 1:
                        self.dma(ST_Q, mix[0:1024, tcol:tcol + 128].rearrange("(h p) t -> p h t", p=128), oa_t[:],
                                 reads=[oa_r])
                else:
                    ptb_t, ptb_r = J['PTB']
                    hq = h % 4
                    if hq == 0:
                        curO['B'] = R['O'].nxt()
                    O_t, O_r = curO['B']
                    for c in range(nch):
                        if isx and c < 5:
                            v, v_r = T['vb'][0][:, c, h * 128:(h + 1) * 128], T['vb'][1]
                        else:
                            cc = c - 5 if isx else c
                            v, v_r = vcB[:, cc, h * 128:(h + 1) * 128], r_c
                        self.mm(O_t[:, hq * 128:(hq + 1) * 128], v, ptb_t[:, c, :], c == 0, c == nch - 1,
                                [v_r, ptb_r], [O_r])
                    ob_t, ob_r = T['ob']
                    if hq == 3:
                        self.cp(ev_eng(), ob_t[:, h - 3:h + 1, :], O_t[:].rearrange("p (h q) -> p h q", h=4),
                                [O_r], [ob_r])
                    if h == 7:
                        self.dma(ST_Q, mix[1024:2048, tcol:tcol + 128].rearrange("(h p) t -> p h t", p=128), ob_t[:],
                                 reads=[ob_r])

            stages = [stageA, stageB1, stageB2, stageC, stageD]
            nj = len(jobs)
            for t in range(nj + len(stages) - 1):
                for k, fn in enumerate(stages):
                    j = t - k
                    if 0 <= j < nj:
                        fn(jobs[j])
            P.flush()

    def phase_conv(self, l, do_ctx):
        P = self.P
        nmix = NMIX[l]
        Tc = nmix * 128
        proj, mix = self.proj, self.mix
        with ExitStack() as st:
            nc = self.nc
            cw = self.sb(st, [128, 8, 3], F32)
            r_cw = Res()
            self.dma('sp', cw[:], self.convc[l], writes=[r_cw])
            W = Tc + 2
            inb = Ring(st, nc, 'cin', [128, 3, W], BF16, 2)
            pp = Ring(st, nc, 'cp', [128, W], F32, 2)
            oo = Ring(st, nc, 'co', [128, W], F32, 2)
            cb = Ring(st, nc, 'cb', [128, W], BF16, 2)
            segs = [(0, Tc, True)] + ([(CTX0, 256, False)] if do_ctx else [])
            k = 0
            for cc in range(8):
                for (t0, n, has_next) in segs:
                    eng = 'dve'
                    i_t, i_r = inb.nxt()
                    nl = n + 1 if has_next else n
                    self.memset(eng, i_t[:, :, 0:1], 0.0, [i_r])
                    if not has_next:
                        self.memset(eng, i_t[:, :, n + 1:n + 2], 0.0, [i_r])
                    for z, base in enumerate((UC, GPRE, GPOST)):
                        self.dma('sp', i_t[:, z, 1:1 + nl], proj[base + cc * 128:base + (cc + 1) * 128, t0:t0 + nl],
                                 writes=[i_r])
                    p_t, p_r = pp.nxt()
                    self.tt('dve', p_t[:, 0:n + 2], i_t[:, 0, 0:n + 2], i_t[:, 1, 0:n + 2], ALU.mult, [i_r], [p_r])
                    o_t, o_r = oo.nxt()
                    self.act(o_t[:, 0:n], p_t[:, 0:n], AF.Copy, [p_r, r_cw], [o_r], scale=cw[:, cc, 0:1])
                    self.stt('dve', o_t[:, 0:n], p_t[:, 1:n + 1], cw[:, cc, 1:2], o_t[:, 0:n], ALU.mult, ALU.add,
                             [p_r, r_cw, o_r], [o_r])
                    self.stt('dve', o_t[:, 0:n], p_t[:, 2:n + 2], cw[:, cc, 2:3], o_t[:, 0:n], ALU.mult, ALU.add,
                             [p_r, r_cw, o_r], [o_r])
                    c_t, c_r = cb.nxt()
                    self.tt('dve', c_t[:, 0:n], o_t[:, 0:n], i_t[:, 2, 1:n + 1], ALU.mult, [o_r, i_r], [c_r])
                    self.dma(ST_Q, mix[2048 + cc * 128:2048 + (cc + 1) * 128, t0:t0 + n], c_t[:, 0:n], reads=[c_r])
            P.flush()

    def phase_merge(self, l, do_ctx):
        P = self.P
        nmix = NMIX[l]
        Tm = nmix * 128
        proj, mix, mT = self.proj, self.mix, self.mT
        with ExitStack() as st:
            nc = self.nc
            wp = [self.sb(st, [128, 8, D], BF16, 'wp') for _ in range(3)]
            r_wp = [Res() for _ in range(3)]
            for z, src in enumerate((self.wpa, self.wpb, self.wpc)):
                for hf in range(2):
                    self.dma('pool', wp[z][:, hf * 4:(hf + 1) * 4, :], src[l, :, hf * 4:(hf + 1) * 4, :], writes=[r_wp[z]])
            mb = Ring(st, nc, 'mb', [128, 24, 512], BF16, 2)
            zb = Ring(st, nc, 'zb', [128, 3, 512], BF16, 2)
            sg = Ring(st, nc, 'sg', [128, 3, 512], F32, 2)
            ps = Ring(st, nc, 'ps', [128, 512], F32, 6, psum=True)
            t1 = Ring(st, nc, 't1', [128, 512], F32, 2)
            t2 = Ring(st, nc, 't2', [128, 512], F32, 2)
            t3 = Ring(st, nc, 't3', [128, 512], F32, 2)
            mo = Ring(st, nc, 'mo', [128, 512], BF16, 2)
            tblocks = [(b0, min(512, Tm - b0)) for b0 in range(0, Tm, 512)] + ([(CTX0, 256)] if do_ctx else [])
            mixv = mix.rearrange("(k p) t -> p k t", p=128)
            zv = proj[ZA:ZA + 6144, :].rearrange("(z c p) t -> p z c t", z=3, c=16, p=128)
            for (t0, wd) in tblocks:
                m_t, m_r = mb.nxt()
                for z in range(3):
                    self.dma('sp', m_t[:, z * 8:(z + 1) * 8, :wd], mixv[:, z * 8:(z + 1) * 8, t0:t0 + wd], writes=[m_r])
                for n in range(16):
                    z_t, z_r = zb.nxt()
                    self.dma('sp', z_t[:, :, :wd], zv[:, :, n, t0:t0 + wd], writes=[z_r])
                    s_t, s_r = sg.nxt()
                    self.act(s_t[:, :, :wd], z_t[:, :, :wd], AF.Sigmoid, [z_r], [s_r])
                    pz = []
                    for z in range(3):
                        p_t, p_r = ps.nxt()
                        for kc in range(8):
                            self.mm(p_t[:, :wd], wp[z][:, kc, n * 128:(n + 1) * 128], m_t[:, z * 8 + kc, :wd],
                                    kc == 0, kc == 7, [r_wp[z], m_r], [p_r])
                        pz.append((p_t, p_r))
                    a_t, a_r = t1.nxt()
                    b_t, b_r = t2.nxt()
                    c_t, c_r = t3.nxt()
                    self.tt('dve', a_t[:, :wd], pz[0][0][:, :wd], s_t[:, 0, :wd], ALU.mult, [pz[0][1], s_r], [a_r])
                    self.tt('dve', b_t[:, :wd], pz[1][0][:, :wd], s_t[:, 1, :wd], ALU.mult, [pz[1][1], s_r], [b_r])
                    self.tt('dve', c_t[:, :wd], pz[2][0][:, :wd], s_t[:, 2, :wd], ALU.mult, [pz[2][1], s_r], [c_r])
                    self.tt('dve', a_t[:, :wd], a_t[:, :wd], b_t[:, :wd], ALU.add, [a_r, b_r], [a_r])
                    o_t, o_r = mo.nxt()
                    self.tt('dve', o_t[:, :wd], a_t[:, :wd], c_t[:, :wd], ALU.add, [a_r, c_r], [o_r])
                    self.dma(ST_Q, mT[n * 128:(n + 1) * 128, t0:t0 + wd], o_t[:, :wd], reads=[o_r])
            P.flush()

    def resid_update(self, rings, p_t, p_r, wd, n, t0, gate, r_mc):
        xo = rings.nxt()
        x_t, x_r = xo
        src = self.xT[n * 128:(n + 1) * 128, t0:t0 + wd]
        self.dma('sp', x_t[:, :wd], src, writes=[x_r])
        self.stt('dve', x_t[:, :wd], p_t[:, :wd], gate, x_t[:, :wd], ALU.mult, ALU.add, [p_r, x_r, r_mc], [x_r])
        self.dma(ST_Q, src, x_t[:, :wd], reads=[x_r])

    def phase_wo(self, l, mc, do_ctx):
        P = self.P
        nmix = NMIX[l]
        Tm = nmix * 128
        Tt = Tm + (256 if do_ctx else 0)
        with ExitStack() as st:
            nc = self.nc
            ms = self.sb(st, [128, 16, Tt], BF16, 'ms')
            r_ms = Res()
            mTv = self.mT.rearrange("(k p) t -> p k t", p=128)
            for k4 in range(4):
                self.dma('sp', ms[:, k4 * 4:(k4 + 1) * 4, 0:Tm], mTv[:, k4 * 4:(k4 + 1) * 4, 0:Tm], writes=[r_ms])
            if do_ctx:
                self.dma('sp', ms[:, :, Tm:Tm + 256], mTv[:, :, CTX0:CTX0 + 256], writes=[r_ms])
            wr = Ring(st, nc, 'w', [128, 16, 256], BF16, 3)
            ps = Ring(st, nc, 'ps', [128, 512], F32, 4, psum=True)
            xo = Ring(st, nc, 'xo', [128, 512], F32, 4)
            r_mc = Res()
            tblocks = [(b0, min(512, Tm - b0), b0, 0) for b0 in range(0, Tm, 512)] + ([(Tm, 256, CTX0, 1)] if do_ctx else [])
            for wb, w_t, w_r in self.wstream(wr, 8, lambda i: self.wo[l, i]):
                for j in range(2):
                    n = wb * 2 + j
                    for (c0, wd, t0, s) in tblocks:
                        p_t, p_r = ps.nxt()
                        for kc in range(16):
                            self.mm(p_t[:, :wd], w_t[:, kc, j * 128:(j + 1) * 128], ms[:, kc, c0:c0 + wd],
                                    kc == 0, kc == 15, [w_r, r_ms], [p_r])
                        self.resid_update(xo, p_t, p_r, wd, n, t0, mc[:, s, 2, n:n + 1], r_mc)
            P.flush()

    def phase_ffn(self, l, mc, do_ctx, mc_next=None):
        P = self.P
        nmix = NMIX[l]
        nout = NOUT[l]
        Tx = nmix * 128
        Hc = Tx + 259
        cbase = Tx + 1
        with ExitStack() as st0:
            nc = self.nc
            hT = self.sb(st0, [128, 16, Hc], BF16, 'h2T')
            with ExitStack() as st:
                ones = self.sb(st, [128, 128], BF16)
                r_ones = Res()
                self.memset('pool', ones[:], 1.0, [r_ones])
                for col in (0, Tx + 1, Tx + 258):
                    self.memset('pool', hT[:, :, col:col + 1], 0.0, [])
                blocks = [(i * 128, 0, 1 + i * 128) for i in range(nmix)]
                if do_ctx:
                    blocks += [(CTX0 + j * 128, 1, Tx + 2 + j * 128) for j in range(2)]
                self.norm_blocks(st, blocks, hT, lambda s, c: mc[:, s, 3, c:c + 1], lambda s, c: mc[:, s, 4, c:c + 1],
                                 ones, r_ones)
                P.flush()
            with ExitStack() as st:
                cf = self.sb(st, [128, 88, 3], F32)
                r_cf = Res()
                self.dma('sp', cf[:], self.convf[l], writes=[r_cf])
                wr = Ring(st, nc, 'w', [128, 16, 256], BF16, 3)
                ps = Ring(st, nc, 'ps', [128, 512], F32, 6, psum=True)
                cg = Ring(st, nc, 'cg', [128, 2, 512], F32, 3)
                sl = Ring(st, nc, 'sl', [128, 512], F32, 2)
                go = Ring(st, nc, 'go', [128, 512], BF16, 3)
                To = nout * 128
                oblocks = [(o0, min(510, To - o0), o0) for o0 in range(0, To, 510)]
                if do_ctx:
                    oblocks += [(cbase, 256, CTX0)]
                units = self.mod_units(l + 1, st, mc_next) if mc_next is not None else iter(())
                for j, w_t, w_r in self.wstream(wr, 44, lambda i: self.wup[l, i]):
                    next(units, None)
                    for (c0, n, t0) in oblocks:
                        pg, pg_r = ps.nxt()
                        pv, pv_r = ps.nxt()
                        for kc in range(16):
                            self.mm(pg[:, :n + 2], w_t[:, kc, 0:128], hT[:, kc, c0:c0 + n + 2], kc == 0, kc == 15,
                                    [w_r], [pg_r])
                        for kc in range(16):
                            self.mm(pv[:, :n + 2], w_t[:, kc, 128:256], hT[:, kc, c0:c0 + n + 2], kc == 0, kc == 15,
                                    [w_r], [pv_r])
                        c_t, c_r = cg.nxt()
                        for z, (pz, pz_r, fcol) in enumerate(((pg, pg_r, j), (pv, pv_r, 44 + j))):
                            self.act(c_t[:, z, :n], pz[:, 1:n + 1], AF.Copy, [pz_r, r_cf], [c_r], scale=cf[:, fcol, 1:2])
                            self.stt('dve', c_t[:, z, :n], pz[:, 0:n], cf[:, fcol, 0:1], c_t[:, z, :n], ALU.mult, ALU.add,
                                     [pz_r, r_cf, c_r], [c_r])
                            self.stt('dve', c_t[:, z, :n], pz[:, 2:n + 2], cf[:, fcol, 2:3], c_t[:, z, :n], ALU.mult, ALU.add,
                                     [pz_r, r_cf, c_r], [c_r])
                        s_t, s_r = sl.nxt()
                        self.act(s_t[:, :n], c_t[:, 0, :n], AF.Silu, [c_r], [s_r])
                        g_t, g_r = go.nxt()
                        self.tt('dve', g_t[:, :n], s_t[:, :n], c_t[:, 1, :n], ALU.mult, [s_r, c_r], [g_r])
                        self.dma(ST_Q, self.gT[j * 128:(j + 1) * 128, t0:t0 + n], g_t[:, :n], reads=[g_r])
                for _ in units:
                    pass
                P.flush()

    def phase_down(self, l, mc, do_ctx):
        P = self.P
        To = NOUT[l] * 128
        h1 = ((To // 2 + 127) // 128) * 128
        passes = [[(0, h1, 0)], [(h1, To - h1, 0)] + ([(CTX0, 256, 1)] if do_ctx else [])]
        maxc = max(sum(w for _, w, _ in p) for p in passes)
        with ExitStack() as st:
            nc = self.nc
            gs = self.sb(st, [128, 44, maxc], BF16, 'gs')
            r_gs = Res()
            wr = Ring(st, nc, 'w', [128, 44, 128], BF16, 3)
            ps = Ring(st, nc, 'ps', [128, 512], F32, 4, psum=True)
            xo = Ring(st, nc, 'xo', [128, 512], F32, 4)
            r_mc = Res()
            gTv = self.gT.rearrange("(k p) t -> p k t", p=128)
            ws = self.wstream(wr, 16 * len(passes), lambda i: self.wdn[l, i % 16])
            for segs in passes:
                c0 = 0
                for (t0, w, s_) in segs:
                    for k4 in range(4):
                        self.dma('sp', gs[:, k4 * 11:(k4 + 1) * 11, c0:c0 + w], gTv[:, k4 * 11:(k4 + 1) * 11, t0:t0 + w],
                                 writes=[r_gs])
                    c0 += w
                for n in range(16):
                    _, w_t, w_r = next(ws)
                    c0 = 0
                    for (t0, w, s_) in segs:
                        for b0 in range(0, w, 512):
                            wd = min(512, w - b0)
                            p_t, p_r = ps.nxt()
                            for kc in range(44):
                                self.mm(p_t[:, :wd], w_t[:, kc, :], gs[:, kc, c0 + b0:c0 + b0 + wd], kc == 0, kc == 43,
                                        [w_r, r_gs], [p_r])
                            self.resid_update(xo, p_t, p_r, wd, n, t0 + b0, mc[:, s_, 5, n:n + 1], r_mc)
                        c0 += w
            P.flush()

    def phase_final(self):
        P = self.P
        with ExitStack() as st:
            nc = self.nc
            gfs = self.sb(st, [128, 16], F32)
            idf = self.sb(st, [128, 128], F32)
            ones = self.sb(st, [128, 128], BF16)
            r_g, r_id, r_ones = Res(), Res(), Res()
            self.dma('sp', gfs[:], self.gf, writes=[r_g])
            self.dma('sp', idf[:], self.ident, writes=[r_id])
            self.memset('pool', ones[:], 1.0, [r_ones])
            xb = Ring(st, nc, 'xb', [128, 16, 128], F32, 2)
            sq = Ring(st, nc, 'sq', [128, 16, 128], BF16, 2)
            ss = Ring(st, nc, 'ss', [128, 128], F32, 2, psum=True)
            rs = Ring(st, nc, 'rs', [128, 128], F32, 2)
            tm = Ring(st, nc, 'tm', [128, 16, 128], F32, 2)
            ps = Ring(st, nc, 'tp', [128, 4, 128], F32, 4, psum=True)
            ot = Ring(st, nc, 'ot', [128, D], F32, 2, nsub=4)
            for i in range(16):
                x_t, x_r = xb.nxt()
                self.dma('sp', x_t[:], self.xTv[:, :, i * 128:(i + 1) * 128], writes=[x_r])
                q_t, q_r = sq.nxt()
                self.act(q_t[:], x_t[:], AF.Square, [x_r], [q_r])
                s_t, s_r = ss.nxt()
                for c in range(16):
                    self.mm(s_t[:], ones[:], q_t[:, c, :], c == 0, c == 15, [q_r, r_ones], [s_r])
                r_t, r_r = rs.nxt()
                self.ts('dve', r_t[:], s_t[:], 1.0 / D, EPS, ALU.mult, ALU.add, [s_r], [r_r])
                self.act(r_t[:], r_t[:], AF.Sqrt, [r_r], [r_r])
                self.P.op('dve', lambda e, r_t=r_t: e.reciprocal(out=r_t[:], in_=r_t[:]), [r_r], [r_r])
                t_t, t_r = tm.nxt()
                self.tt('dve', t_t[:], x_t[:], r_t[:].unsqueeze(1).to_broadcast([128, 16, 128]), ALU.mult,
                        [x_r, r_r], [t_r])
                self.tt('dve', t_t[:], t_t[:], gfs[:].unsqueeze(2).to_broadcast([128, 16, 128]), ALU.mult,
                        [t_r, r_g], [t_r])
                o_t, o_r = ot.nxt()
                for q in range(4):
                    p_t, p_r = ps.nxt()
                    for j in range(4):
                        c = q * 4 + j
                        self.tr(p_t[:, j, :], t_t[:, c, :], idf[:], [t_r, r_id], [p_r])
                    self.cp('act' if q % 2 else 'dve', o_t[:, q * 512:(q + 1) * 512],
                            p_t[:].rearrange("p a b -> p (a b)"), [p_r], [o_r[q]])
                self.dma(ST_Q, self.out[i * 128:(i + 1) * 128, :], o_t[:], reads=o_r)
            P.flush()

    def dump(self, name, src):
        self.dma('sp', self.dbg_out[name], src)
        self.P.flush()

    def build(self, stop_after=None):
        nc = self.nc
        with ExitStack() as st:
            self.P = Prog(nc, st)
            self.scb = self.sb(st, [128, 16, 2], BF16, 'scb')
            mcs = [self.sb(st, [128, 2, 6, 16], F32, 'mc') for _ in range(2)]
            self.phase_transpose_in(mcs[0])
            for l in range(self.NL):
                do_ctx = l < L - 1
                mc = mcs[l % 2]
                mc_next = mcs[(l + 1) % 2] if l + 1 < self.NL else None
                self.phase_norm_inproj(l, mc)
                self.phase_attn(l, do_ctx)
                self.phase_conv(l, do_ctx)
                self.phase_merge(l, do_ctx)
                self.phase_wo(l, mc, do_ctx)
                self.phase_ffn(l, mc, do_ctx, mc_next)
                self.phase_down(l, mc, do_ctx)
            if self.dbg:
                for name, shape, dt in self.dbg:
                    self.dump(name, getattr(self, name))
            self.phase_final()
        return nc


_ROT_SRC = np.concatenate([np.arange(32, 64), np.arange(0, 32), np.arange(96, 128), np.arange(64, 96)])
_ROT_SIGN = np.concatenate([-np.ones(32), np.ones(32), -np.ones(32), np.ones(32)]).astype(np.float32)


def _fm(v, nch):
    return np.ascontiguousarray(np.swapaxes(v.reshape(v.shape[:-1] + (nch, 128)), -1, -2))


def _wblocks(w, order_cols, bw):
    Kd = w.shape[0]
    wg = w[:, order_cols]
    nb = wg.shape[1] // bw
    return np.ascontiguousarray(wg.reshape(Kd // 128, 128, nb, bw).transpose(2, 1, 0, 3))


def _win_cols():
    cols = []
    o = 0
    for h in range(8):
        base = h * 128
        cols += [base + np.arange(128), base + _ROT_SRC]
    o = 1024
    for g in range(2):
        base = o + g * 128
        cols += [base + np.arange(128), base + _ROT_SRC]
    cols += [np.arange(1536, 2560), np.arange(2560, 3584), np.arange(4608, 7680), np.arange(7680, 13824),
             np.arange(1280, 1536), np.arange(3584, 4608)]
    return np.concatenate(cols)


def _gpos(half, t):
    return t if half == 0 else 4095 - t


def _rope_tables(half):
    t = np.arange(3584)
    g = _gpos(half, t)
    row = (g // 64).astype(np.float32)
    col = (g % 64).astype(np.float32)
    inv = (1.0 / (np.float32(10000.0) ** (np.arange(32, dtype=np.float32) / np.float32(32)))).astype(np.float32)
    ar = row[:, None] * inv
    ac = col[:, None] * inv
    ang = np.concatenate([ar, ar, ac, ac], axis=-1)
    cos = np.ones((TOK, 128), np.float32)
    sin = np.zeros((TOK, 128), np.float32)
    cos[:3584] = np.cos(ang)
    sin[:3584] = np.sin(ang) * _ROT_SIGN[None, :]
    return np.ascontiguousarray(cos.T), np.ascontiguousarray(sin.T)


def _maskA():
    m = np.zeros((128, 2, 640), np.float32)
    a = np.arange(128)[:, None]
    kk = np.arange(384)[None, :]
    m[:, 0, :384] = np.where(np.abs(kk - a) <= 128, 0.0, NEGM)
    m[:, 1, :384] = np.where(np.abs(kk - 128 - a) <= 128, 0.0, NEGM)
    return m


def _biasB(rpb, half):
    out = np.zeros((L, 3, 128, 8, 896), np.float32)
    for var, (i, s) in enumerate(((0, 0), (1, 0), (5, 3))):
        tq = i * 128 + np.arange(128)
        tk = s * 128 + np.arange(640)
        gq = _gpos(half, tq)
        gk = _gpos(half, tk)
        rq, cq = gq // 64, gq % 64
        rk, ck = gk // 64, gk % 64
        rs = np.clip(rq - 4, 0, 56)
        cst = np.clip(cq - 8, 0, 48)
        ok = ((rk[None, :] >= rs[:, None]) & (rk[None, :] < rs[:, None] + 8)
              & (ck[None, :] >= cst[:, None]) & (ck[None, :] < cst[:, None] + 16))
        ri = np.clip(rk[None, :] - rq[:, None] + 7, 0, 14)
        ci = np.clip(ck[None, :] - cq[:, None] + 15, 0, 30)
        for l in range(L):
            for h in range(8):
                out[l, var, :, h, :640] = np.where(ok, rpb[l, h][ri, ci], NEGM)
    return out


_CACHE = {}


def _get_nc():
    if 'nc' not in _CACHE:
        _CACHE['nc'] = K().build()
    return _CACHE['nc']


def prep_inputs(x, c, ctx, c_ctx, w_mod, b_mod, norm1, w_in, sink, rpb, conv_c, w_pa, w_pb, w_pc, w_o, norm2, w_up,
                conv_f, w_down, norm_f):
    f = lambda a: np.asarray(a, dtype=np.float32)
    x, c, ctx, c_ctx = f(x), f(c), f(ctx), f(c_ctx)
    shared = {}
    shared['ident'] = np.eye(128, dtype=np.float32)
    wm = f(w_mod)
    shared['wmod'] = np.ascontiguousarray(wm.reshape(L, 16, 128, 48, 256).transpose(0, 3, 2, 1, 4))
    shared['bmod'] = _fm(f(b_mod), 96)
    shared['g1'] = _fm(f(norm1), 16)
    shared['g2'] = _fm(f(norm2), 16)
    shared['gf'] = _fm(f(norm_f), 16)
    wcols = _win_cols()
    wi = f(w_in)
    shared['win'] = np.stack([_wblocks(wi[l], wcols, 256) for l in range(L)])
    shared['sinkr'] = np.ascontiguousarray(np.broadcast_to(f(sink).reshape(1, L * 8), (128, L * 8)))
    shared['maskA'] = _maskA()
    for nm, w in (('wpa', w_pa), ('wpb', w_pb), ('wpc', w_pc)):
        shared[nm] = np.ascontiguousarray(f(w).reshape(L, 8, 128, D).transpose(0, 2, 1, 3))
    wo_ = f(w_o)
    shared['wo'] = np.stack([_wblocks(wo_[l], np.arange(D), 256) for l in range(L)])
    upcols = np.concatenate([np.concatenate([j * 128 + np.arange(128), 5632 + j * 128 + np.arange(128)])
                             for j in range(44)])
    wu = f(w_up)
    shared['wup'] = np.stack([_wblocks(wu[l], upcols, 256) for l in range(L)])
    wd_ = f(w_down)
    shared['wdn'] = np.stack([_wblocks(wd_[l], np.arange(D), 128) for l in range(L)])
    cc_ = f(conv_c)
    cf_ = f(conv_f)
    rp = f(rpb)
    per_half = []
    for half in range(2):
        d = {}
        d['cosT'], d['sinT'] = _rope_tables(half)
        d['biasB'] = _biasB(rp, half)
        cc = cc_ if half == 0 else cc_[:, ::-1, :]
        cf = cf_ if half == 0 else cf_[:, ::-1, :]
        d['convc'] = np.ascontiguousarray(cc.reshape(L, 3, 8, 128).transpose(0, 3, 2, 1))
        d['convf'] = np.ascontiguousarray(cf.reshape(L, 3, 88, 128).transpose(0, 3, 2, 1))
        per_half.append(d)
    in_maps = []
    for core in range(8):
        b, half = core // 2, core % 2
        m = dict(shared)
        m.update(per_half[half])
        xs = x[b] if half == 0 else x[b, ::-1]
        cs = ctx[b] if half == 0 else ctx[b, ::-1]
        m['xin'] = np.ascontiguousarray(np.concatenate([xs[:3584], cs], axis=0))
        cv = np.stack([c[b], c_ctx], axis=-1)
        m['csil'] = np.ascontiguousarray(cv.reshape(16, 128, 2).transpose(1, 0, 2))
        in_maps.append(m)
    return in_maps


def kernel(**inputs):
    in_maps = prep_inputs(**inputs)
    nc = _get_nc()
    res = run_bass_kernel_spmd(nc, in_maps, core_ids=list(range(8)))
    out = np.empty((4, 4096, D), np.float32)
    for core in range(8):
        b, half = core // 2, core % 2
        y = np.asarray(res.results[core]["out"], dtype=np.float32)
        if half == 0:
            out[b, :2048] = y
        else:
            out[b, 2048:] = y[::-1]
    return out
```

```python
import numpy as np
import concourse.bass as bass
import concourse.mybir as mybir

F32 = mybir.dt.float32
BF16 = mybir.dt.bfloat16
ALU = mybir.AluOpType
AF = mybir.ActivationFunctionType
AX = mybir.AxisListType

ENGS = ('pe', 'act', 'dve', 'pool', 'sp')
DMA_ENGS = ('sp', 'act', 'pool')
NDMASEM = 20


class Res:
    __slots__ = ('name', 'w', 'r')

    def __init__(self, name=''):
        self.name = name
        self.w = None
        self.r = []


class Op:
    __slots__ = ('eng', 'fn', 'deps', 'dma', 'sem', 'val', 'sig', 'pre')

    def __init__(self, eng, fn, dma):
        self.eng = eng
        self.fn = fn
        self.dma = dma
        self.deps = []
        self.sem = None
        self.val = 0
        self.sig = False
        self.pre = None


class Prog:
    def __init__(self, nc, stack, same_eng_sync=('act', 'dve', 'pool')):
        self.nc = nc
        self.same = set(same_eng_sync)
        self.q = {e: [] for e in ENGS}
        self.touched = []
        self.esem = {e: stack.enter_context(nc.semaphore('s_' + e)) for e in ENGS if e != 'sp'}
        self.ecnt = {e: 0 for e in ENGS}
        self.dsem = {e: [stack.enter_context(nc.semaphore('d_%s%d' % (e, i))) for i in range(NDMASEM)]
                     for e in DMA_ENGS}
        self.dcnt = {e: [0] * NDMASEM for e in DMA_ENGS}
        self.drr = {e: 0 for e in DMA_ENGS}
        self.bar = stack.enter_context(nc.semaphore('bar'))
        self.nbar = 0
        self.waited = {e: {} for e in ENGS}
        self.nops = 0

    def eng_obj(self, e):
        nc = self.nc
        return {'pe': nc.tensor, 'act': nc.scalar, 'dve': nc.vector, 'pool': nc.gpsimd, 'sp': nc.sync}[e]

    def op(self, eng, fn, reads=(), writes=(), dma=False):
        o = Op(eng, fn, dma)
        deps = []
        for r in reads:
            if r.w is not None:
                deps.append(r.w)
        for w in writes:
            if w.w is not None:
                deps.append(w.w)
            deps.extend(w.r)
        for r in reads:
            if not r.r and r.w is None:
                self.touched.append(r)
            r.r.append(o)
        for w in writes:
            if not w.r and w.w is None:
                self.touched.append(w)
            w.w = o
            w.r = []
        seen = set()
        for d in deps:
            if d is o or id(d) in seen:
                continue
            seen.add(id(d))
            if d.eng == eng and not d.dma and eng not in self.same:
                continue
            o.deps.append(d)
            d.sig = True
        if dma:
            o.sig = True
        self.q[eng].append(o)
        self.nops += 1
        return o

    def dma(self, eng, out, in_, reads=(), writes=()):
        return self.op(eng, lambda e: e.dma_start(out=out, in_=in_), reads, writes, dma=True)

    def flush(self):
        nc = self.nc
        for e in ENGS:
            if e == 'sp':
                continue
            last = None
            for o in self.q[e]:
                if not o.dma:
                    last = o
            if last is not None:
                last.sig = True
        for e in ENGS:
            for o in self.q[e]:
                if o.dma:
                    k = self.drr[e]
                    self.drr[e] = (k + 1) % NDMASEM
                    if self.dcnt[e][k] > 0:
                        o.pre = (self.dsem[e][k], self.dcnt[e][k])
                    self.dcnt[e][k] += 16
                    o.sem = self.dsem[e][k]
                    o.val = self.dcnt[e][k]
                elif o.sig:
                    self.ecnt[e] += 1
                    o.sem = self.esem[e]
                    o.val = self.ecnt[e]
        self.nbar += 1
        nactive = len(ENGS)
        bar_target = self.nbar * nactive

        def make(e):
            def body(eng):
                wt = self.waited[e]

                def wait(sem, val):
                    if wt.get(id(sem), 0) >= val:
                        return
                    eng.wait_ge(sem, val)
                    wt[id(sem)] = val

                for o in self.q[e]:
                    need = {}
                    for d in o.deps:
                        k = id(d.sem)
                        if k not in need or need[k][1] < d.val:
                            need[k] = (d.sem, d.val)
                    if o.pre is not None:
                        k = id(o.pre[0])
                        if k not in need or need[k][1] < o.pre[1]:
                            need[k] = o.pre
                    for sem, val in need.values():
                        wait(sem, val)
                    ins = o.fn(eng)
                    if o.sem is not None:
                        ins.then_inc(o.sem, 16 if o.dma else 1)
                if e in DMA_ENGS:
                    for k in range(NDMASEM):
                        if self.dcnt[e][k] > 0:
                            wait(self.dsem[e][k], self.dcnt[e][k])
                if e != 'sp' and self.ecnt[e] > 0:
                    wait(self.esem[e], self.ecnt[e])
                eng.sem_inc(self.bar, 1)
                eng.wait_ge(self.bar, bar_target)
            return body

        with nc.Block() as block:
            block.tensor(make('pe'))
            block.scalar(make('act'))
            block.vector(make('dve'))
            block.gpsimd(make('pool'))
            block.sync(make('sp'))
        for r in self.touched:
            r.w = None
            r.r = []
        self.touched = []
        self.q = {e: [] for e in ENGS}

from contextlib import ExitStack
from concourse.bass_utils import run_bass_kernel_spmd
import ml_dtypes

L = 4
D = 2048
TOK = 3840
CTX0 = 3584
NXIN = [26, 24, 21, 19]
NMIX = [24, 22, 19, 17]
NOUT = [24, 21, 19, 16]
EPS = 1e-6
QS = 128.0 ** -0.5
NEGM = -30000.0
QA, KA, QB, KB, UC, GPRE, GPOST, ZA, ZB, ZC = 0, 1024, 1280, 2304, 3328, 4352, 5376, 6400, 8448, 10496
PROJ_ROWS = 12544
BLOCKS = ([('rope', QA + h * 128, QS, True) for h in range(8)] + [('rope', KA + g * 128, 1.0, False) for g in range(2)]
          + [('plain', QB + j * 256, QS, True) for j in range(4)] + [('plain', KB + j * 256, 1.0, False) for j in range(4)]
          + [('plain', UC + j * 256, 1.0, False) for j in range(8)] + [('plain', GPOST + j * 256, 1.0, True) for j in range(4)]
          + [('plain', ZA + j * 256, 1.0, True) for j in range(24)]
          + [('tok', j * 256, 1.0, False) for j in range(5)])
assert len(BLOCKS) == 59

_uid = [0]
ST_Q = 'pool'


def uid():
    _uid[0] += 1
    return _uid[0]


class Ring:
    def __init__(self, st, nc, name, shape, dt, n, psum=False, nsub=1):
        mk = nc.psum_tensor if psum else nc.sbuf_tensor
        self.t = [st.enter_context(mk("%s%d_%d" % (name, i, uid()), shape, dt)) for i in range(n)]
        self.r = [[Res() for _ in range(nsub)] for _ in range(n)]
        self.nsub = nsub
        self.i = -1

    def nxt(self):
        self.i = (self.i + 1) % len(self.t)
        r = self.r[self.i]
        return self.t[self.i], (r[0] if self.nsub == 1 else r)


class K:
    def __init__(self, NL=4, dbg=None):
        self.NL = NL
        self.dbg = dbg
        nc = self.nc = bass.Bass("TRN2", target_bir_lowering=False)

        def din(name, shape, dt=F32):
            return nc.dram_tensor(name, shape, dt, kind="ExternalInput").ap()
        self.xin = din("xin", [TOK, D])
        self.csil = din("csil", [128, 16, 2])
        self.ident = din("ident", [128, 128])
        self.wmod = din("wmod", [L, 48, 128, 16, 256])
        self.bmod = din("bmod", [L, 128, 96])
        self.g1 = din("g1", [L, 128, 16])
        self.g2 = din("g2", [L, 128, 16])
        self.gf = din("gf", [128, 16])
        self.win = din("win", [L, 59, 128, 16, 256])
        self.cosT = din("cosT", [128, TOK])
        self.sinT = din("sinT", [128, TOK])
        self.sinkr = din("sinkr", [128, L * 8])
        self.maskA = din("maskA", [128, 2, 640])
        self.biasB = din("biasB", [L, 3, 128, 8, 896])
        self.convc = din("convc", [L, 128, 8, 3])
        self.convf = din("convf", [L, 128, 88, 3])
        self.wpa = din("wpa", [L, 128, 8, D])
        self.wpb = din("wpb", [L, 128, 8, D])
        self.wpc = din("wpc", [L, 128, 8, D])
        self.wo = din("wo", [L, 8, 128, 16, 256])
        self.wup = din("wup", [L, 44, 128, 16, 256])
        self.wdn = din("wdn", [L, 16, 128, 44, 128])
        self.out = nc.dram_tensor("out", [2048, D], F32, kind="ExternalOutput").ap()
        self.xT = nc.dram_tensor("xT", [D, TOK], F32).ap()
        self.proj = nc.dram_tensor("proj", [PROJ_ROWS, TOK], BF16).ap()
        self.vtok = nc.dram_tensor("vtok", [TOK, 1280], BF16).ap()
        self.mix = nc.dram_tensor("mix", [3072, TOK], BF16).ap()
        self.mT = nc.dram_tensor("mT", [D, TOK], BF16).ap()
        self.gT = nc.dram_tensor("gT", [5632, TOK], BF16).ap()
        self.xTv = self.xT.rearrange("(c p) t -> p c t", p=128)
        self.dbg_out = {}
        if dbg:
            for name, shape, dt in dbg:
                self.dbg_out[name] = nc.dram_tensor("dbg_" + name, shape, dt, kind="ExternalOutput").ap()

    def sb(self, st, shape, dt, name='t'):
        return st.enter_context(self.nc.sbuf_tensor("%s_%d" % (name, uid()), shape, dt))

    def mm(self, out, lhsT, rhs, start, stop, reads, writes):
        return self.P.op('pe', lambda e: e.matmul(out, lhsT=lhsT, rhs=rhs, start=start, stop=stop), reads, writes)

    def tr(self, out, in_, ident, reads, writes):
        return self.P.op('pe', lambda e: e.transpose(out, in_, ident), reads, writes)

    def act(self, out, in_, func, reads, writes, bias=None, scale=None, accum=None):
        kw = {}
        if bias is not None:
            kw['bias'] = bias
        if scale is not None:
            kw['scale'] = scale
        if accum is not None:
            kw['accum_out'] = accum
        return self.P.op('act', lambda e: e.activation(out=out, in_=in_, func=func, **kw), reads, writes)

    def tt(self, eng, out, in0, in1, op, reads, writes):
        return self.P.op(eng, lambda e: e.tensor_tensor(out=out, in0=in0, in1=in1, op=op), reads, writes)

    def ts(self, eng, out, in0, s1, s2, op0, op1, reads, writes):
        if s2 is None:
            return self.P.op(eng, lambda e: e.tensor_scalar(out=out, in0=in0, scalar1=s1, scalar2=None, op0=op0),
                             reads, writes)
        return self.P.op(eng, lambda e: e.tensor_scalar(out=out, in0=in0, scalar1=s1, scalar2=s2, op0=op0, op1=op1),
                         reads, writes)

    def stt(self, eng, out, in0, scalar, in1, op0, op1, reads, writes):
        return self.P.op(eng, lambda e: e.scalar_tensor_tensor(out=out, in0=in0, scalar=scalar, in1=in1,
                                                               op0=op0, op1=op1), reads, writes)

    def cp(self, eng, out, in_, reads, writes):
        if eng == 'act':
            return self.act(out, in_, AF.Copy, reads, writes)
        return self.P.op(eng, lambda e: e.tensor_copy(out=out, in_=in_), reads, writes)

    def memset(self, eng, ap, val, writes):
        return self.P.op(eng, lambda e: e.memset(ap, val), [], writes)

    def dma(self, eng, out, in_, reads=(), writes=()):
        return self.P.dma(eng, out, in_, reads, writes)

    def wstream(self, ring, n, src_fn, depth=2):
        slots = {}

        def issue(i):
            w_t, w_r = ring.nxt()
            self.dma('pool', w_t[:], src_fn(i), writes=[w_r])
            slots[i] = (w_t, w_r)
        for i in range(min(depth, n)):
            issue(i)
        for i in range(n):
            if i + depth < n:
                issue(i + depth)
            w_t, w_r = slots.pop(i)
            yield i, w_t, w_r

    def phase_transpose_in(self, mc0):
        P = self.P
        with ExitStack() as st:
            idf = self.sb(st, [128, 128], F32)
            r_id = Res()
            self.dma('sp', idf[:], self.ident, writes=[r_id])
            scf = self.sb(st, [128, 16, 2], F32)
            r_sc = Res()
            self.dma('sp', scf[:], self.csil, writes=[r_sc])
            self.act(scf[:], scf[:], AF.Silu, [r_sc], [r_sc])
            self.cp('dve', self.scb[:], scf[:], [r_sc], [r_sc])
            P.flush()
            units = self.mod_units(0, st, mc0)
            xt = Ring(st, self.nc, 'xt', [128, D], F32, 2)
            ps = Ring(st, self.nc, 'tp', [128, 4, 128], F32, 4, psum=True)
            ot = Ring(st, self.nc, 'ot', [128, 16, 128], F32, 2, nsub=4)
            for i in range(TOK // 128):
                x_t, x_r = xt.nxt()
                self.dma('sp', x_t[:], self.xin[i * 128:(i + 1) * 128, :], writes=[x_r])
                o_t, o_r = ot.nxt()
                for q in range(4):
                    p_t, p_r = ps.nxt()
                    for j in range(4):
                        c = q * 4 + j
                        self.tr(p_t[:, j, :], x_t[:, c * 128:(c + 1) * 128], idf[:], [x_r, r_id], [p_r])
                    self.cp('act' if q % 2 else 'dve', o_t[:, q * 4:(q + 1) * 4, :], p_t[:], [p_r], [o_r[q]])
                self.dma(ST_Q, self.xTv[:, :, i * 128:(i + 1) * 128], o_t[:], reads=o_r)
                next(units, None)
                next(units, None)
            for _ in units:
                pass
            P.flush()

    def mod_units(self, l, st, mc):
        bm = self.sb(st, [128, 96], F32)
        gg = self.sb(st, [128, 2, 16], F32)
        raw = self.sb(st, [128, 96, 2], F32)
        r_bm, r_gg = Res(), Res()
        self.dma('sp', bm[:], self.bmod[l], writes=[r_bm])
        self.dma('sp', gg[:, 0, :], self.g1[l], writes=[r_gg])
        self.dma('sp', gg[:, 1, :], self.g2[l], writes=[r_gg])
        wm = Ring(st, self.nc, 'wm', [128, 16, 256], BF16, 3)
        ps = Ring(st, self.nc, 'mp', [128, 2], F32, 2, psum=True)
        r_raw = [Res() for _ in range(96)]
        scb = self.scb
        for ch, w_t, w_r in self.wstream(wm, 48, lambda i: self.wmod[l, i]):
            for cc in range(2):
                col = ch * 2 + cc
                p_t, p_r = ps.nxt()
                for kc in range(16):
                    self.mm(p_t[:], w_t[:, kc, cc * 128:(cc + 1) * 128], scb[:, kc, :], kc == 0, kc == 15,
                            [w_r], [p_r])
                self.ts('dve', raw[:, col, :], p_t[:], bm[:, col:col + 1], None, ALU.add, None,
                        [p_r, r_bm], [r_raw[col]])
            yield
        rv = raw[:].rearrange("p (m c) s -> p m c s", m=6)
        r_mc = Res()
        for s_ in range(2):
            for half in range(2):
                m0 = half * 3
                self.stt('dve', mc[:, s_, m0 + 0, :], rv[:, m0 + 1, :, s_], 1.0, gg[:, half, :], ALU.add, ALU.mult,
                         r_raw + [r_gg], [r_mc])
                self.cp('dve', mc[:, s_, m0 + 1, :], rv[:, m0 + 0, :, s_], r_raw, [r_mc])
                self.cp('dve', mc[:, s_, m0 + 2, :], rv[:, m0 + 2, :, s_], r_raw, [r_mc])
        yield

    def norm_blocks(self, st, blocks, dst, Aof, Bof, ones, r_ones):
        xb = Ring(st, self.nc, 'xb', [128, 16, 128], F32, 2)
        sq = Ring(st, self.nc, 'sq', [128, 16, 128], BF16, 2)
        ss = Ring(st, self.nc, 'ss', [128, 128], F32, 2, psum=True)
        rs = Ring(st, self.nc, 'rs', [128, 128], F32, 2)
        tm = Ring(st, self.nc, 'tm', [128, 16, 128], F32, 2)
        r_dst = Res()
        for (tok0, s, col0) in blocks:
            x_t, x_r = xb.nxt()
            self.dma('sp', x_t[:], self.xTv[:, :, tok0:tok0 + 128], writes=[x_r])
            q_t, q_r = sq.nxt()
            self.act(q_t[:], x_t[:], AF.Square, [x_r], [q_r])
            s_t, s_r = ss.nxt()
            for c in range(16):
                self.mm(s_t[:], ones[:], q_t[:, c, :], c == 0, c == 15, [q_r, r_ones], [s_r])
            r_t, r_r = rs.nxt()
            self.ts('dve', r_t[:], s_t[:], 1.0 / D, EPS, ALU.mult, ALU.add, [s_r], [r_r])
            self.act(r_t[:], r_t[:], AF.Sqrt, [r_r], [r_r])
            self.P.op('dve', lambda e, r_t=r_t: e.reciprocal(out=r_t[:], in_=r_t[:]), [r_r], [r_r])
            t_t, t_r = tm.nxt()
            self.tt('dve', t_t[:], x_t[:], r_t[:].unsqueeze(1).to_broadcast([128, 16, 128]), ALU.mult,
                    [x_r, r_r], [t_r])
            for c in range(16):
                if c % 2 == 0:
                    self.act(dst[:, c, col0:col0 + 128], t_t[:, c, :], AF.Identity, [t_r], [],
                             bias=Bof(s, c), scale=Aof(s, c))
                else:
                    self.ts('dve', dst[:, c, col0:col0 + 128], t_t[:, c, :], Aof(s, c), Bof(s, c),
                            ALU.mult, ALU.add, [t_r], [])

    def phase_norm_inproj(self, l, mc):
        P = self.P
        nin = NXIN[l]
        T_in = nin * 128
        Th = T_in + 256
        with ExitStack() as st0:
            hT = self.sb(st0, [128, 16, Th], BF16, 'hT')
            with ExitStack() as st:
                ones = self.sb(st, [128, 128], BF16)
                r_ones = Res()
                self.memset('pool', ones[:], 1.0, [r_ones])
                blocks = [(i * 128, 0, i * 128) for i in range(nin)] + [(CTX0 + j * 128, 1, T_in + j * 128)
                                                                         for j in range(2)]
                self.norm_blocks(st, blocks, hT, lambda s, c: mc[:, s, 0, c:c + 1], lambda s, c: mc[:, s, 1, c:c + 1],
                                 ones, r_ones)
                P.flush()
            with ExitStack() as st:
                wr = Ring(st, self.nc, 'w', [128, 16, 256], BF16, 3)
                ps = Ring(st, self.nc, 'ps', [128, 512], F32, 6, psum=True)
                cs = Ring(st, self.nc, 'cs', [128, 2, 512], F32, 2)
                tmp = Ring(st, self.nc, 'tmp', [128, 2, 512], F32, 2)
                sg = Ring(st, self.nc, 'sg', [128, 512], BF16, 4)
                xb_full = [(b0, min(512, T_in - b0), b0) for b0 in range(0, T_in, 512)] + [(T_in, 256, CTX0)]
                Tq = NMIX[l] * 128
                xb_q = [(b0, min(512, Tq - b0), b0) for b0 in range(0, Tq, 512)] + ([(T_in, 256, CTX0)] if l < L - 1 else [])
                tiles = [(i * 128, i * 128) for i in range(nin)] + [(T_in + j * 128, CTX0 + j * 128) for j in range(2)]
                ev = 0
                for bi, w_t, w_r in self.wstream(wr, len(BLOCKS), lambda i: self.win[l, i]):
                    kind, dest, scale, qonly = BLOCKS[bi]
                    xblocks = xb_q if qonly else xb_full
                    if kind == 'rope':
                        for (c0, wd, t0) in xblocks:
                            pa, pa_r = ps.nxt()
                            pb, pb_r = ps.nxt()
                            for kc in range(16):
                                self.mm(pa[:, :wd], w_t[:, kc, 0:128], hT[:, kc, c0:c0 + wd], kc == 0, kc == 15,
                                        [w_r], [pa_r])
                            for kc in range(16):
                                self.mm(pb[:, :wd], w_t[:, kc, 128:256], hT[:, kc, c0:c0 + wd], kc == 0, kc == 15,
                                        [w_r], [pb_r])
                            c_t, c_r = cs.nxt()
                            self.dma('sp', c_t[:, 0, :wd], self.cosT[:, t0:t0 + wd], writes=[c_r])
                            self.dma('sp', c_t[:, 1, :wd], self.sinT[:, t0:t0 + wd], writes=[c_r])
                            m_t, m_r = tmp.nxt()
                            self.stt('dve', m_t[:, 0, :wd], pa[:, :wd], scale, c_t[:, 0, :wd], ALU.mult, ALU.mult,
                                     [pa_r, c_r], [m_r])
                            self.stt('dve', m_t[:, 1, :wd], pb[:, :wd], scale, c_t[:, 1, :wd], ALU.mult, ALU.mult,
                                     [pb_r, c_r], [m_r])
                            g_t, g_r = sg.nxt()
                            self.tt('dve', g_t[:, :wd], m_t[:, 0, :wd], m_t[:, 1, :wd], ALU.add, [m_r], [g_r])
                            self.dma(ST_Q, self.proj[dest:dest + 128, t0:t0 + wd], g_t[:, :wd], reads=[g_r])
                    elif kind == 'plain':
                        for j in range(2):
                            for (c0, wd, t0) in xblocks:
                                pa, pa_r = ps.nxt()
                                for kc in range(16):
                                    self.mm(pa[:, :wd], w_t[:, kc, j * 128:(j + 1) * 128], hT[:, kc, c0:c0 + wd],
                                            kc == 0, kc == 15, [w_r], [pa_r])
                                g_t, g_r = sg.nxt()
                                ev += 1
                                if ev % 2:
                                    self.act(g_t[:, :wd], pa[:, :wd], AF.Copy, [pa_r], [g_r], scale=scale)
                                else:
                                    self.ts('dve', g_t[:, :wd], pa[:, :wd], scale, None, ALU.mult, None, [pa_r], [g_r])
                                self.dma(ST_Q, self.proj[dest + j * 128:dest + (j + 1) * 128, t0:t0 + wd], g_t[:, :wd],
                                         reads=[g_r])
                    else:
                        for (c0, t0) in tiles:
                            pa, pa_r = ps.nxt()
                            for kc in range(16):
                                self.mm(pa[:, :256], hT[:, kc, c0:c0 + 128], w_t[:, kc, :], kc == 0, kc == 15,
                                        [w_r], [pa_r])
                            g_t, g_r = sg.nxt()
                            ev += 1
                            self.cp('act' if ev % 2 else 'dve', g_t[:, :256], pa[:, :256], [pa_r], [g_r])
                            self.dma(ST_Q, self.vtok[t0:t0 + 128, dest:dest + 256], g_t[:, :256], reads=[g_r])
                P.flush()

    def phase_attn(self, l, do_ctx):
        P = self.P
        nmix = NMIX[l]
        proj, vtok, mix = self.proj, self.vtok, self.mix
        with ExitStack() as st:
            nc = self.nc
            kcA = self.sb(st, [128, 2, 256], BF16)
            vcA = self.sb(st, [128, 2, 256], BF16)
            kcB = self.sb(st, [128, 8, 256], BF16)
            vcB = self.sb(st, [128, 2, 1024], BF16)
            snk = self.sb(st, [128, 8], F32)
            mA = self.sb(st, [128, 2, 640], F32)
            idf = self.sb(st, [128, 128], F32)
            idb = self.sb(st, [128, 128], BF16)
            r_c, r_sink, r_mA, r_idf, r_id = Res(), Res(), Res(), Res(), Res()
            self.dma('sp', kcA[:], proj[KA:KA + 256, CTX0:CTX0 + 256].rearrange("(g p) t -> p g t", p=128), writes=[r_c])
            self.dma('sp', kcB[:], proj[KB:KB + 1024, CTX0:CTX0 + 256].rearrange("(g p) t -> p g t", p=128), writes=[r_c])
            self.dma('sp', vcA[:], vtok[CTX0:CTX0 + 256, 0:256].rearrange("(c p) v -> p c v", p=128), writes=[r_c])
            self.dma('sp', vcB[:], vtok[CTX0:CTX0 + 256, 256:1280].rearrange("(c p) v -> p c v", p=128), writes=[r_c])
            self.dma('sp', snk[:], self.sinkr[:, l * 8:(l + 1) * 8], writes=[r_sink])
            self.dma('sp', mA[:], self.maskA, writes=[r_mA])
            self.dma('sp', idf[:], self.ident, writes=[r_idf])
            self.cp('dve', idb[:], idf[:], [r_idf], [r_id])
            mAb = self.sb(st, [128, 2, 384], BF16)
            self.cp('dve', mAb[:], mA[:, :, 0:384], [r_mA], [r_mA])
            ND = 3
            R = dict(
                qa=Ring(st, nc, 'qa', [128, 8, 128], BF16, ND), ka=Ring(st, nc, 'ka', [128, 2, 384], BF16, ND),
                va=Ring(st, nc, 'va', [128, 3, 256], BF16, ND), qb=Ring(st, nc, 'qb', [128, 8, 128], BF16, ND),
                kb=Ring(st, nc, 'kb', [128, 8, 640], BF16, 2), vb=Ring(st, nc, 'vb', [128, 5, 1024], BF16, 2),
                bB=Ring(st, nc, 'bB', [128, 8, 896], F32, 2),
                S=Ring(st, nc, 'S', [128, 1024], F32, 2, psum=True),
                PTp=Ring(st, nc, 'PTp', [128, 8, 128], BF16, 2, psum=True),
                O=Ring(st, nc, 'O', [128, 512], F32, 2, psum=True),
                Sb=Ring(st, nc, 'Sb', [128, 896], F32, 4), Pm=Ring(st, nc, 'Pm', [128, 896], BF16, 4),
                Pn=Ring(st, nc, 'Pn', [128, 896], BF16, 4), st=Ring(st, nc, 'st', [128, 8], F32, 8),
                PTA=Ring(st, nc, 'PTA', [128, 5, 4, 128], BF16, 3), PTB=Ring(st, nc, 'PTB', [128, 7, 128], BF16, 4),
                oa=Ring(st, nc, 'oa', [128, 8, 128], BF16, 2), ob=Ring(st, nc, 'ob', [128, 8, 128], BF16, 2),
            )
            qtiles = [('x', i) for i in range(nmix)] + ([('c', j) for j in range(2)] if do_ctx else [])
            evc = [0]

            def ev_eng():
                evc[0] += 1
                return 'act' if evc[0] % 2 else 'dve'

            def load_tile(ti):
                kind, i = qtiles[ti]
                isx = kind == 'x'
                tcol = i * 128 if isx else CTX0 + i * 128
                T = dict(isx=isx, i=i, tcol=tcol)
                T['qa'] = R['qa'].nxt()
                T['qb'] = R['qb'].nxt()
                self.dma('sp', T['qa'][0][:], proj[QA:QA + 1024, tcol:tcol + 128].rearrange("(h p) t -> p h t", p=128),
                         writes=[T['qa'][1]])
                self.dma('sp', T['qb'][0][:], proj[QB:QB + 1024, tcol:tcol + 128].rearrange("(h p) t -> p h t", p=128),
                         writes=[T['qb'][1]])
                if isx:
                    sa = max(i - 1, 0)
                    sbt = max(i - 2, 0)
                    var = 0 if i == 0 else (1 if i == 1 else 2)
                    for nm in ('ka', 'va', 'kb', 'vb', 'bB'):
                        T[nm] = R[nm].nxt()
                    self.dma('sp', T['ka'][0][:], proj[KA:KA + 256, sa * 128:(sa + 3) * 128].rearrange("(g p) t -> p g t", p=128),
                             writes=[T['ka'][1]])
                    self.dma('sp', T['va'][0][:], vtok[sa * 128:(sa + 3) * 128, 0:256].rearrange("(c p) v -> p c v", p=128),
                             writes=[T['va'][1]])
                    self.dma('sp', T['kb'][0][:], proj[KB:KB + 1024, sbt * 128:(sbt + 5) * 128].rearrange("(g p) t -> p g t", p=128),
                             writes=[T['kb'][1]])
                    self.dma('sp', T['vb'][0][:], vtok[sbt * 128:(sbt + 5) * 128, 256:1280].rearrange("(c p) v -> p c v", p=128),
                             writes=[T['vb'][1]])
                    self.dma('sp', T['bB'][0][:], self.biasB[l, var], writes=[T['bB'][1]])
                T['oa'] = R['oa'].nxt()
                T['ob'] = R['ob'].nxt()
                return T

            tiles = {}
            jobs = []
            for ti in range(len(qtiles)):
                for mixer in ('A', 'B'):
                    for h in range(8):
                        jobs.append(dict(ti=ti, mixer=mixer, h=h))
            grpA = {}

            def stageA(J):
                ti, h, mixer = J['ti'], J['h'], J['mixer']
                if mixer == 'A' and h == 0 and ti == 0:
                    tiles[0] = load_tile(0)
                if mixer == 'A' and h == 5 and ti + 1 < len(qtiles):
                    tiles[ti + 1] = load_tile(ti + 1)
                T = tiles[ti]
                isx = T['isx']
                S_t, S_r = R['S'].nxt()
                stt, st_r = R['st'].nxt()
                J['st'] = (stt, st_r)
                if mixer == 'A':
                    g = h // 4
                    q, q_r = T['qa'][0][:, h, :], T['qa'][1]
                    W = 640 if isx else 256
                    if isx:
                        ka_t, ka_r = T['ka']
                        self.mm(S_t[:, 0:384], q, ka_t[:, g, :], True, False, [q_r, ka_r], [S_r])
                        self.mm(S_t[:, 0:384], idb[:], mAb[:, 0 if T['i'] == 0 else 1, :], False, True, [r_id, r_mA], [S_r])
                        self.mm(S_t[:, 384:512], q, kcA[:, g, 0:128], True, True, [q_r, r_c], [S_r])
                        self.mm(S_t[:, 512:640], q, kcA[:, g, 128:256], True, True, [q_r, r_c], [S_r])
                        table, t_r = None, None
                    else:
                        self.mm(S_t[:, 0:256], q, kcA[:, g, :], True, True, [q_r, r_c], [S_r])
                        table, t_r = None, None
                    sinkcol = snk[:, h:h + 1]
                else:
                    q, q_r = T['qb'][0][:, h, :], T['qb'][1]
                    W = 896 if isx else 256
                    if isx:
                        kb_t, kb_r = T['kb']
                        self.mm(S_t[:, 0:512], q, kb_t[:, h, 0:512], True, True, [q_r, kb_r], [S_r])
                        self.mm(S_t[:, 512:640], q, kb_t[:, h, 512:640], True, True, [q_r, kb_r], [S_r])
                        self.mm(S_t[:, 640:896], q, kcB[:, h, :], True, True, [q_r, r_c], [S_r])
                        table, t_r = T['bB'][0][:, h, :], T['bB'][1]
                    else:
                        self.mm(S_t[:, 0:256], q, kcB[:, h, :], True, True, [q_r, r_c], [S_r])
                        table, t_r = None, None
                    sinkcol = None
                J['W'] = W
                J['sink'] = sinkcol
                S_view = S_t[:, 0:W]
                if table is not None:
                    sb_t, sb_r = R['Sb'].nxt()
                    self.tt('dve', sb_t[:, :W], S_view, table, ALU.add, [S_r, t_r], [sb_r])
                    src, src_r = sb_t[:, :W], sb_r
                else:
                    src, src_r = S_view, S_r
                self.P.op('dve', lambda e: e.reduce_max(out=stt[:, 0:1], in_=src, axis=AX.X), [src_r], [st_r])
                if sinkcol is not None:
                    self.ts('dve', stt[:, 1:2], stt[:, 0:1], sinkcol, -1.0, ALU.max, ALU.mult, [st_r, r_sink], [st_r])
                else:
                    self.ts('dve', stt[:, 1:2], stt[:, 0:1], -1.0, None, ALU.mult, None, [st_r], [st_r])
                self.memset('dve', stt[:, 2:3], 0.0, [st_r])
                J['Sb'] = (src, src_r)

            def stageB1(J):
                stt, st_r = J['st']
                src, sb_r = J['Sb']
                W = J['W']
                pm_t, pm_r = R['Pm'].nxt()
                J['Pm'] = (pm_t, pm_r)
                self.act(pm_t[:, :W], src, AF.Exp, [sb_r, st_r], [pm_r, st_r], bias=stt[:, 1:2], scale=1.0,
                         accum=stt[:, 2:3])
                if J['sink'] is not None:
                    self.act(stt[:, 3:4], J['sink'], AF.Exp, [st_r, r_sink], [st_r], bias=stt[:, 1:2], scale=1.0)

            def stageB2(J):
                stt, st_r = J['st']
                pm_t, pm_r = J['Pm']
                W = J['W']
                if J['sink'] is not None:
                    self.tt('dve', stt[:, 2:3], stt[:, 2:3], stt[:, 3:4], ALU.add, [st_r], [st_r])
                self.P.op('dve', lambda e: e.reciprocal(out=stt[:, 5:6], in_=stt[:, 2:3]), [st_r], [st_r])
                pn_t, pn_r = R['Pn'].nxt()
                J['Pn'] = (pn_t, pn_r)
                self.act(pn_t[:, :W], pm_t[:, :W], AF.Copy, [pm_r, st_r], [pn_r], scale=stt[:, 5:6])

            def stageC(J):
                pn_t, pn_r = J['Pn']
                nch = J['W'] // 128
                pt_t, pt_r = R['PTp'].nxt()
                for c in range(nch):
                    self.tr(pt_t[:, c, :], pn_t[:, c * 128:(c + 1) * 128], idb[:], [pn_r, r_id], [pt_r])
                if J['mixer'] == 'A':
                    hh = J['h'] % 4
                    key = (J['ti'], J['h'] // 4)
                    if hh == 0:
                        grpA[key] = R['PTA'].nxt()
                    pta_t, pta_r = grpA[key]
                    self.cp(ev_eng(), pta_t[:, 0:nch, hh, :], pt_t[:, 0:nch, :], [pt_r], [pta_r])
                else:
                    ptb_t, ptb_r = R['PTB'].nxt()
                    J['PTB'] = (ptb_t, ptb_r)
                    self.cp(ev_eng(), ptb_t[:, 0:nch, :], pt_t[:, 0:nch, :], [pt_r], [ptb_r])

            curO = {}

            def stageD(J):
                T = tiles[J['ti']]
                isx, tcol = T['isx'], T['tcol']
                h = J['h']
                nch = J['W'] // 128
                if J['mixer'] == 'A':
                    if h % 4 != 3:
                        return
                    g = h // 4
                    pta_t, pta_r = grpA.pop((J['ti'], g))
                    O_t, O_r = R['O'].nxt()
                    for c in range(nch):
                        if isx and c < 3:
                            v, v_r = T['va'][0][:, c, g * 128:(g + 1) * 128], T['va'][1]
                        else:
                            cc = c - 3 if isx else c
                            v, v_r = vcA[:, cc, g * 128:(g + 1) * 128], r_c
                        self.mm(O_t[:], v, pta_t[:, c, :, :].rearrange("p h q -> p (h q)"), c == 0, c == nch - 1,
                                [v_r, pta_r], [O_r])
                    oa_t, oa_r = T['oa']
                    self.cp(ev_eng(), oa_t[:, g * 4:(g + 1) * 4, :], O_t[:].rearrange("p (h q) -> p h q", h=4),
                            [O_r], [oa_r])
                    if g == 1:
                        self.dma(ST_Q, mix[0:1024, tcol:tcol + 128].rearrange("(h p) t -> p h t", p=128), oa_t[:],
                                 reads=[oa_r])
                else:
                    ptb_t, ptb_r = J['PTB']
                    hq = h % 4
                    if hq == 0:
                        curO['B'] = R['O'].nxt()
                    O_t, O_r = curO['B']
                    for c in range(nch):
                        if isx and c < 5:
                            v, v_r = T['vb'][0][:, c, h * 128:(h + 1) * 128], T['vb'][1]
                        else:
                            cc = c - 5 if isx else c
                            v, v_r = vcB[:, cc, h * 128:(h + 1) * 128], r_c
                        self.mm(O_t[:, hq * 128:(hq + 1) * 128], v, ptb_t[:, c, :], c == 0, c == nch - 1,
                                [v_r, ptb_r], [O_r])
                    ob_t, ob_r = T['ob']
                    if hq == 3:
                        self.cp(ev_eng(), ob_t[:, h - 3:h + 1, :], O_t[:].rearrange("p (h q) -> p h q", h=4),
                                [O_r], [ob_r])
                    if h == 7:
                        self.dma(ST_Q, mix[1024:2048, tcol:tcol + 128].rearrange("(h p) t -> p h t", p=128), ob_t[:],
                                 reads=[ob_r])

            stages = [stageA, stageB1, stageB2, stageC, stageD]
            nj = len(jobs)
            for t in range(nj + len(stages) - 1):
                for k, fn in enumerate(stages):
                    j = t - k
                    if 0 <= j < nj:
                        fn(jobs[j])
            P.flush()

    def phase_conv(self, l, do_ctx):
        P = self.P
        nmix = NMIX[l]
        Tc = nmix * 128
        proj, mix = self.proj, self.mix
        with ExitStack() as st:
            nc = self.nc
            cw = self.sb(st, [128, 8, 3], F32)
            r_cw = Res()
            self.dma('sp', cw[:], self.convc[l], writes=[r_cw])
            W = Tc + 2
            inb = Ring(st, nc, 'cin', [128, 3, W], BF16, 2)
            pp = Ring(st, nc, 'cp', [128, W], F32, 2)
            oo = Ring(st, nc, 'co', [128, W], F32, 2)
            cb = Ring(st, nc, 'cb', [128, W], BF16, 2)
            segs = [(0, Tc, True)] + ([(CTX0, 256, False)] if do_ctx else [])
            k = 0
            for cc in range(8):
                for (t0, n, has_next) in segs:
                    eng = 'dve'
                    i_t, i_r = inb.nxt()
                    nl = n + 1 if has_next else n
                    self.memset(eng, i_t[:, :, 0:1], 0.0, [i_r])
                    if not has_next:
                        self.memset(eng, i_t[:, :, n + 1:n + 2], 0.0, [i_r])
                    for z, base in enumerate((UC, GPRE, GPOST)):
                        self.dma('sp', i_t[:, z, 1:1 + nl], proj[base + cc * 128:base + (cc + 1) * 128, t0:t0 + nl],
                                 writes=[i_r])
                    p_t, p_r = pp.nxt()
                    self.tt('dve', p_t[:, 0:n + 2], i_t[:, 0, 0:n + 2], i_t[:, 1, 0:n + 2], ALU.mult, [i_r], [p_r])
                    o_t, o_r = oo.nxt()
                    self.act(o_t[:, 0:n], p_t[:, 0:n], AF.Copy, [p_r, r_cw], [o_r], scale=cw[:, cc, 0:1])
                    self.stt('dve', o_t[:, 0:n], p_t[:, 1:n + 1], cw[:, cc, 1:2], o_t[:, 0:n], ALU.mult, ALU.add,
                             [p_r, r_cw, o_r], [o_r])
                    self.stt('dve', o_t[:, 0:n], p_t[:, 2:n + 2], cw[:, cc, 2:3], o_t[:, 0:n], ALU.mult, ALU.add,
                             [p_r, r_cw, o_r], [o_r])
                    c_t, c_r = cb.nxt()
                    self.tt('dve', c_t[:, 0:n], o_t[:, 0:n], i_t[:, 2, 1:n + 1], ALU.mult, [o_r, i_r], [c_r])
                    self.dma(ST_Q, mix[2048 + cc * 128:2048 + (cc + 1) * 128, t0:t0 + n], c_t[:, 0:n], reads=[c_r])
            P.flush()

    def phase_merge(self, l, do_ctx):
        P = self.P
        nmix = NMIX[l]
        Tm = nmix * 128
        proj, mix, mT = self.proj, self.mix, self.mT
        with ExitStack() as st:
            nc = self.nc
            wp = [self.sb(st, [128, 8, D], BF16, 'wp') for _ in range(3)]
            r_wp = [Res() for _ in range(3)]
            for z, src in enumerate((self.wpa, self.wpb, self.wpc)):
                for hf in range(2):
                    self.dma('pool', wp[z][:, hf * 4:(hf + 1) * 4, :], src[l, :, hf * 4:(hf + 1) * 4, :], writes=[r_wp[z]])
            mb = Ring(st, nc, 'mb', [128, 24, 512], BF16, 2)
            zb = Ring(st, nc, 'zb', [128, 3, 512], BF16, 2)
            sg = Ring(st, nc, 'sg', [128, 3, 512], F32, 2)
            ps = Ring(st, nc, 'ps', [128, 512], F32, 6, psum=True)
            t1 = Ring(st, nc, 't1', [128, 512], F32, 2)
            t2 = Ring(st, nc, 't2', [128, 512], F32, 2)
            t3 = Ring(st, nc, 't3', [128, 512], F32, 2)
            mo = Ring(st, nc, 'mo', [128, 512], BF16, 2)
            tblocks = [(b0, min(512, Tm - b0)) for b0 in range(0, Tm, 512)] + ([(CTX0, 256)] if do_ctx else [])
            mixv = mix.rearrange("(k p) t -> p k t", p=128)
            zv = proj[ZA:ZA + 6144, :].rearrange("(z c p) t -> p z c t", z=3, c=16, p=128)
            for (t0, wd) in tblocks:
                m_t, m_r = mb.nxt()
                for z in range(3):
                    self.dma('sp', m_t[:, z * 8:(z + 1) * 8, :wd], mixv[:, z * 8:(z + 1) * 8, t0:t0 + wd], writes=[m_r])
                for n in range(16):
                    z_t, z_r = zb.nxt()
                    self.dma('sp', z_t[:, :, :wd], zv[:, :, n, t0:t0 + wd], writes=[z_r])
                    s_t, s_r = sg.nxt()
                    self.act(s_t[:, :, :wd], z_t[:, :, :wd], AF.Sigmoid, [z_r], [s_r])
                    pz = []
                    for z in range(3):
                        p_t, p_r = ps.nxt()
                        for kc in range(8):
                            self.mm(p_t[:, :wd], wp[z][:, kc, n * 128:(n + 1) * 128], m_t[:, z * 8 + kc, :wd],
                                    kc == 0, kc == 7, [r_wp[z], m_r], [p_r])
                        pz.append((p_t, p_r))
                    a_t, a_r = t1.nxt()
                    b_t, b_r = t2.nxt()
                    c_t, c_r = t3.nxt()
                    self.tt('dve', a_t[:, :wd], pz[0][0][:, :wd], s_t[:, 0, :wd], ALU.mult, [pz[0][1], s_r], [a_r])
                    self.tt('dve', b_t[:, :wd], pz[1][0][:, :wd], s_t[:, 1, :wd], ALU.mult, [pz[1][1], s_r], [b_r])
                    self.tt('dve', c_t[:, :wd], pz[2][0][:, :wd], s_t[:, 2, :wd], ALU.mult, [pz[2][1], s_r], [c_r])
                    self.tt('dve', a_t[:, :wd], a_t[:, :wd], b_t[:, :wd], ALU.add, [a_r, b_r], [a_r])
                    o_t, o_r = mo.nxt()
                    self.tt('dve', o_t[:, :wd], a_t[:, :wd], c_t[:, :wd], ALU.add, [a_r, c_r], [o_r])
                    self.dma(ST_Q, mT[n * 128:(n + 1) * 128, t0:t0 + wd], o_t[:, :wd], reads=[o_r])
            P.flush()

    def resid_update(self, rings, p_t, p_r, wd, n, t0, gate, r_mc):
        xo = rings.nxt()
        x_t, x_r = xo
        src = self.xT[n * 128:(n + 1) * 128, t0:t0 + wd]
        self.dma('sp', x_t[:, :wd], src, writes=[x_r])
        self.stt('dve', x_t[:, :wd], p_t[:, :wd], gate, x_t[:, :wd], ALU.mult, ALU.add, [p_r, x_r, r_mc], [x_r])
        self.dma(ST_Q, src, x_t[:, :wd], reads=[x_r])

    def phase_wo(self, l, mc, do_ctx):
        P = self.P
        nmix = NMIX[l]
        Tm = nmix * 128
        Tt = Tm + (256 if do_ctx else 0)
        with ExitStack() as st:
            nc = self.nc
            ms = self.sb(st, [128, 16, Tt], BF16, 'ms')
            r_ms = Res()
            mTv = self.mT.rearrange("(k p) t -> p k t", p=128)
            for k4 in range(4):
                self.dma('sp', ms[:, k4 * 4:(k4 + 1) * 4, 0:Tm], mTv[:, k4 * 4:(k4 + 1) * 4, 0:Tm], writes=[r_ms])
            if do_ctx:
                self.dma('sp', ms[:, :, Tm:Tm + 256], mTv[:, :, CTX0:CTX0 + 256], writes=[r_ms])
            wr = Ring(st, nc, 'w', [128, 16, 256], BF16, 3)
            ps = Ring(st, nc, 'ps', [128, 512], F32, 4, psum=True)
            xo = Ring(st, nc, 'xo', [128, 512], F32, 4)
            r_mc = Res()
            tblocks = [(b0, min(512, Tm - b0), b0, 0) for b0 in range(0, Tm, 512)] + ([(Tm, 256, CTX0, 1)] if do_ctx else [])
            for wb, w_t, w_r in self.wstream(wr, 8, lambda i: self.wo[l, i]):
                for j in range(2):
                    n = wb * 2 + j
                    for (c0, wd, t0, s) in tblocks:
                        p_t, p_r = ps.nxt()
                        for kc in range(16):
                            self.mm(p_t[:, :wd], w_t[:, kc, j * 128:(j + 1) * 128], ms[:, kc, c0:c0 + wd],
                                    kc == 0, kc == 15, [w_r, r_ms], [p_r])
                        self.resid_update(xo, p_t, p_r, wd, n, t0, mc[:, s, 2, n:n + 1], r_mc)
            P.flush()

    def phase_ffn(self, l, mc, do_ctx, mc_next=None):
        P = self.P
        nmix = NMIX[l]
        nout = NOUT[l]
        Tx = nmix * 128
        Hc = Tx + 259
        cbase = Tx + 1
        with ExitStack() as st0:
            nc = self.nc
            hT = self.sb(st0, [128, 16, Hc], BF16, 'h2T')
            with ExitStack() as st:
                ones = self.sb(st, [128, 128], BF16)
                r_ones = Res()
                self.memset('pool', ones[:], 1.0, [r_ones])
                for col in (0, Tx + 1, Tx + 258):
                    self.memset('pool', hT[:, :, col:col + 1], 0.0, [])
                blocks = [(i * 128, 0, 1 + i * 128) for i in range(nmix)]
                if do_ctx:
                    blocks += [(CTX0 + j * 128, 1, Tx + 2 + j * 128) for j in range(2)]
                self.norm_blocks(st, blocks, hT, lambda s, c: mc[:, s, 3, c:c + 1], lambda s, c: mc[:, s, 4, c:c + 1],
                                 ones, r_ones)
                P.flush()
            with ExitStack() as st:
                cf = self.sb(st, [128, 88, 3], F32)
                r_cf = Res()
                self.dma('sp', cf[:], self.convf[l], writes=[r_cf])
                wr = Ring(st, nc, 'w', [128, 16, 256], BF16, 3)
                ps = Ring(st, nc, 'ps', [128, 512], F32, 6, psum=True)
                cg = Ring(st, nc, 'cg', [128, 2, 512], F32, 3)
                sl = Ring(st, nc, 'sl', [128, 512], F32, 2)
                go = Ring(st, nc, 'go', [128, 512], BF16, 3)
                To = nout * 128
                oblocks = [(o0, min(510, To - o0), o0) for o0 in range(0, To, 510)]
                if do_ctx:
                    oblocks += [(cbase, 256, CTX0)]
                units = self.mod_units(l + 1, st, mc_next) if mc_next is not None else iter(())
                for j, w_t, w_r in self.wstream(wr, 44, lambda i: self.wup[l, i]):
                    next(units, None)
                    for (c0, n, t0) in oblocks:
                        pg, pg_r = ps.nxt()
                        pv, pv_r = ps.nxt()
                        for kc in range(16):
                            self.mm(pg[:, :n + 2], w_t[:, kc, 0:128], hT[:, kc, c0:c0 + n + 2], kc == 0, kc == 15,
                                    [w_r], [pg_r])
                        for kc in range(16):
                            self.mm(pv[:, :n + 2], w_t[:, kc, 128:256], hT[:, kc, c0:c0 + n + 2], kc == 0, kc == 15,
                                    [w_r], [pv_r])
                        c_t, c_r = cg.nxt()
                        for z, (pz, pz_r, fcol) in enumerate(((pg, pg_r, j), (pv, pv_r, 44 + j))):
                            self.act(c_t[:, z, :n], pz[:, 1:n + 1], AF.Copy, [pz_r, r_cf], [c_r], scale=cf[:, fcol, 1:2])
                            self.stt('dve', c_t[:, z, :n], pz[:, 0:n], cf[:, fcol, 0:1], c_t[:, z, :n], ALU.mult, ALU.add,
                                     [pz_r, r_cf, c_r], [c_r])
                            self.stt('dve', c_t[:, z, :n], pz[:, 2:n + 2], cf[:, fcol, 2:3], c_t[:, z, :n], ALU.mult, ALU.add,
                                     [pz_r, r_cf, c_r], [c_r])
                        s_t, s_r = sl.nxt()
                        self.act(s_t[:, :n], c_t[:, 0, :n], AF.Silu, [c_r], [s_r])
                        g_t, g_r = go.nxt()
                        self.tt('dve', g_t[:, :n], s_t[:, :n], c_t[:, 1, :n], ALU.mult, [s_r, c_r], [g_r])
                        self.dma(ST_Q, self.gT[j * 128:(j + 1) * 128, t0:t0 + n], g_t[:, :n], reads=[g_r])
                for _ in units:
                    pass
                P.flush()

    def phase_down(self, l, mc, do_ctx):
        P = self.P
        To = NOUT[l] * 128
        with ExitStack() as st:
            nc = self.nc
            gs = self.sb(st, [128, 44, 1024], BF16, 'gs')
            r_gs = Res()
            wr = Ring(st, nc, 'w', [128, 44, 128], BF16, 3)
            ps = Ring(st, nc, 'ps', [128, 512], F32, 4, psum=True)
            xo = Ring(st, nc, 'xo', [128, 512], F32, 4)
            r_mc = Res()
            gTv = self.gT.rearrange("(k p) t -> p k t", p=128)
            sblocks = [(b0, min(1024, To - b0), 0) for b0 in range(0, To, 1024)] + ([(CTX0, 256, 1)] if do_ctx else [])
            ws = self.wstream(wr, 16 * len(sblocks), lambda i: self.wdn[l, i % 16])
            for (s0, sw, s) in sblocks:
                for k4 in range(4):
                    self.dma('sp', gs[:, k4 * 11:(k4 + 1) * 11, 0:sw], gTv[:, k4 * 11:(k4 + 1) * 11, s0:s0 + sw],
                             writes=[r_gs])
                for n in range(16):
                    _, w_t, w_r = next(ws)
                    for b0 in range(0, sw, 512):
                        wd = min(512, sw - b0)
                        p_t, p_r = ps.nxt()
                        for kc in range(44):
                            self.mm(p_t[:, :wd], w_t[:, kc, :], gs[:, kc, b0:b0 + wd], kc == 0, kc == 43,
                                    [w_r, r_gs], [p_r])
                        self.resid_update(xo, p_t, p_r, wd, n, s0 + b0, mc[:, s, 5, n:n + 1], r_mc)
            P.flush()

    def phase_final(self):
        P = self.P
        with ExitStack() as st:
            nc = self.nc
            gfs = self.sb(st, [128, 16], F32)
            idf = self.sb(st, [128, 128], F32)
            ones = self.sb(st, [128, 128], BF16)
            r_g, r_id, r_ones = Res(), Res(), Res()
            self.dma('sp', gfs[:], self.gf, writes=[r_g])
            self.dma('sp', idf[:], self.ident, writes=[r_id])
            self.memset('pool', ones[:], 1.0, [r_ones])
            xb = Ring(st, nc, 'xb', [128, 16, 128], F32, 2)
            sq = Ring(st, nc, 'sq', [128, 16, 128], BF16, 2)
            ss = Ring(st, nc, 'ss', [128, 128], F32, 2, psum=True)
            rs = Ring(st, nc, 'rs', [128, 128], F32, 2)
            tm = Ring(st, nc, 'tm', [128, 16, 128], F32, 2)
            ps = Ring(st, nc, 'tp', [128, 4, 128], F32, 4, psum=True)
            ot = Ring(st, nc, 'ot', [128, D], F32, 2, nsub=4)
            for i in range(16):
                x_t, x_r = xb.nxt()
                self.dma('sp', x_t[:], self.xTv[:, :, i * 128:(i + 1) * 128], writes=[x_r])
                q_t, q_r = sq.nxt()
                self.act(q_t[:], x_t[:], AF.Square, [x_r], [q_r])
                s_t, s_r = ss.nxt()
                for c in range(16):
                    self.mm(s_t[:], ones[:], q_t[:, c, :], c == 0, c == 15, [q_r, r_ones], [s_r])
                r_t, r_r = rs.nxt()
                self.ts('dve', r_t[:], s_t[:], 1.0 / D, EPS, ALU.mult, ALU.add, [s_r], [r_r])
                self.act(r_t[:], r_t[:], AF.Sqrt, [r_r], [r_r])
                self.P.op('dve', lambda e, r_t=r_t: e.reciprocal(out=r_t[:], in_=r_t[:]), [r_r], [r_r])
                t_t, t_r = tm.nxt()
                self.tt('dve', t_t[:], x_t[:], r_t[:].unsqueeze(1).to_broadcast([128, 16, 128]), ALU.mult,
                        [x_r, r_r], [t_r])
                self.tt('dve', t_t[:], t_t[:], gfs[:].unsqueeze(2).to_broadcast([128, 16, 128]), ALU.mult,
                        [t_r, r_g], [t_r])
                o_t, o_r = ot.nxt()
                for q in range(4):
                    p_t, p_r = ps.nxt()
                    for j in range(4):
                        c = q * 4 + j
                        self.tr(p_t[:, j, :], t_t[:, c, :], idf[:], [t_r, r_id], [p_r])
                    self.cp('act' if q % 2 else 'dve', o_t[:, q * 512:(q + 1) * 512],
                            p_t[:].rearrange("p a b -> p (a b)"), [p_r], [o_r[q]])
                self.dma(ST_Q, self.out[i * 128:(i + 1) * 128, :], o_t[:], reads=o_r)
            P.flush()

    def dump(self, name, src):
        self.dma('sp', self.dbg_out[name], src)
        self.P.flush()

    def build(self, stop_after=None):
        nc = self.nc
        with ExitStack() as st:
            self.P = Prog(nc, st)
            self.scb = self.sb(st, [128, 16, 2], BF16, 'scb')
            mcs = [self.sb(st, [128, 2, 6, 16], F32, 'mc') for _ in range(2)]
            self.phase_transpose_in(mcs[0])
            for l in range(self.NL):
                do_ctx = l < L - 1
                mc = mcs[l % 2]
                mc_next = mcs[(l + 1) % 2] if l + 1 < self.NL else None
                self.phase_norm_inproj(l, mc)
                self.phase_attn(l, do_ctx)
                self.phase_conv(l, do_ctx)
                self.phase_merge(l, do_ctx)
                self.phase_wo(l, mc, do_ctx)
                self.phase_ffn(l, mc, do_ctx, mc_next)
                self.phase_down(l, mc, do_ctx)
            if self.dbg:
                for name, shape, dt in self.dbg:
                    self.dump(name, getattr(self, name))
            self.phase_final()
        return nc


_ROT_SRC = np.concatenate([np.arange(32, 64), np.arange(0, 32), np.arange(96, 128), np.arange(64, 96)])
_ROT_SIGN = np.concatenate([-np.ones(32), np.ones(32), -np.ones(32), np.ones(32)]).astype(np.float32)


def _fm(v, nch):
    return np.ascontiguousarray(np.swapaxes(v.reshape(v.shape[:-1] + (nch, 128)), -1, -2))


def _wblocks(w, order_cols, bw):
    Kd = w.shape[0]
    wg = w[:, order_cols]
    nb = wg.shape[1] // bw
    return np.ascontiguousarray(wg.reshape(Kd // 128, 128, nb, bw).transpose(2, 1, 0, 3))


def _win_cols():
    cols = []
    o = 0
    for h in range(8):
        base = h * 128
        cols += [base + np.arange(128), base + _ROT_SRC]
    o = 1024
    for g in range(2):
        base = o + g * 128
        cols += [base + np.arange(128), base + _ROT_SRC]
    cols += [np.arange(1536, 2560), np.arange(2560, 3584), np.arange(4608, 7680), np.arange(7680, 13824),
             np.arange(1280, 1536), np.arange(3584, 4608)]
    return np.concatenate(cols)


def _gpos(half, t):
    return t if half == 0 else 4095 - t


def _rope_tables(half):
    t = np.arange(3584)
    g = _gpos(half, t)
    row = (g // 64).astype(np.float32)
    col = (g % 64).astype(np.float32)
    inv = (1.0 / (np.float32(10000.0) ** (np.arange(32, dtype=np.float32) / np.float32(32)))).astype(np.float32)
    ar = row[:, None] * inv
    ac = col[:, None] * inv
    ang = np.concatenate([ar, ar, ac, ac], axis=-1)
    cos = np.ones((TOK, 128), np.float32)
    sin = np.zeros((TOK, 128), np.float32)
    cos[:3584] = np.cos(ang)
    sin[:3584] = np.sin(ang) * _ROT_SIGN[None, :]
    return np.ascontiguousarray(cos.T), np.ascontiguousarray(sin.T)


def _maskA():
    m = np.zeros((128, 2, 640), np.float32)
    a = np.arange(128)[:, None]
    kk = np.arange(384)[None, :]
    m[:, 0, :384] = np.where(np.abs(kk - a) <= 128, 0.0, NEGM)
    m[:, 1, :384] = np.where(np.abs(kk - 128 - a) <= 128, 0.0, NEGM)
    return m


def _biasB(rpb, half):
    out = np.zeros((L, 3, 128, 8, 896), np.float32)
    for var, (i, s) in enumerate(((0, 0), (1, 0), (5, 3))):
        tq = i * 128 + np.arange(128)
        tk = s * 128 + np.arange(640)
        gq = _gpos(half, tq)
        gk = _gpos(half, tk)
        rq, cq = gq // 64, gq % 64
        rk, ck = gk // 64, gk % 64
        rs = np.clip(rq - 4, 0, 56)
        cst = np.clip(cq - 8, 0, 48)
        ok = ((rk[None, :] >= rs[:, None]) & (rk[None, :] < rs[:, None] + 8)
              & (ck[None, :] >= cst[:, None]) & (ck[None, :] < cst[:, None] + 16))
        ri = np.clip(rk[None, :] - rq[:, None] + 7, 0, 14)
        ci = np.clip(ck[None, :] - cq[:, None] + 15, 0, 30)
        for l in range(L):
            for h in range(8):
                out[l, var, :, h, :640] = np.where(ok, rpb[l, h][ri, ci], NEGM)
    return out


_CACHE = {}


def _get_nc():
    if 'nc' not in _CACHE:
        _CACHE['nc'] = K().build()
    return _CACHE['nc']


def prep_inputs(x, c, ctx, c_ctx, w_mod, b_mod, norm1, w_in, sink, rpb, conv_c, w_pa, w_pb, w_pc, w_o, norm2, w_up,
                conv_f, w_down, norm_f):
    f = lambda a: np.asarray(a, dtype=np.float32)
    x, c, ctx, c_ctx = f(x), f(c), f(ctx), f(c_ctx)
    shared = {}
    shared['ident'] = np.eye(128, dtype=np.float32)
    wm = f(w_mod)
    shared['wmod'] = np.ascontiguousarray(wm.reshape(L, 16, 128, 48, 256).transpose(0, 3, 2, 1, 4))
    shared['bmod'] = _fm(f(b_mod), 96)
    shared['g1'] = _fm(f(norm1), 16)
    shared['g2'] = _fm(f(norm2), 16)
    shared['gf'] = _fm(f(norm_f), 16)
    wcols = _win_cols()
    wi = f(w_in)
    shared['win'] = np.stack([_wblocks(wi[l], wcols, 256) for l in range(L)])
    shared['sinkr'] = np.ascontiguousarray(np.broadcast_to(f(sink).reshape(1, L * 8), (128, L * 8)))
    shared['maskA'] = _maskA()
    for nm, w in (('wpa', w_pa), ('wpb', w_pb), ('wpc', w_pc)):
        shared[nm] = np.ascontiguousarray(f(w).reshape(L, 8, 128, D).transpose(0, 2, 1, 3))
    wo_ = f(w_o)
    shared['wo'] = np.stack([_wblocks(wo_[l], np.arange(D), 256) for l in range(L)])
    upcols = np.concatenate([np.concatenate([j * 128 + np.arange(128), 5632 + j * 128 + np.arange(128)])
                             for j in range(44)])
    wu = f(w_up)
    shared['wup'] = np.stack([_wblocks(wu[l], upcols, 256) for l in range(L)])
    wd_ = f(w_down)
    shared['wdn'] = np.stack([_wblocks(wd_[l], np.arange(D), 128) for l in range(L)])
    cc_ = f(conv_c)
    cf_ = f(conv_f)
    rp = f(rpb)
    per_half = []
    for half in range(2):
        d = {}
        d['cosT'], d['sinT'] = _rope_tables(half)
        d['biasB'] = _biasB(rp, half)
        cc = cc_ if half == 0 else cc_[:, ::-1, :]
        cf = cf_ if half == 0 else cf_[:, ::-1, :]
        d['convc'] = np.ascontiguousarray(cc.reshape(L, 3, 8, 128).transpose(0, 3, 2, 1))
        d['convf'] = np.ascontiguousarray(cf.reshape(L, 3, 88, 128).transpose(0, 3, 2, 1))
        per_half.append(d)
    in_maps = []
    for core in range(8):
        b, half = core // 2, core % 2
        m = dict(shared)
        m.update(per_half[half])
        xs = x[b] if half == 0 else x[b, ::-1]
        cs = ctx[b] if half == 0 else ctx[b, ::-1]
        m['xin'] = np.ascontiguousarray(np.concatenate([xs[:3584], cs], axis=0))
        cv = np.stack([c[b], c_ctx], axis=-1)
        m['csil'] = np.ascontiguousarray(cv.reshape(16, 128, 2).transpose(1, 0, 2))
        in_maps.append(m)
    return in_maps


def kernel(**inputs):
    in_maps = prep_inputs(**inputs)
    nc = _get_nc()
    res = run_bass_kernel_spmd(nc, in_maps, core_ids=list(range(8)))
    out = np.empty((4, 4096, D), np.float32)
    for core in range(8):
        b, half = core // 2, core % 2
        y = np.asarray(res.results[core]["out"], dtype=np.float32)
        if half == 0:
            out[b, :2048] = y
        else:
            out[b, 2048:] = y[::-1]
    return out
```
